# Optimizing a Trainium2 kernel written in Bass

```python
import jax
import jax.numpy as jnp
from jax import lax
import numpy as np


D_MODEL = 4096
BATCH = 1
SEQ = 8192
DEPTH = 4

GRID_W = 64
CTX_LEN = 256
HEAD_DIM = 128
N_Q_HEADS = (D_MODEL // 2) // HEAD_DIM
N_KV_HEADS = max(N_Q_HEADS // 4, 1)
GQA_GROUP = N_Q_HEADS // N_KV_HEADS
ATTN_W = N_Q_HEADS * HEAD_DIM
KV_W = N_KV_HEADS * HEAD_DIM
WINDOW = 128
BLOCK = 128
CHUNK = 128
GMLP_GROUP_W = 128
GMLP_GROUPS = (D_MODEL // 2) // GMLP_GROUP_W
GMLP_W = GMLP_GROUPS * GMLP_GROUP_W
AB_SPLITS = (ATTN_W, KV_W, KV_W, ATTN_W, GMLP_W, GMLP_W, GMLP_W)
AB_IN_W = 2 * ATTN_W + 2 * KV_W + 3 * GMLP_W
AB_OUT_W = ATTN_W + GMLP_W
CONV_W = D_MODEL
CONV_K = 31
C_IN_W = 3 * CONV_W
ROPE_BASE = 10000.0
RMS_EPS = 1e-6
LN_EPS = 1e-5
NEG_INF = -1e30

kernel_name = 'hybrid_swa_gmlp_conformer_dit'


def _split(t, sizes):
    idx = []
    acc = 0
    for s in sizes[:-1]:
        acc += s
        idx.append(acc)
    return jnp.split(t, idx, axis=-1)


def rmsnorm(x, g):
    xf = x.astype(jnp.float32)
    y = xf * lax.rsqrt(jnp.mean(xf * xf, axis=-1, keepdims=True) + RMS_EPS)
    return (y * g.astype(jnp.float32)).astype(x.dtype)


def layernorm(x, g, b):
    xf = x.astype(jnp.float32)
    mu = jnp.mean(xf, axis=-1, keepdims=True)
    var = jnp.mean(jnp.square(xf - mu), axis=-1, keepdims=True)
    y = (xf - mu) * lax.rsqrt(var + LN_EPS)
    return (y * g.astype(jnp.float32) + b.astype(jnp.float32)).astype(x.dtype)


def axial_rope(x, row, col):
    half = HEAD_DIM // 2
    quarter = HEAD_DIM // 4
    inv = 1.0 / (ROPE_BASE ** (jnp.arange(quarter, dtype=jnp.float32) / quarter))

    def rot(xp, pos):
        ang = pos.astype(jnp.float32)[:, None] * inv[None, :]
        cos = jnp.cos(ang)[None, :, None, :].astype(x.dtype)
        sin = jnp.sin(ang)[None, :, None, :].astype(x.dtype)
        a, b = xp[..., :quarter], xp[..., quarter:]
        return jnp.concatenate([a * cos - b * sin, b * cos + a * sin], axis=-1)

    return jnp.concatenate([rot(x[..., :half], row), rot(x[..., half:], col)], axis=-1)


def band_blocks(t):
    B, S, H, D = t.shape
    nb = S // BLOCK
    tp = jnp.pad(t, ((0, 0), (BLOCK, BLOCK), (0, 0), (0, 0))).reshape(B, nb + 2, BLOCK, H, D)
    return jnp.concatenate([tp[:, :-2], tp[:, 1:-1], tp[:, 2:]], axis=2)


def windowed_attention(q, k, v, kc, vc, sink):
    B, S, _, _ = q.shape
    nb = S // BLOCK
    n_band = 3 * BLOCK
    scale = HEAD_DIM ** -0.5
    qb = q.reshape(B, nb, BLOCK, N_KV_HEADS, GQA_GROUP, HEAD_DIM)
    kb = band_blocks(k)
    vb = band_blocks(v)
    s_band = jnp.einsum('bnqhgd,bnkhd->bnhgqk', qb, kb).astype(jnp.float32) * scale
    s_ctx = jnp.einsum('bnqhgd,bkhd->bnhgqk', qb, kc).astype(jnp.float32) * scale
    qpos = jnp.arange(nb)[:, None] * BLOCK + jnp.arange(BLOCK)[None, :]
    kpos = (jnp.arange(nb)[:, None] - 1) * BLOCK + jnp.arange(n_band)[None, :]
    valid = ((jnp.abs(qpos[:, :, None] - kpos[:, None, :]) <= WINDOW)
             & (kpos[:, None, :] >= 0) & (kpos[:, None, :] < S))
    s_band = jnp.where(valid[None, :, None, None, :, :], s_band, NEG_INF)
    sink_l = jnp.broadcast_to(
        sink.astype(jnp.float32).reshape(N_KV_HEADS, GQA_GROUP)[None, None, :, :, None, None],
        s_band.shape[:-1] + (1,))
    p = jax.nn.softmax(jnp.concatenate([s_band, s_ctx, sink_l], axis=-1), axis=-1).astype(v.dtype)
    n_ctx = kc.shape[1]
    o = (jnp.einsum('bnhgqk,bnkhd->bnqhgd', p[..., :n_band], vb)
         + jnp.einsum('bnhgqk,bkhd->bnqhgd', p[..., n_band:n_band + n_ctx], vc))
    return o.reshape(B, S, ATTN_W)


def context_attention(q, k, v, sink):
    B, C, _, _ = q.shape
    scale = HEAD_DIM ** -0.5
    qg = q.reshape(B, C, N_KV_HEADS, GQA_GROUP, HEAD_DIM)
    s = jnp.einsum('bqhgd,bkhd->bhgqk', qg, k).astype(jnp.float32) * scale
    sink_l = jnp.broadcast_to(
        sink.astype(jnp.float32).reshape(N_KV_HEADS, GQA_GROUP)[None, :, :, None, None],
        s.shape[:-1] + (1,))
    p = jax.nn.softmax(jnp.concatenate([s, sink_l], axis=-1), axis=-1)[..., :C].astype(v.dtype)
    o = jnp.einsum('bhgqk,bkhd->bqhgd', p, v)
    return o.reshape(B, C, ATTN_W)


def chunk_gmlp(u, v, ln_g, ln_b, ws, ws_b):
    u = jax.nn.gelu(u, approximate=False)
    v = layernorm(jax.nn.gelu(v, approximate=False), ln_g, ln_b)
    B, L, _ = v.shape
    nc = L // CHUNK
    vch = v.reshape(B, nc, CHUNK, GMLP_GROUPS, GMLP_GROUP_W)
    s = jnp.einsum('gpq,bnqgc->bnpgc', ws, vch) + ws_b.T[None, None, :, :, None]
    return u * s.reshape(B, L, GMLP_W)


def conformer_conv(a, b, dw, dw_b, ln_g, ln_b):
    glu = a * jax.nn.sigmoid(b)
    y = lax.conv_general_dilated(
        glu, dw[:, None, :], window_strides=(1,), padding=((CONV_K // 2, CONV_K // 2),),
        dimension_numbers=('NWC', 'WIO', 'NWC'), feature_group_count=CONV_W) + dw_b
    return jax.nn.silu(layernorm(y, ln_g, ln_b))


def setup_inputs(seed: int = 0) -> dict:
    key = jax.random.key(seed)
    ks = jax.random.split(key, 24)
    n_even = (DEPTH + 1) // 2
    n_odd = DEPTH // 2

    def nrm(k, shape, s):
        return jax.random.normal(k, shape, jnp.float32) * s

    return {
        'x': nrm(ks[0], (BATCH, SEQ, D_MODEL), 1.0),
        'c': nrm(ks[1], (BATCH, D_MODEL), 1.0),
        'ctx': nrm(ks[2], (BATCH, CTX_LEN, D_MODEL), 1.0),
        'c_ctx': nrm(ks[3], (D_MODEL,), 1.0),
        'ada_w': nrm(ks[4], (DEPTH, D_MODEL, 3 * D_MODEL), 0.5 * D_MODEL ** -0.5),
        'ada_b': nrm(ks[5], (DEPTH, 3 * D_MODEL), 0.02),
        'pre_g': 1.0 + nrm(ks[6], (DEPTH, D_MODEL), 0.02),
        'post_g': 1.0 + nrm(ks[7], (DEPTH, D_MODEL), 0.02),
        'ab_w_in': nrm(ks[8], (n_even, D_MODEL, AB_IN_W), D_MODEL ** -0.5),
        'ab_sink': nrm(ks[9], (n_even, N_Q_HEADS), 0.5),
        'ab_ln_g': 1.0 + nrm(ks[10], (n_even, GMLP_W), 0.02),
        'ab_ln_b': nrm(ks[11], (n_even, GMLP_W), 0.02),
        'ab_ws': nrm(ks[12], (n_even, GMLP_GROUPS, CHUNK, CHUNK), CHUNK ** -0.5),
        'ab_ws_b': 1.0 + nrm(ks[13], (n_even, GMLP_GROUPS, CHUNK), 0.02),
        'ab_w_out': nrm(ks[14], (n_even, AB_OUT_W, D_MODEL), AB_OUT_W ** -0.5),
        'cv_w_in': nrm(ks[15], (n_odd, D_MODEL, C_IN_W), D_MODEL ** -0.5),
        'cv_dw': nrm(ks[16], (n_odd, CONV_K, CONV_W), CONV_K ** -0.5),
        'cv_dw_b': nrm(ks[17], (n_odd, CONV_W), 0.02),
        'cv_ln_g': 1.0 + nrm(ks[18], (n_odd, CONV_W), 0.02),
        'cv_ln_b': nrm(ks[19], (n_odd, CONV_W), 0.02),
        'cv_w_out': nrm(ks[20], (n_odd, CONV_W, D_MODEL), CONV_W ** -0.5),
    }


def reference(x, c, ctx, c_ctx, ada_w, ada_b, pre_g, post_g, ab_w_in, ab_sink, ab_ln_g, ab_ln_b,
              ab_ws, ab_ws_b, ab_w_out, cv_w_in, cv_dw, cv_dw_b, cv_ln_g, cv_ln_b, cv_w_out):
    B, S, D = x.shape
    Lc = ctx.shape[1]
    ROWS = S // GRID_W
    row = jnp.repeat(jnp.arange(ROWS, dtype=jnp.int32), GRID_W)
    col = jnp.tile(jnp.arange(GRID_W, dtype=jnp.int32), ROWS)

    for l in range(DEPTH):
        last = l == DEPTH - 1
        even = l % 2 == 0
        need_ctx = even or (not last)
        i = l // 2

        mod = jax.nn.silu(c) @ ada_w[l] + ada_b[l]
        shift, scale, gate = jnp.split(mod, 3, axis=-1)
        h = rmsnorm(x, pre_g[l]) * (1.0 + scale[:, None, :]) + shift[:, None, :]
        if need_ctx:
            mod_c = jax.nn.silu(c_ctx) @ ada_w[l] + ada_b[l]
            shift_c, scale_c, gate_c = jnp.split(mod_c, 3, axis=-1)
            hc = rmsnorm(ctx, pre_g[l]) * (1.0 + scale_c) + shift_c

        if even:
            q, k, v, ga, u, vg, gb = _split(h @ ab_w_in[i], AB_SPLITS)
            q = axial_rope(q.reshape(B, S, N_Q_HEADS, HEAD_DIM), row, col)
            k = axial_rope(k.reshape(B, S, N_KV_HEADS, HEAD_DIM), row, col)
            v = v.reshape(B, S, N_KV_HEADS, HEAD_DIM)
            qc, kc, vc, gac, uc, vgc, gbc = _split(hc @ ab_w_in[i], AB_SPLITS)
            kc = kc.reshape(B, Lc, N_KV_HEADS, HEAD_DIM)
            vc = vc.reshape(B, Lc, N_KV_HEADS, HEAD_DIM)
            attn = windowed_attention(q, k, v, kc, vc, ab_sink[i])
            mix = chunk_gmlp(u, vg, ab_ln_g[i], ab_ln_b[i], ab_ws[i], ab_ws_b[i])
            y = jnp.concatenate([attn * jax.nn.silu(ga), mix * jax.nn.silu(gb)], axis=-1) @ ab_w_out[i]
            if not last:
                attn_c = context_attention(qc.reshape(B, Lc, N_Q_HEADS, HEAD_DIM), kc, vc, ab_sink[i])
                mix_c = chunk_gmlp(uc, vgc, ab_ln_g[i], ab_ln_b[i], ab_ws[i], ab_ws_b[i])
                yc = jnp.concatenate([attn_c * jax.nn.silu(gac), mix_c * jax.nn.silu(gbc)], axis=-1) @ ab_w_out[i]
        else:
            a, b, g = jnp.split(h @ cv_w_in[i], 3, axis=-1)
            y = (conformer_conv(a, b, cv_dw[i], cv_dw_b[i], cv_ln_g[i], cv_ln_b[i]) * jax.nn.silu(g)) @ cv_w_out[i]
            if not last:
                ac, bc, gcv = jnp.split(hc @ cv_w_in[i], 3, axis=-1)
                yc = (conformer_conv(ac, bc, cv_dw[i], cv_dw_b[i], cv_ln_g[i], cv_ln_b[i])
                      * jax.nn.silu(gcv)) @ cv_w_out[i]

        x = x + gate[:, None, :] * rmsnorm(y, post_g[l])
        if not last:
            ctx = ctx + gate_c * rmsnorm(yc, post_g[l])

    return x
```

```python
import numpy as np
import ml_dtypes
from contextlib import ExitStack
import concourse.bass as bass
import concourse.mybir as mybir
from concourse.bass_utils import run_bass_kernel_spmd

F32 = mybir.dt.float32
BF16 = mybir.dt.bfloat16
ALU = mybir.AluOpType
AF = mybir.ActivationFunctionType

NCORES = 8
D = 4096
KC = 32
SEQ = 8192
OWN = 1024
HALO = 512
TW = 2048
CTX = 256
TALL = TW + CTX
DEPTH = 4
RMS_EPS = 1e-6
LN_EPS = 1e-5
NEG = -30000.0
ATT_SCALE = 128 ** -0.5

IN_R = [(0, 16), (1, 15), (2, 14), (3, 13)]
OUT_R = [(1, 15), (2, 14), (3, 13), (4, 12)]
CTX_IN = [True, True, True, False]
CTX_OUT = [True, True, False, False]

ENGS = ["pe", "act", "dve", "pool", "sp"]
NDSEM = 64


class Buf:
    __slots__ = ("name", "w", "r", "ds")

    def __init__(self, name=""):
        self.name = name
        self.w = None
        self.r = []
        self.ds = None


class Sched:
    def __init__(self, nc, es):
        self.nc = nc
        self.q = {e: [] for e in ENGS}
        self.epoch_sems = []
        self.es = es
        self.cnt = {e: 0 for e in ENGS}
        self.esem = {}
        self.seen = {e: {} for e in ENGS}
        self.dsems = [es.enter_context(nc.semaphore(f"dq{i}")) for i in range(NDSEM)]
        self.dcnt = [0] * NDSEM
        self.dnext = 0
        self.dstage = 0
        self.dissued = {e: {} for e in ENGS}
        self.bar = es.enter_context(nc.semaphore("bar"))
        self.nbar = 0
        self.nep = 0
        self.bufs = []
        self.new_epoch()

    def new_epoch(self):
        for e in ENGS:
            self.esem[e] = (f"e{self.nep}_{e}", self.es.enter_context(self.nc.semaphore(f"s{self.nep}_{e}")))
            self.cnt[e] = 0
        self.nep += 1

    def buf(self, name=""):
        b = Buf(name)
        self.bufs.append(b)
        return b

    def _waits(self, eng, reads, writes):
        evs = []
        for b in reads:
            if b.w is not None:
                evs.append(b.w)
        for b in writes:
            if b.w is not None:
                evs.append(b.w)
            evs.extend(b.r)
        seen = self.seen[eng]
        for (key, sem, val) in evs:
            if eng == "pe" and key.endswith("_pe"):
                continue
            if seen.get(key, 0) < val:
                self.q[eng].append(("wait", sem, val))
                seen[key] = val

    def op(self, eng, fn, reads=(), writes=()):
        self._waits(eng, reads, writes)
        key, sem = self.esem[eng]
        self.cnt[eng] += 1
        ev = (key, sem, self.cnt[eng])
        self.q[eng].append(("op", fn, sem))
        for b in writes:
            b.w = ev
            b.r = []
        for b in reads:
            b.r.append(ev)

    def dma(self, eng, fn, sb, reads=(), writes=()):
        self._waits(eng, reads, writes)
        if sb.ds is None:
            sb.ds = self.dnext % NDSEM
            self.dnext += 1
            self.dstage += 1
            assert self.dstage <= NDSEM, "out of dma semaphores"
        i = sb.ds
        self.dcnt[i] += 16
        ev = (f"d{i}", self.dsems[i], self.dcnt[i])
        self.q[eng].append(("dma", fn, self.dsems[i]))
        self.dissued[eng][i] = self.dcnt[i]
        for b in writes:
            b.w = ev
            b.r = []
        for b in reads:
            b.r.append(ev)

    def barrier(self, new_epoch=False):
        self.nbar += 1
        for e in ENGS:
            key, sem = self.esem[e]
            if self.cnt[e] > 0 and self.seen[e].get(key, 0) < self.cnt[e]:
                self.q[e].append(("wait", sem, self.cnt[e]))
                self.seen[e][key] = self.cnt[e]
            for i, val in self.dissued[e].items():
                if self.seen[e].get(f"d{i}", 0) < val:
                    self.q[e].append(("wait", self.dsems[i], val))
                    self.seen[e][f"d{i}"] = val
            self.dissued[e] = {}
        for e in ENGS:
            self.q[e].append(("inc", self.bar))
        for e in ENGS:
            self.q[e].append(("wait", self.bar, 5 * self.nbar))
        for b in self.bufs:
            b.w = None
            b.r = []
            b.ds = None
        self.dstage = 0
        if new_epoch:
            self.new_epoch()

    def replay(self, eng_name, eng):
        for item in self.q[eng_name]:
            if item[0] == "wait":
                eng.wait_ge(item[1], item[2])
            elif item[0] == "op":
                ins = item[1](eng)
                ins.then_inc(item[2], 1)
            elif item[0] == "dma":
                ins = item[1](eng)
                ins.then_inc(item[2], 16)
            elif item[0] == "inc":
                eng.sem_inc(item[1], 1)


class Arena:
    def __init__(self, t, nbytes):
        self.t = t
        self.nbytes = nbytes
        self.off = 0

    def reset(self):
        self.off = 0

    def alloc(self, shape, dtype):
        esz = 4 if dtype == F32 else 2
        n = 1
        for s in shape:
            n *= s
        nb = n * esz
        nb = (nb + 63) // 64 * 64
        assert self.off + nb <= self.nbytes, f"arena overflow {self.off + nb} > {self.nbytes}"
        v = self.t[:, self.off // 2:(self.off + n * esz) // 2]
        self.off += nb
        if dtype == F32:
            v = v.bitcast(F32)
        if len(shape) == 2:
            v = v.rearrange("p (a b) -> p a b", a=shape[0])
        elif len(shape) == 3:
            v = v.rearrange("p (a b c) -> p a b c", a=shape[0], b=shape[1])
        return v


def tok_blocks(lo, hi, maxn=512):
    out = []
    c = lo
    while c < hi:
        n = min(maxn, hi - c)
        out.append((c, n))
        c += n
    return out


def build_program(depth=DEPTH, debug=False):
    nc = bass.Bass("TRN2", target_bir_lowering=False)

    def din(name, shape, dt=F32):
        return nc.dram_tensor(name, list(shape), dt, kind="ExternalInput").ap()

    def dscr(name, shape, dt=F32):
        kind = "ExternalOutput" if (debug and name in ("XA", "XB", "MT", "MODD", "CA")) else "Internal"
        return nc.dram_tensor(name, list(shape), dt, kind=kind).ap()

    xT = din("xT", [D, TW])
    ctxT = din("ctxT", [D, CTX])
    cT = din("cT", [128, KC, 2])
    ada_w = din("ada_w", [DEPTH, D, 3 * D])
    ada_bT = din("ada_bT", [DEPTH, 128, 96])
    pre_gT = din("pre_gT", [DEPTH, 128, KC])
    post_gT = din("post_gT", [DEPTH, 128, KC])
    ab_w_in = din("ab_w_in", [2, D, 11264])
    ab_sinkB = din("ab_sinkB", [2, 128, 16])
    ab_ln_gT = din("ab_ln_gT", [2, 128, 16])
    ab_ln_bT = din("ab_ln_bT", [2, 128, 16])
    ab_wsT = din("ab_wsT", [2, 128, 16, 128])
    ab_wsbB = din("ab_wsbB", [2, 128, 2048])
    ab_w_out = din("ab_w_out", [2, D, D])
    cv_w_in = din("cv_w_in", [2, D, 3 * D])
    cv_dwT = din("cv_dwT", [2, 128, KC, 31])
    cv_dw_bT = din("cv_dw_bT", [2, 128, KC])
    cv_ln_gT = din("cv_ln_gT", [2, 128, KC])
    cv_ln_bT = din("cv_ln_bT", [2, 128, KC])
    cv_w_out = din("cv_w_out", [2, D, D])
    c_ones = din("c_ones", [128, 128], BF16)
    c_ident = din("c_ident", [128, 128], BF16)
    c_identf = din("c_identf", [128, 128])
    c_perm = din("c_perm", [128, 128], BF16)
    c_tri = din("c_tri", [128, 2, 512], BF16)
    c_cos = din("c_cos", [128, TALL])
    c_sin = din("c_sin", [128, TALL])
    c_kbias = din("c_kbias", [128, 18])
    c_tmask = din("c_tmask", [128, TALL])
    out = nc.dram_tensor("out", [D, OWN], F32, kind="ExternalOutput").ap()

    XA = dscr("XA", [D, TW])
    XB = dscr("XB", [D, TW])
    CA = dscr("CA", [D, CTX])
    CB = dscr("CB", [D, CTX])
    QT = dscr("QT", [2048, TALL], BF16)
    KT = dscr("KT", [512, TALL], BF16)
    VT = dscr("VT", [512, TALL], BF16)
    GA = dscr("GA", [2048, TALL], BF16)
    UU = dscr("UU", [2048, TALL], BF16)
    VG = dscr("VG", [2048, TALL], BF16)
    GB = dscr("GB", [2048, TALL], BF16)
    MT = dscr("MT", [D, TALL], BF16)
    GLU = dscr("GLU", [D, TALL])
    SG = dscr("SG", [D, TALL], BF16)
    YY = dscr("YY", [D, TALL])
    MODD = dscr("MODD", [DEPTH, 2, 3 * D])

    with ExitStack() as es:
        ARENA_BYTES = 170 * 1024
        arena_t = es.enter_context(nc.sbuf_tensor("arena", [128, ARENA_BYTES // 2], BF16))
        AR = Arena(arena_t, ARENA_BYTES)
        NWB = 3
        wbt = [es.enter_context(nc.sbuf_tensor(f"wb{i}", [128, KC, 128], BF16)) for i in range(NWB)]
        ones = es.enter_context(nc.sbuf_tensor("ones", [128, 128], BF16))
        ident = es.enter_context(nc.sbuf_tensor("ident", [128, 128], BF16))
        identf = es.enter_context(nc.sbuf_tensor("identf", [128, 128], F32))
        perm = es.enter_context(nc.sbuf_tensor("perm", [128, 128], BF16))
        tri = es.enter_context(nc.sbuf_tensor("tri", [128, 2, 512], BF16))
        kbias = es.enter_context(nc.sbuf_tensor("kbias", [128, 18], F32))
        scT = es.enter_context(nc.sbuf_tensor("scT", [128, KC, 2], BF16))
        vecs = es.enter_context(nc.sbuf_tensor("vecs", [128, 6, KC], F32))
        rpost = es.enter_context(nc.sbuf_tensor("rpost", [128, TW], F32))
        ps = [es.enter_context(nc.psum_tensor(f"ps{i}", [128, 512], F32)) for i in range(8)]
        S = Sched(nc, es)
        import os as _os
        _lim = int(_os.environ.get("KSTOP", "100000"))
        _cnt = [0]

        class _Stop(Exception):
            pass

        def chk(name):
            _cnt[0] += 1
            if _cnt[0] > _lim:
                raise _Stop()
            if _os.environ.get("KVERB"):
                print("stage", _cnt[0], name, flush=True)

        def load(eng, dst_ap, src_ap, b):
            S.dma(eng, lambda e: e.dma_start(out=dst_ap, in_=src_ap), b, writes=[b])

        def store(eng, dst_ap, src_ap, b):
            S.dma(eng, lambda e: e.dma_start(out=dst_ap, in_=src_ap), b, reads=[b])

        def stage_setup():
            chk("stage_setup")
            AR.reset()
            cfb = AR.alloc([KC, 2], F32)
            bl = [S.buf() for _ in range(8)]
            load("sp", ones[:], c_ones[:, :], bl[0])
            load("sp", ident[:], c_ident[:, :], bl[1])
            load("sp", identf[:], c_identf[:, :], bl[2])
            load("sp", perm[:], c_perm[:, :], bl[3])
            load("sp", tri[:], c_tri[:, :, :], bl[4])
            load("sp", kbias[:], c_kbias[:, :], bl[5])
            load("sp", cfb, cT[:, :, :], bl[6])
            S.op("act", lambda e: e.activation(out=scT[:], in_=cfb, func=AF.Silu), reads=[bl[6]], writes=[bl[7]])
            S.barrier()

        def proj_stage(jobs, actT, tblocks, act_bufs_ready=None):
            wbufs = [S.buf(f"w{i}") for i in range(NWB)]
            wslot = [0]
            psb = [S.buf(f"psb{i}") for i in range(8)]
            ctx_state = {"psrot": 0}
            return wbufs, psb

        def load_w(wbuf_b, wtile, wsrc):
            src = wsrc.rearrange("(kc p) c -> p kc c", p=128)
            S.dma("pool", lambda e: e.dma_start(out=wtile[:], in_=src), wbuf_b, writes=[wbuf_b])

        def ada_jobs(l):
            return [{"kind": "ada", "w": [ada_w[l, :, j * 128:(j + 1) * 128]], "j": j, "l": l} for j in range(96)]

        def stage_modprep(l):
            chk("stage_modprep")
            AR.reset()
            m96 = AR.alloc([2, 128], F32)
            modT = AR.alloc([2, 96], F32)
            abT = AR.alloc([96], F32)
            pg = AR.alloc([KC], F32)
            qg = AR.alloc([KC], F32)
            tmp = AR.alloc([KC], F32)
            b_m96, b_ab, b_pg, b_qg, b_mod, b_tmp, b_vecs = [S.buf() for _ in range(7)]
            bps = S.buf()
            load("sp", m96[0:96], MODD[l].rearrange("r (j c) -> j r c", c=128), b_m96)
            load("sp", abT, ada_bT[l], b_ab)
            load("sp", pg, pre_gT[l], b_pg)
            load("sp", qg, post_gT[l], b_qg)
            for r in range(2):
                S.op("pe", lambda e, r=r: e.transpose(ps[0][:, r * 128:r * 128 + 96], m96[0:96, r, :], identf[0:96, 0:96]),
                     reads=[b_m96], writes=[bps])
            for r in range(2):
                S.op("dve", lambda e, r=r: e.tensor_tensor(out=modT[:, r, :], in0=ps[0][:, r * 128:r * 128 + 96], in1=abT, op=ALU.add),
                     reads=[bps, b_ab], writes=[b_mod])
            for r in range(2):
                S.op("dve", lambda e, r=r: e.tensor_scalar(out=tmp, in0=modT[:, r, 32:64], scalar1=1.0, scalar2=None, op0=ALU.add),
                     reads=[b_mod], writes=[b_tmp])
                S.op("dve", lambda e, r=r: e.tensor_tensor(out=vecs[:, 3 * r + 0, :], in0=tmp, in1=pg, op=ALU.mult),
                     reads=[b_tmp, b_pg], writes=[b_vecs])
                S.op("dve", lambda e, r=r: e.tensor_copy(out=vecs[:, 3 * r + 1, :], in_=modT[:, r, 0:32]),
                     reads=[b_mod], writes=[b_vecs])
                S.op("dve", lambda e, r=r: e.tensor_tensor(out=vecs[:, 3 * r + 2, :], in0=modT[:, r, 64:96], in1=qg, op=ALU.mult),
                     reads=[b_mod, b_qg], writes=[b_vecs])
            S.barrier()

        def stage_prenorm(l, x_in, c_in, actT, segs):
            chk("stage_prenorm")
            NX = 4
            xt = [AR.alloc([512], F32) for _ in range(NX)]
            xb = [S.buf() for _ in range(NX)]
            sq = [AR.alloc([512], BF16) for _ in range(2)]
            sqb = [S.buf() for _ in range(2)]
            tm = [AR.alloc([512], F32) for _ in range(2)]
            tmb = [S.buf() for _ in range(2)]
            rst = [AR.alloc([512], F32) for _ in range(2)]
            rstb = [S.buf() for _ in range(2)]
            accb = [S.buf() for _ in range(2)]
            ab = S.buf()
            xi = 0
            blocks_ = []
            for (is_ctx, sc0, ac0, sn_) in segs:
                for (b0, n) in tok_blocks(0, sn_):
                    blocks_.append((is_ctx, sc0 + b0, ac0 + b0, n))
            for bi, (is_ctx, sc0, ac0, n) in enumerate(blocks_):
                src = c_in if is_ctx else x_in
                vo = 3 if is_ctx else 0
                acc = ps[bi % 2]
                for kc in range(KC):
                    s = xi % NX
                    xi += 1
                    load("sp", xt[s][:, 0:n], src[kc * 128:(kc + 1) * 128, sc0:sc0 + n], xb[s])
                    q = kc % 2
                    S.op("act", lambda e, s=s, q=q, n=n: e.activation(out=sq[q][:, 0:n], in_=xt[s][:, 0:n], func=AF.Square),
                         reads=[xb[s]], writes=[sqb[q]])
                    S.op("pe", lambda e, q=q, n=n, kc=kc, acc=acc: e.matmul(acc[:, 0:n], ones[:], sq[q][:, 0:n], start=(kc == 0), stop=(kc == KC - 1)),
                         reads=[sqb[q]], writes=[accb[bi % 2]])
                r = rst[bi % 2]
                rb = rstb[bi % 2]
                S.op("dve", lambda e, r=r, n=n, acc=acc: e.tensor_scalar(out=r[:, 0:n], in0=acc[:, 0:n], scalar1=1.0 / D, scalar2=RMS_EPS, op0=ALU.mult, op1=ALU.add),
                     reads=[accb[bi % 2]], writes=[rb])
                S.op("act", lambda e, r=r, n=n: e.activation(out=r[:, 0:n], in_=r[:, 0:n], func=AF.Sqrt), reads=[rb], writes=[rb])
                S.op("dve", lambda e, r=r, n=n: e.reciprocal(out=r[:, 0:n], in_=r[:, 0:n]), reads=[rb], writes=[rb])
                for kc in range(KC):
                    s = xi % NX
                    xi += 1
                    load("sp", xt[s][:, 0:n], src[kc * 128:(kc + 1) * 128, sc0:sc0 + n], xb[s])
                    q = kc % 2
                    S.op("dve", lambda e, s=s, q=q, n=n, kc=kc, r=r, vo=vo: e.scalar_tensor_tensor(
                        out=tm[q][:, 0:n], in0=xt[s][:, 0:n], scalar=vecs[:, vo, kc:kc + 1], in1=r[:, 0:n], op0=ALU.mult, op1=ALU.mult),
                        reads=[xb[s], rb], writes=[tmb[q]])
                    S.op("act", lambda e, q=q, n=n, kc=kc, ac0=ac0, vo=vo: e.activation(
                        out=actT[:, kc, ac0:ac0 + n], in_=tm[q][:, 0:n], func=AF.Identity, bias=vecs[:, vo + 1, kc:kc + 1], scale=1.0),
                        reads=[tmb[q]], writes=[])

        class Proj:
            def __init__(self, actT, tblocks, n_of=3):
                self.actT = actT
                self.tb = tblocks
                self.wb = [S.buf() for _ in range(NWB)]
                self.wi = 0
                self.psb = [S.buf() for _ in range(8)]
                self.pr = 0
                self.NOB = 4
                self.ob = [AR.alloc([512], BF16) for _ in range(self.NOB)]
                self.obb = [S.buf() for _ in range(self.NOB)]
                self.oi = 0
                self.n_of = n_of
                self.of = [AR.alloc([512], F32) for _ in range(n_of)]
                self.ofb = [S.buf() for _ in range(n_of)]
                self.ofi = 0
                self.ad = [AR.alloc([128], F32) for _ in range(2)]
                self.adb = [S.buf() for _ in range(2)]
                self.adi = 0
                self.pending = []

            def next_w(self, wsrc):
                i = self.wi % NWB
                self.wi += 1
                load_w(self.wb[i], wbt[i], wsrc)
                return i

            def mm_group(self, wslot, c0, n, pbank):
                actT = self.actT

                def fn(e, wslot=wslot, c0=c0, n=n, pbank=pbank):
                    ins = None
                    for kc in range(KC):
                        ins = e.matmul(ps[pbank][:, 0:n], wbt[wslot][:, kc, :], actT[:, kc, c0:c0 + n],
                                       start=(kc == 0), stop=(kc == KC - 1))
                    return ins
                S.op("pe", fn, reads=[self.wb[wslot]], writes=[self.psb[pbank]])

            def out_bf(self):
                i = self.oi % self.NOB
                self.oi += 1
                return self.ob[i], self.obb[i]

            def out_f32(self):
                i = self.ofi % self.n_of
                self.ofi += 1
                return self.of[i], self.ofb[i]

            def out_ada(self):
                i = self.adi % 2
                self.adi += 1
                return self.ad[i], self.adb[i]

        def ada_job(P, job):
            l, j = job["l"], job["j"]
            wslot = P.next_w(job["w"][0])

            def fn(e):
                ins = None
                for kc in range(KC):
                    ins = e.matmul(ps[7][0:2, 0:128], scT[:, kc, :], wbt[wslot][:, kc, :], start=(kc == 0), stop=(kc == KC - 1))
                return ins
            S.op("pe", fn, reads=[P.wb[wslot]], writes=[P.psb[7]])
            o, ob = P.out_ada()
            S.op("act", lambda e: e.activation(out=o[0:2, 0:128], in_=ps[7][0:2, 0:128], func=AF.Copy), reads=[P.psb[7]], writes=[ob])
            store("sp", MODD[l, :, j * 128:(j + 1) * 128], o[0:2, 0:128], ob)

        def stage_ada_only(l):
            chk("stage_ada_only")
            AR.reset()
            P = Proj(None, [], n_of=0)
            for job in ada_jobs(l):
                ada_job(P, job)
            S.barrier()

        def act_to_global(segs, c0, n):
            res = []
            for (is_ctx, sc0, ac0, sn) in segs:
                lo = max(c0, ac0)
                hi = min(c0 + n, ac0 + sn)
                if lo < hi:
                    g = (TW if is_ctx else 0) + sc0 + (lo - ac0)
                    res.append((lo, g, hi - lo))
            return res

        def stage_inproj_even(l, actT, segs, ncols, next_ada):
            chk("stage_inproj_even")
            i = l // 2
            P = Proj(actT, tok_blocks(0, ncols), n_of=0)
            cs = [AR.alloc([512], F32) for _ in range(2)]
            sn = [AR.alloc([512], F32) for _ in range(2)]
            csb = [S.buf() for _ in range(2)]
            snb = [S.buf() for _ in range(2)]
            qb = [AR.alloc([512], BF16) for _ in range(2)]
            qbb = [S.buf() for _ in range(2)]
            t1 = [AR.alloc([512], F32) for _ in range(2)]
            t1b = [S.buf() for _ in range(2)]
            t2 = [AR.alloc([512], F32) for _ in range(2)]
            t2b = [S.buf() for _ in range(2)]
            ri = [0]
            kinds = [("q", 16, QT), ("k", 4, KT), ("v", 4, VT), ("ga", 16, GA), ("u", 16, UU), ("vg", 16, VG), ("gb", 16, GB)]
            jcol = 0
            adaj = list(next_ada)
            nada_per = (len(adaj) + 87) // 88 if adaj else 0
            _kj = int(_os.environ.get("KJOBS", "1000"))
            _ks = int(_os.environ.get("KSKIP", "0"))
            if _os.environ.get("KNOADA"):
                adaj = []
            for (kind, nblk, dst) in kinds:
                for jb in range(nblk):
                    if jcol >= _kj or jcol < _ks:
                        jcol += 1
                        continue
                    wslot = P.next_w(ab_w_in[i, :, jcol * 128:(jcol + 1) * 128])
                    jcol += 1
                    for (c0, n) in P.tb:
                        pbank = P.pr % 4
                        P.pr += 1
                        P.mm_group(wslot, c0, n, pbank)
                        o, ob = P.out_bf()
                        if kind in ("q", "k"):
                            r = ri[0] % 2
                            ri[0] += 1
                            for (ac, g, nn) in ([] if _os.environ.get("KROPE") in ("1", "2") else act_to_global(segs, c0, n)):
                                load("sp", cs[r][:, ac - c0:ac - c0 + nn], c_cos[:, g:g + nn], csb[r])
                                load("sp", sn[r][:, ac - c0:ac - c0 + nn], c_sin[:, g:g + nn], snb[r])
                            S.op("act", lambda e, r=r, n=n, pbank=pbank: e.activation(out=qb[r][:, 0:n], in_=ps[pbank][:, 0:n], func=AF.Copy),
                                 reads=[P.psb[pbank]], writes=[qbb[r]])
                            p2 = 4 + r
                            S.op("pe", lambda e, r=r, n=n, p2=p2: e.matmul(ps[p2][:, 0:n], perm[:], qb[r][:, 0:n], start=True, stop=True),
                                 reads=[qbb[r]], writes=[P.psb[p2]])
                            if _os.environ.get("KROPE") == "2":
                                S.op("act", lambda e, n=n, p2=p2, o=o: e.activation(out=o[:, 0:n], in_=ps[p2][:, 0:n], func=AF.Copy),
                                     reads=[P.psb[p2]], writes=[ob])
                                for (ac, g, nn) in act_to_global(segs, c0, n):
                                    store("sp", dst[jb * 128:(jb + 1) * 128, g:g + nn], o[:, ac - c0:ac - c0 + nn], ob)
                                continue
                            S.op("act", lambda e, r=r, n=n, pbank=pbank: e.activation(out=t1[r][:, 0:n], in_=ps[pbank][:, 0:n], func=AF.Copy),
                                 reads=[P.psb[pbank]], writes=[t1b[r]])
                            S.op("act", lambda e, r=r, n=n, p2=p2: e.activation(out=t2[r][:, 0:n], in_=ps[p2][:, 0:n], func=AF.Copy),
                                 reads=[P.psb[p2]], writes=[t2b[r]])
                            S.op("dve", lambda e, r=r, n=n: e.tensor_tensor(out=t1[r][:, 0:n], in0=t1[r][:, 0:n], in1=cs[r][:, 0:n], op=ALU.mult),
                                 reads=[t1b[r], csb[r]], writes=[t1b[r]])
                            S.op("dve", lambda e, r=r, n=n: e.tensor_tensor(out=t2[r][:, 0:n], in0=t2[r][:, 0:n], in1=sn[r][:, 0:n], op=ALU.mult),
                                 reads=[t2b[r], snb[r]], writes=[t2b[r]])
                            S.op("dve", lambda e, r=r, n=n, o=o: e.tensor_tensor(out=o[:, 0:n], in0=t1[r][:, 0:n], in1=t2[r][:, 0:n], op=ALU.add),
                                 reads=[t1b[r], t2b[r]], writes=[ob])
                        else:
                            func = {"v": AF.Copy, "ga": AF.Silu, "gb": AF.Silu, "u": AF.Gelu, "vg": AF.Gelu}[kind]
                            S.op("act", lambda e, n=n, pbank=pbank, o=o, func=func: e.activation(out=o[:, 0:n], in_=ps[pbank][:, 0:n], func=func),
                                 reads=[P.psb[pbank]], writes=[ob])
                        for (ac, g, nn) in act_to_global(segs, c0, n):
                            store("sp", dst[jb * 128:(jb + 1) * 128, g:g + nn], o[:, ac - c0:ac - c0 + nn], ob)
                    for _ in range(nada_per):
                        if adaj:
                            ada_job(P, adaj.pop(0))
            while adaj:
                ada_job(P, adaj.pop(0))

        def stage_inproj_odd(l, actT, segs, ncols, next_ada):
            chk("stage_inproj_odd")
            i = l // 2
            P = Proj(actT, tok_blocks(0, ncols))
            tmk = AR.alloc([ncols], F32)
            tmkb = S.buf()
            for (is_ctx, sc0, ac0, sn_) in segs:
                g = (TW if is_ctx else 0) + sc0
                load("sp", tmk[:, ac0:ac0 + sn_], c_tmask[:, g:g + sn_], tmkb)
            sg_ = [AR.alloc([512], F32) for _ in range(2)]
            sgb = [S.buf() for _ in range(2)]
            tt = [AR.alloc([512], F32) for _ in range(2)]
            ttb = [S.buf() for _ in range(2)]
            ri = 0
            adaj = list(next_ada)
            nada_per = (len(adaj) + 31) // 32 if adaj else 0
            for jb in range(KC):
                wa = P.next_w(cv_w_in[i, :, jb * 128:(jb + 1) * 128])
                wb_ = P.next_w(cv_w_in[i, :, D + jb * 128:D + (jb + 1) * 128])
                for (c0, n) in P.tb:
                    pa = (P.pr % 2) * 2
                    pb = pa + 1
                    P.pr += 1
                    P.mm_group(wa, c0, n, pa)
                    P.mm_group(wb_, c0, n, pb)
                    r = ri % 2
                    ri += 1
                    o, ob = P.out_f32()
                    S.op("act", lambda e, r=r, n=n, pb=pb: e.activation(out=sg_[r][:, 0:n], in_=ps[pb][:, 0:n], func=AF.Sigmoid),
                         reads=[P.psb[pb]], writes=[sgb[r]])
                    S.op("act", lambda e, r=r, n=n, pa=pa: e.activation(out=tt[r][:, 0:n], in_=ps[pa][:, 0:n], func=AF.Copy),
                         reads=[P.psb[pa]], writes=[ttb[r]])
                    S.op("dve", lambda e, r=r, n=n: e.tensor_tensor(out=tt[r][:, 0:n], in0=tt[r][:, 0:n], in1=sg_[r][:, 0:n], op=ALU.mult),
                         reads=[ttb[r], sgb[r]], writes=[ttb[r]])
                    S.op("dve", lambda e, r=r, n=n, c0=c0, o=o: e.tensor_tensor(out=o[:, 0:n], in0=tt[r][:, 0:n], in1=tmk[:, c0:c0 + n], op=ALU.mult),
                         reads=[ttb[r], tmkb], writes=[ob])
                    for (ac, g, nn) in act_to_global(segs, c0, n):
                        store("sp", GLU[jb * 128:(jb + 1) * 128, g:g + nn], o[:, ac - c0:ac - c0 + nn], ob)
                wg = P.next_w(cv_w_in[i, :, 2 * D + jb * 128:2 * D + (jb + 1) * 128])
                for (c0, n) in P.tb:
                    pg_ = 4 + (P.pr % 2)
                    P.pr += 1
                    P.mm_group(wg, c0, n, pg_)
                    o, ob = P.out_bf()
                    S.op("act", lambda e, n=n, pg_=pg_, o=o: e.activation(out=o[:, 0:n], in_=ps[pg_][:, 0:n], func=AF.Silu),
                         reads=[P.psb[pg_]], writes=[ob])
                    for (ac, g, nn) in act_to_global(segs, c0, n):
                        store("sp", SG[jb * 128:(jb + 1) * 128, g:g + nn], o[:, ac - c0:ac - c0 + nn], ob)
                for _ in range(nada_per):
                    if adaj:
                        ada_job(P, adaj.pop(0))
            while adaj:
                ada_job(P, adaj.pop(0))

        def stage_outproj(l, w_out, segs, ncols):
            chk("stage_outproj")
            AR.reset()
            actT = AR.alloc([KC, ncols], BF16)
            ldb = [S.buf() for _ in range(8)]
            for qd in range(8):
                for (is_ctx, sc0, ac0, sn_) in segs:
                    g = (TW if is_ctx else 0) + sc0
                    load("sp", actT[:, qd * 4:(qd + 1) * 4, ac0:ac0 + sn_],
                         MT[qd * 512:(qd + 1) * 512, g:g + sn_].rearrange("(c p) t -> p c t", p=128), ldb[qd])
            P = Proj(actT, tok_blocks(0, ncols))
            assert len(P.tb) <= 4
            sq = [AR.alloc([512], BF16) for _ in range(2)]
            sqb = [S.buf() for _ in range(2)]
            accb = [S.buf() for _ in range(4)]
            ri = 0
            first = True
            for jb in range(KC):
                wslot = P.next_w(w_out[:, jb * 128:(jb + 1) * 128])
                for ti, (c0, n) in enumerate(P.tb):
                    pbank = 4 + (P.pr % 3)
                    P.pr += 1
                    actT_ = actT

                    def fn(e, wslot=wslot, c0=c0, n=n, pbank=pbank):
                        ins = None
                        for kc in range(KC):
                            ins = e.matmul(ps[pbank][:, 0:n], wbt[wslot][:, kc, :], actT_[:, kc, c0:c0 + n],
                                           start=(kc == 0), stop=(kc == KC - 1))
                        return ins
                    S.op("pe", fn, reads=[P.wb[wslot]] + (ldb if first else []), writes=[P.psb[pbank]])
                    first = False
                    o, ob = P.out_f32()
                    r = ri % 2
                    ri += 1
                    S.op("act", lambda e, n=n, pbank=pbank, o=o: e.activation(out=o[:, 0:n], in_=ps[pbank][:, 0:n], func=AF.Copy),
                         reads=[P.psb[pbank]], writes=[ob])
                    S.op("act", lambda e, n=n, pbank=pbank, r=r: e.activation(out=sq[r][:, 0:n], in_=ps[pbank][:, 0:n], func=AF.Square),
                         reads=[P.psb[pbank]], writes=[sqb[r]])
                    S.op("pe", lambda e, n=n, r=r, ti=ti, jb=jb: e.matmul(ps[ti][:, 0:n], ones[:], sq[r][:, 0:n], start=(jb == 0), stop=(jb == KC - 1)),
                         reads=[sqb[r]], writes=[accb[ti]])
                    store("sp", YY[jb * 128:(jb + 1) * 128, c0:c0 + n], o[:, 0:n], ob)
            rpb = S.buf()
            for ti, (c0, n) in enumerate(P.tb):
                S.op("dve", lambda e, ti=ti, c0=c0, n=n: e.tensor_scalar(out=rpost[:, c0:c0 + n], in0=ps[ti][:, 0:n], scalar1=1.0 / D, scalar2=RMS_EPS, op0=ALU.mult, op1=ALU.add),
                     reads=[accb[ti]], writes=[rpb])
            S.op("act", lambda e: e.activation(out=rpost[:, 0:ncols], in_=rpost[:, 0:ncols], func=AF.Sqrt), reads=[rpb], writes=[rpb])
            S.op("dve", lambda e: e.reciprocal(out=rpost[:, 0:ncols], in_=rpost[:, 0:ncols]), reads=[rpb], writes=[rpb])
            S.barrier()

        def stage_postnorm(l, segs, x_in, c_in, x_out, c_out, final):
            chk("stage_postnorm")
            AR.reset()
            NB_ = 3
            yt = [AR.alloc([512], F32) for _ in range(NB_)]
            ytb = [S.buf() for _ in range(NB_)]
            xt = [AR.alloc([512], F32) for _ in range(NB_)]
            xtb = [S.buf() for _ in range(NB_)]
            ot = [AR.alloc([512], F32) for _ in range(NB_)]
            otb = [S.buf() for _ in range(NB_)]
            it = 0
            for (is_ctx, sc0, ac0, sn_) in segs:
                for (b0, n) in tok_blocks(0, sn_):
                    for kc in range(KC):
                        s = it % NB_
                        it += 1
                        load("sp", yt[s][:, 0:n], YY[kc * 128:(kc + 1) * 128, ac0 + b0:ac0 + b0 + n], ytb[s])
                        xsrc = c_in if is_ctx else x_in
                        load("sp", xt[s][:, 0:n], xsrc[kc * 128:(kc + 1) * 128, sc0 + b0:sc0 + b0 + n], xtb[s])
                        vo = 5 if is_ctx else 2
                        S.op("dve", lambda e, s=s, n=n, kc=kc, vo=vo, a0=ac0 + b0: e.scalar_tensor_tensor(
                            out=yt[s][:, 0:n], in0=yt[s][:, 0:n], scalar=vecs[:, vo, kc:kc + 1], in1=rpost[:, a0:a0 + n], op0=ALU.mult, op1=ALU.mult),
                            reads=[ytb[s]], writes=[ytb[s]])
                        S.op("pool", lambda e, s=s, n=n: e.tensor_tensor(out=ot[s][:, 0:n], in0=yt[s][:, 0:n], in1=xt[s][:, 0:n], op=ALU.add),
                             reads=[ytb[s], xtb[s]], writes=[otb[s]])
                        if is_ctx:
                            dst = c_out[kc * 128:(kc + 1) * 128, sc0 + b0:sc0 + b0 + n]
                        elif final:
                            dst = x_out[kc * 128:(kc + 1) * 128, sc0 + b0 - HALO:sc0 + b0 - HALO + n]
                        else:
                            dst = x_out[kc * 128:(kc + 1) * 128, sc0 + b0:sc0 + b0 + n]
                        store("sp", dst, ot[s][:, 0:n], otb[s])
            S.barrier()

        def stage_attn(l):
            chk("stage_attn")
            i = l // 2
            AR.reset()
            ilo, ihi = IN_R[l]
            olo, ohi = OUT_R[l]
            do_ctxq = CTX_OUT[l]
            nk_lat = (ihi - ilo) * 128
            NK = nk_lat + CTX
            nq_lat = (ohi - olo) * 128
            NQ = nq_lat + (CTX if do_ctxq else 0)
            QTh = [AR.alloc([4, NQ], BF16) for _ in range(1)]
            GAh = [AR.alloc([4, NQ], BF16) for _ in range(1)]
            KTh = AR.alloc([NK], BF16)
            VTh = AR.alloc([NK], BF16)
            nkt = NK // 128
            Vtok = AR.alloc([nkt, 128], BF16)
            sinkb = AR.alloc([16], F32)
            se = AR.alloc([16], F32)
            Eb = [AR.alloc([512], BF16) for _ in range(5)]
            dn = AR.alloc([512], F32)
            tO = AR.alloc([512], F32)
            NMO = 2
            mo = [AR.alloc([512], BF16) for _ in range(NMO)]
            bq, bg, bk, bv, bvt, bsk, bse, bdn, btO = [S.buf() for _ in range(9)]
            bE = [S.buf() for _ in range(5)]
            bmo = [S.buf() for _ in range(NMO)]
            pb = [S.buf() for _ in range(8)]
            load("sp", sinkb, ab_sinkB[i], bsk)
            S.op("act", lambda e: e.activation(out=se, in_=sinkb, func=AF.Exp), reads=[bsk], writes=[bse])
            moi = 0
            for hk in range(4):
                load("sp", QTh[0][:, :, 0:nq_lat], QT[hk * 512:(hk + 1) * 512, olo * 128:ohi * 128].rearrange("(h d) t -> d h t", d=128), bq)
                load("sp", GAh[0][:, :, 0:nq_lat], GA[hk * 512:(hk + 1) * 512, olo * 128:ohi * 128].rearrange("(h d) t -> d h t", d=128), bg)
                load("sp", KTh[:, 0:nk_lat], KT[hk * 128:(hk + 1) * 128, ilo * 128:ihi * 128], bk)
                load("sp", VTh[:, 0:nk_lat], VT[hk * 128:(hk + 1) * 128, ilo * 128:ihi * 128], bv)
                if do_ctxq:
                    load("sp", QTh[0][:, :, nq_lat:NQ], QT[hk * 512:(hk + 1) * 512, TW:TALL].rearrange("(h d) t -> d h t", d=128), bq)
                    load("sp", GAh[0][:, :, nq_lat:NQ], GA[hk * 512:(hk + 1) * 512, TW:TALL].rearrange("(h d) t -> d h t", d=128), bg)
                load("sp", KTh[:, nk_lat:NK], KT[hk * 128:(hk + 1) * 128, TW:TALL], bk)
                load("sp", VTh[:, nk_lat:NK], VT[hk * 128:(hk + 1) * 128, TW:TALL], bv)
                for kt in range(nkt):
                    pbank = 6 + (kt % 2)
                    pv = ps[pbank][:].bitcast(BF16)
                    S.op("pe", lambda e, kt=kt, pv=pv: e.transpose(pv[:, 0:128], VTh[:, kt * 128:(kt + 1) * 128], ident[:]),
                         reads=[bv], writes=[pb[pbank]])
                    S.op("dve", lambda e, kt=kt, pv=pv: e.tensor_copy(out=Vtok[:, kt, :], in_=pv[:, 0:128]),
                         reads=[pb[pbank]], writes=[bvt])
                qtiles = [("lat", t) for t in range(olo, ohi)] + ([("ctx", 0), ("ctx", 1)] if do_ctxq else [])
                for (qk, t) in qtiles:
                    if qk == "lat":
                        qc = (t - olo) * 128
                        klist = [(t - 1 - ilo, t - 1, 0), (t - ilo, t, None), (t + 1 - ilo, t + 1, 1)]
                        klist += [(nk_lat // 128, 16, None), (nk_lat // 128 + 1, 17, None)]
                    else:
                        qc = nq_lat + t * 128
                        klist = [(nk_lat // 128, 16, None), (nk_lat // 128 + 1, 17, None)]
                    nkl = len(klist)
                    for ki, (kti, kbi, mk) in enumerate(klist):
                        S.op("pe", lambda e, ki=ki, kti=kti, qc=qc: e.matmul(ps[ki][:, :].rearrange("p (h q) -> p h q", h=4),
                                                                           KTh[:, kti * 128:(kti + 1) * 128], QTh[0][:, :, qc:qc + 128], start=True, stop=True),
                             reads=[bk, bq], writes=[pb[ki]])
                        S.op("act", lambda e, ki=ki, kbi=kbi: e.activation(out=Eb[ki][:], in_=ps[ki][:], func=AF.Exp, bias=kbias[:, kbi:kbi + 1], scale=ATT_SCALE),
                             reads=[pb[ki]], writes=[bE[ki]])
                        if mk is not None:
                            S.op("dve", lambda e, ki=ki, mk=mk: e.tensor_tensor(out=Eb[ki][:], in0=Eb[ki][:], in1=tri[:, mk, :], op=ALU.mult),
                                 reads=[bE[ki]], writes=[bE[ki]])
                    for ki, (kti, kbi, mk) in enumerate(klist):
                        S.op("pe", lambda e, ki=ki, nkl=nkl: e.matmul(ps[5][:], ones[:], Eb[ki][:], start=(ki == 0), stop=(ki == nkl - 1)),
                             reads=[bE[ki]], writes=[pb[5]])
                    for ki, (kti, kbi, mk) in enumerate(klist):
                        S.op("pe", lambda e, ki=ki, kti=kti, nkl=nkl: e.matmul(ps[6][:], Vtok[:, kti, :], Eb[ki][:], start=(ki == 0), stop=(ki == nkl - 1)),
                             reads=[bE[ki], bvt], writes=[pb[6]])
                    for h in range(4):
                        S.op("dve", lambda e, h=h, hk=hk: e.tensor_scalar(out=dn[:, h * 128:(h + 1) * 128], in0=ps[5][:, h * 128:(h + 1) * 128],
                                                                         scalar1=se[:, hk * 4 + h:hk * 4 + h + 1], scalar2=None, op0=ALU.add),
                             reads=[pb[5], bse], writes=[bdn])
                    S.op("dve", lambda e: e.reciprocal(out=dn[:], in_=dn[:]), reads=[bdn], writes=[bdn])
                    S.op("dve", lambda e: e.tensor_tensor(out=tO[:], in0=ps[6][:], in1=dn[:], op=ALU.mult), reads=[pb[6], bdn], writes=[btO])
                    m = moi % NMO
                    moi += 1
                    S.op("dve", lambda e, m=m, qc=qc: e.tensor_tensor(out=mo[m][:].rearrange("p (h q) -> p h q", h=4), in0=tO[:].rearrange("p (h q) -> p h q", h=4),
                                                                     in1=GAh[0][:, :, qc:qc + 128], op=ALU.mult),
                         reads=[btO, bg], writes=[bmo[m]])
                    gcol = (t * 128) if qk == "lat" else (TW + t * 128)
                    store("sp", MT[hk * 512:(hk + 1) * 512, gcol:gcol + 128].rearrange("(h d) t -> d h t", d=128),
                          mo[m][:].rearrange("p (h q) -> p h q", h=4), bmo[m])
                S.barrier()

        def stage_gmlp(l):
            chk("stage_gmlp")
            i = l // 2
            AR.reset()
            olo, ohi = OUT_R[l]
            tiles = [t * 128 for t in range(olo, ohi)] + ([TW, TW + 128] if CTX_OUT[l] else [])
            wsT = AR.alloc([16, 128], BF16)
            wsb = AR.alloc([2048], F32)
            lg = AR.alloc([16], F32)
            lb = AR.alloc([16], F32)
            bws, bwsb, blg, blb = [S.buf() for _ in range(4)]
            wsf = AR.alloc([16, 128], F32)
            bwsf = S.buf()
            load("sp", wsf, ab_wsT[i], bwsf)
            S.op("act", lambda e: e.activation(out=wsT[:], in_=wsf[:], func=AF.Copy), reads=[bwsf], writes=[bws])
            load("sp", wsb, ab_wsbB[i], bwsb)
            load("sp", lg, ab_ln_gT[i], blg)
            load("sp", lb, ab_ln_bT[i], blb)
            NR = 2
            vg = [AR.alloc([16, 128], BF16) for _ in range(NR)]
            uu = [AR.alloc([16, 128], BF16) for _ in range(NR)]
            gb = [AR.alloc([16, 128], BF16) for _ in range(NR)]
            bvg = [S.buf() for _ in range(NR)]
            buu = [S.buf() for _ in range(NR)]
            bgb = [S.buf() for _ in range(NR)]
            sq = AR.alloc([16, 128], BF16)
            bsq = S.buf()
            mean = AR.alloc([128], F32)
            msq = AR.alloc([128], F32)
            rstd = AR.alloc([128], F32)
            bmean, bmsq, brstd = S.buf(), S.buf(), S.buf()
            tn = AR.alloc([16, 128], F32)
            btn = S.buf()
            vn = AR.alloc([16, 128], BF16)
            bvn = S.buf()
            vlnT = AR.alloc([2048], BF16)
            bvl = S.buf()
            t1 = [AR.alloc([512], F32) for _ in range(2)]
            bt1 = [S.buf() for _ in range(2)]
            mo = [AR.alloc([512], BF16) for _ in range(4)]
            bmo = [S.buf() for _ in range(4)]
            pb = [S.buf() for _ in range(8)]
            for ti, gc in enumerate(tiles):
                r = ti % NR
                load("sp", vg[r], VG[:, gc:gc + 128].rearrange("(g c) t -> c g t", c=128), bvg[r])
                load("sp", uu[r], UU[:, gc:gc + 128].rearrange("(g c) t -> c g t", c=128), buu[r])
                load("sp", gb[r], GB[:, gc:gc + 128].rearrange("(g c) t -> c g t", c=128), bgb[r])
                S.op("act", lambda e, r=r: e.activation(out=sq[:], in_=vg[r][:], func=AF.Square), reads=[bvg[r]], writes=[bsq])

                def fsum(e, r=r):
                    ins = None
                    for g in range(16):
                        ins = e.matmul(ps[0][:, 0:128], ones[:], vg[r][:, g, :], start=(g == 0), stop=(g == 15))
                    return ins
                S.op("pe", fsum, reads=[bvg[r]], writes=[pb[0]])

                def fsq(e):
                    ins = None
                    for g in range(16):
                        ins = e.matmul(ps[1][:, 0:128], ones[:], sq[:, g, :], start=(g == 0), stop=(g == 15))
                    return ins
                S.op("pe", fsq, reads=[bsq], writes=[pb[1]])
                S.op("dve", lambda e: e.tensor_scalar(out=mean[:], in0=ps[0][:, 0:128], scalar1=1.0 / 2048, scalar2=None, op0=ALU.mult),
                     reads=[pb[0]], writes=[bmean])
                S.op("dve", lambda e: e.tensor_tensor(out=msq[:], in0=mean[:], in1=mean[:], op=ALU.mult), reads=[bmean], writes=[bmsq])
                S.op("dve", lambda e: e.scalar_tensor_tensor(out=rstd[:], in0=ps[1][:, 0:128], scalar=1.0 / 2048, in1=msq[:], op0=ALU.mult, op1=ALU.subtract),
                     reads=[pb[1], bmsq], writes=[brstd])
                S.op("dve", lambda e: e.tensor_scalar(out=rstd[:], in0=rstd[:], scalar1=LN_EPS, scalar2=None, op0=ALU.add), reads=[brstd], writes=[brstd])
                S.op("act", lambda e: e.activation(out=rstd[:], in_=rstd[:], func=AF.Sqrt), reads=[brstd], writes=[brstd])
                S.op("dve", lambda e: e.reciprocal(out=rstd[:], in_=rstd[:]), reads=[brstd], writes=[brstd])
                for g in range(16):
                    S.op("dve", lambda e, r=r, g=g: e.tensor_tensor(out=tn[:, g, :], in0=vg[r][:, g, :], in1=mean[:], op=ALU.subtract),
                         reads=[bvg[r], bmean], writes=[btn])
                    S.op("dve", lambda e, g=g: e.tensor_tensor(out=tn[:, g, :], in0=tn[:, g, :], in1=rstd[:], op=ALU.mult),
                         reads=[btn, brstd], writes=[btn])
                    S.op("act", lambda e, g=g: e.activation(out=vn[:, g, :], in_=tn[:, g, :], func=AF.Identity, bias=lb[:, g:g + 1], scale=lg[:, g:g + 1]),
                         reads=[btn, blg, blb], writes=[bvn])
                for half in range(2):
                    pbank = 2 + half
                    pv = ps[pbank][:].bitcast(BF16)

                    def ftr(e, half=half, pv=pv):
                        ins = None
                        for g8 in range(8):
                            g = half * 8 + g8
                            ins = e.transpose(pv[:, g8 * 128:(g8 + 1) * 128], vn[:, g, :], ident[:])
                        return ins
                    S.op("pe", ftr, reads=[bvn], writes=[pb[pbank]])
                    S.op("act", lambda e, half=half, pv=pv: e.activation(out=vlnT[:, half * 1024:(half + 1) * 1024], in_=pv[:, 0:1024], func=AF.Copy),
                         reads=[pb[pbank]], writes=[bvl])
                for g4 in range(4):
                    pbank = 4 + g4

                    def fsp(e, g4=g4, pbank=pbank):
                        ins = None
                        for gg in range(4):
                            g = g4 * 4 + gg
                            ins = e.matmul(ps[pbank][:, gg * 128:(gg + 1) * 128], vlnT[:, g * 128:(g + 1) * 128], wsT[:, g, :], start=True, stop=True)
                        return ins
                    S.op("pe", fsp, reads=[bvl, bws], writes=[pb[pbank]])
                    q = g4 % 2
                    S.op("dve", lambda e, g4=g4, pbank=pbank, q=q: e.tensor_tensor(out=t1[q][:], in0=ps[pbank][:], in1=wsb[:, g4 * 512:(g4 + 1) * 512], op=ALU.add),
                         reads=[pb[pbank], bwsb], writes=[bt1[q]])
                    S.op("dve", lambda e, g4=g4, q=q, r=r: e.tensor_tensor(out=t1[q][:].rearrange("p (g t) -> p g t", g=4), in0=t1[q][:].rearrange("p (g t) -> p g t", g=4),
                                                                         in1=uu[r][:, g4 * 4:(g4 + 1) * 4, :], op=ALU.mult),
                         reads=[bt1[q], buu[r]], writes=[bt1[q]])
                    S.op("dve", lambda e, g4=g4, q=q, r=r: e.tensor_tensor(out=mo[g4][:].rearrange("p (g t) -> p g t", g=4), in0=t1[q][:].rearrange("p (g t) -> p g t", g=4),
                                                                         in1=gb[r][:, g4 * 4:(g4 + 1) * 4, :], op=ALU.mult),
                         reads=[bt1[q], bgb[r]], writes=[bmo[g4]])
                    store("sp", MT[2048 + g4 * 512:2048 + (g4 + 1) * 512, gc:gc + 128].rearrange("(g c) t -> c g t", c=128),
                          mo[g4][:].rearrange("p (g t) -> p g t", g=4), bmo[g4])
            S.barrier()

        def stage_conv(l):
            chk("stage_conv")
            i = l // 2
            AR.reset()
            olo, ohi = OUT_R[l]
            dw = AR.alloc([KC, 31], F32)
            dwb = AR.alloc([KC], F32)
            lg = AR.alloc([KC], F32)
            lb = AR.alloc([KC], F32)
            bdw, bdwb, blg, blb = [S.buf() for _ in range(4)]
            load("sp", dw, cv_dwT[i], bdw)
            load("sp", dwb, cv_dw_bT[i], bdwb)
            load("sp", lg, cv_ln_gT[i], blg)
            load("sp", lb, cv_ln_bT[i], blb)
            NB_ = 256
            blocks = [(False, c0, n) for (c0, n) in tok_blocks(olo * 128, ohi * 128, NB_)]
            if CTX_OUT[l]:
                blocks.append((True, 0, CTX))
            ybuf = [AR.alloc([KC, NB_], F32) for _ in range(2)]
            by = [[S.buf() for _ in range(KC)] for _ in range(2)]
            NG = 4
            gin = [AR.alloc([NB_ + 32], F32) for _ in range(NG)]
            bgin = [S.buf() for _ in range(NG)]
            pacc = [AR.alloc([NB_], F32) for _ in range(3)]
            bpacc = [S.buf() for _ in range(3)]
            yb = [AR.alloc([NB_], BF16) for _ in range(2)]
            byb = [S.buf() for _ in range(2)]
            sq = [AR.alloc([NB_], BF16) for _ in range(2)]
            bsq = [S.buf() for _ in range(2)]
            mean = [AR.alloc([NB_], F32) for _ in range(2)]
            msq = AR.alloc([NB_], F32)
            rstd = [AR.alloc([NB_], F32) for _ in range(2)]
            bmean = [S.buf() for _ in range(2)]
            bmsq = S.buf()
            brstd = [S.buf() for _ in range(2)]
            sgt = [AR.alloc([NB_], BF16) for _ in range(3)]
            bsg = [S.buf() for _ in range(3)]
            tz = [AR.alloc([NB_], F32) for _ in range(2)]
            btz = [S.buf() for _ in range(2)]
            zz = [AR.alloc([NB_], F32) for _ in range(2)]
            bzz = [S.buf() for _ in range(2)]
            mo = [AR.alloc([NB_], BF16) for _ in range(3)]
            bmo = [S.buf() for _ in range(3)]
            pb = [S.buf() for _ in range(8)]
            gi = 0
            for bi, (is_ctx, c0, n) in enumerate(blocks):
                yy = ybuf[bi % 2]
                byy = by[bi % 2]
                psum_s = ps[(bi % 2) * 2]
                psum_q = ps[(bi % 2) * 2 + 1]
                pbs = pb[(bi % 2) * 2]
                pbq = pb[(bi % 2) * 2 + 1]
                gbase = TW if is_ctx else 0
                for kc in range(KC):
                    s = gi % NG
                    gi += 1
                    if is_ctx:
                        S.op("pool", lambda e, s=s: e.memset(gin[s][:, :], 0.0), writes=[bgin[s]])
                        load("sp", gin[s][:, 15:15 + n], GLU[kc * 128:(kc + 1) * 128, TW:TW + n], bgin[s])
                    else:
                        load("sp", gin[s][:, 0:n + 30], GLU[kc * 128:(kc + 1) * 128, c0 - 15:c0 + n + 15], bgin[s])
                    accs = [yy[:, kc, 0:n], pacc[0][:, 0:n], pacc[1][:, 0:n], pacc[2][:, 0:n]]
                    accb_ = [byy[kc], bpacc[0], bpacc[1], bpacc[2]]
                    for j in range(31):
                        a = j % 4
                        if j < 4:
                            if a == 0:
                                S.op("dve", lambda e, s=s, kc=kc, n=n, j=j, o=accs[a]: e.tensor_scalar(out=o, in0=gin[s][:, j:j + n], scalar1=dw[:, kc, j:j + 1], scalar2=dwb[:, kc:kc + 1],
                                                                                                   op0=ALU.mult, op1=ALU.add),
                                     reads=[bgin[s], bdw, bdwb], writes=[accb_[a]])
                            else:
                                S.op("dve", lambda e, s=s, kc=kc, n=n, j=j, o=accs[a]: e.tensor_scalar(out=o, in0=gin[s][:, j:j + n], scalar1=dw[:, kc, j:j + 1], scalar2=None,
                                                                                                   op0=ALU.mult),
                                     reads=[bgin[s], bdw], writes=[accb_[a]])
                        else:
                            S.op("dve", lambda e, s=s, kc=kc, n=n, j=j, o=accs[a]: e.scalar_tensor_tensor(out=o, in0=gin[s][:, j:j + n], scalar=dw[:, kc, j:j + 1],
                                                                                                      in1=o, op0=ALU.mult, op1=ALU.add),
                                 reads=[bgin[s], accb_[a]], writes=[accb_[a]])
                    S.op("dve", lambda e, a0=accs[0], a1=accs[1]: e.tensor_tensor(out=a0, in0=a0, in1=a1, op=ALU.add), reads=[accb_[0], accb_[1]], writes=[accb_[0]])
                    S.op("dve", lambda e, a2=accs[2], a3=accs[3]: e.tensor_tensor(out=a2, in0=a2, in1=a3, op=ALU.add), reads=[accb_[2], accb_[3]], writes=[accb_[2]])
                    S.op("dve", lambda e, a0=accs[0], a2=accs[2]: e.tensor_tensor(out=a0, in0=a0, in1=a2, op=ALU.add), reads=[accb_[0], accb_[2]], writes=[accb_[0]])
                    q = kc % 2
                    S.op("act", lambda e, kc=kc, n=n, q=q, yy=yy: e.activation(out=yb[q][:, 0:n], in_=yy[:, kc, 0:n], func=AF.Copy), reads=[byy[kc]], writes=[byb[q]])
                    S.op("act", lambda e, kc=kc, n=n, q=q, yy=yy: e.activation(out=sq[q][:, 0:n], in_=yy[:, kc, 0:n], func=AF.Square), reads=[byy[kc]], writes=[bsq[q]])
                    S.op("pe", lambda e, kc=kc, n=n, q=q, psum_s=psum_s: e.matmul(psum_s[:, 0:n], ones[:], yb[q][:, 0:n], start=(kc == 0), stop=(kc == KC - 1)),
                         reads=[byb[q]], writes=[pbs])
                    S.op("pe", lambda e, kc=kc, n=n, q=q, psum_q=psum_q: e.matmul(psum_q[:, 0:n], ones[:], sq[q][:, 0:n], start=(kc == 0), stop=(kc == KC - 1)),
                         reads=[bsq[q]], writes=[pbq])
                mm = mean[bi % 2]
                rr = rstd[bi % 2]
                bm = bmean[bi % 2]
                br = brstd[bi % 2]
                S.op("dve", lambda e, n=n, mm=mm, psum_s=psum_s: e.tensor_scalar(out=mm[:, 0:n], in0=psum_s[:, 0:n], scalar1=1.0 / D, scalar2=None, op0=ALU.mult), reads=[pbs], writes=[bm])
                S.op("dve", lambda e, n=n, mm=mm: e.tensor_tensor(out=msq[:, 0:n], in0=mm[:, 0:n], in1=mm[:, 0:n], op=ALU.mult), reads=[bm], writes=[bmsq])
                S.op("dve", lambda e, n=n, rr=rr, psum_q=psum_q: e.scalar_tensor_tensor(out=rr[:, 0:n], in0=psum_q[:, 0:n], scalar=1.0 / D, in1=msq[:, 0:n], op0=ALU.mult, op1=ALU.subtract),
                     reads=[pbq, bmsq], writes=[br])
                S.op("dve", lambda e, n=n, rr=rr: e.tensor_scalar(out=rr[:, 0:n], in0=rr[:, 0:n], scalar1=LN_EPS, scalar2=None, op0=ALU.add), reads=[br], writes=[br])
                S.op("act", lambda e, n=n, rr=rr: e.activation(out=rr[:, 0:n], in_=rr[:, 0:n], func=AF.Sqrt), reads=[br], writes=[br])
                S.op("dve", lambda e, n=n, rr=rr: e.reciprocal(out=rr[:, 0:n], in_=rr[:, 0:n]), reads=[br], writes=[br])
                for kc in range(KC):
                    q = kc % 2
                    s3 = kc % 3
                    load("sp", sgt[s3][:, 0:n], SG[kc * 128:(kc + 1) * 128, gbase + c0:gbase + c0 + n], bsg[s3])
                    S.op("pool", lambda e, kc=kc, n=n, q=q, yy=yy, mm=mm: e.tensor_tensor(out=tz[q][:, 0:n], in0=yy[:, kc, 0:n], in1=mm[:, 0:n], op=ALU.subtract),
                         reads=[byy[kc], bm], writes=[btz[q]])
                    S.op("pool", lambda e, n=n, q=q, rr=rr: e.tensor_tensor(out=tz[q][:, 0:n], in0=tz[q][:, 0:n], in1=rr[:, 0:n], op=ALU.mult),
                         reads=[btz[q], br], writes=[btz[q]])
                    S.op("act", lambda e, kc=kc, n=n, q=q: e.activation(out=zz[q][:, 0:n], in_=tz[q][:, 0:n], func=AF.Silu, bias=lb[:, kc:kc + 1], scale=lg[:, kc:kc + 1]),
                         reads=[btz[q], blg, blb], writes=[bzz[q]])
                    S.op("pool", lambda e, n=n, q=q, s3=s3: e.tensor_tensor(out=mo[s3][:, 0:n], in0=zz[q][:, 0:n], in1=sgt[s3][:, 0:n], op=ALU.mult),
                         reads=[bzz[q], bsg[s3]], writes=[bmo[s3]])
                    store("sp", MT[kc * 128:(kc + 1) * 128, gbase + c0:gbase + c0 + n], mo[s3][:, 0:n], bmo[s3])
            S.barrier()

        def segs_for(lo, hi, with_ctx):
            segs = [(False, lo * 128, 0, (hi - lo) * 128)]
            ncols = (hi - lo) * 128
            if with_ctx:
                segs.append((True, 0, ncols, CTX))
                ncols += CTX
            return segs, ncols

        def _drive():
          stage_setup()
          stage_ada_only(0)
          x_bufs = [xT, XA, XB, XA, None]
          c_bufs = [ctxT, CA, CB, None, None]
          for l in range(depth):
            even = (l % 2 == 0)
            final = (l == DEPTH - 1)
            stage_modprep(l)
            x_in = x_bufs[l]
            x_out = out if final else x_bufs[l + 1]
            c_in = c_bufs[l]
            c_out = c_bufs[l + 1]
            ilo, ihi = IN_R[l]
            olo, ohi = OUT_R[l]
            segs_in, ncols_in = segs_for(ilo, ihi, CTX_IN[l])
            AR.reset()
            actT = AR.alloc([KC, ncols_in], BF16)
            mark = AR.off
            stage_prenorm(l, x_in, c_in, actT, segs_in)
            S.barrier()
            AR.off = mark
            nada = ada_jobs(l + 1) if l + 1 < depth else []
            if even:
                stage_inproj_even(l, actT, segs_in, ncols_in, nada)
                S.barrier()
                stage_attn(l)
                stage_gmlp(l)
                w_out = ab_w_out[l // 2]
            else:
                stage_inproj_odd(l, actT, segs_in, ncols_in, nada)
                S.barrier()
                stage_conv(l)
                w_out = cv_w_out[l // 2]
            segs_out, ncols_out = segs_for(olo, ohi, CTX_OUT[l])
            stage_outproj(l, w_out, segs_out, ncols_out)
            stage_postnorm(l, segs_out, x_in, c_in, x_out, c_out, final)
            S.barrier(new_epoch=True)

        try:
            _drive()
        except _Stop:
            S.barrier()
        block = es.enter_context(nc.Block())

        @block.tensor
        def _(e):
            S.replay("pe", e)

        @block.scalar
        def _(e):
            S.replay("act", e)

        @block.vector
        def _(e):
            S.replay("dve", e)

        @block.gpsimd
        def _(e):
            S.replay("pool", e)

        @block.sync
        def _(e):
            S.replay("sp", e)

    return nc


def _fm(v, k=KC):
    sh = v.shape[:-1]
    return np.ascontiguousarray(np.swapaxes(v.reshape(sh + (k, 128)), -1, -2))


def _rope_tables(core):
    pos = core * OWN - HALO + np.arange(TW)
    row = (pos // 64).astype(np.float32)
    col = (pos % 64).astype(np.float32)
    inv = (1.0 / (10000.0 ** (np.arange(32, dtype=np.float32) / 32))).astype(np.float32)
    cos = np.ones((128, TALL), np.float32)
    sin = np.zeros((128, TALL), np.float32)
    for d in range(128):
        p = row if d < 64 else col
        ang = (p * inv[d % 32]).astype(np.float32)
        cos[d, :TW] = np.cos(ang)
        s = np.sin(ang)
        sin[d, :TW] = -s if (d % 64) < 32 else s
    return cos, sin


def _prep_inputs(inputs):
    f = lambda a: np.ascontiguousarray(np.asarray(a, dtype=np.float32))
    x = f(inputs["x"])[0]
    ctx = f(inputs["ctx"])[0]
    c = f(inputs["c"])[0]
    c_ctx = f(inputs["c_ctx"])
    shared = {}
    shared["ctxT"] = np.ascontiguousarray(ctx.T)
    shared["cT"] = np.ascontiguousarray(np.stack([_fm(c), _fm(c_ctx)], axis=-1))
    shared["ada_w"] = f(inputs["ada_w"])
    shared["ada_bT"] = _fm(f(inputs["ada_b"]), 96)
    shared["pre_gT"] = _fm(f(inputs["pre_g"]))
    shared["post_gT"] = _fm(f(inputs["post_g"]))
    shared["ab_w_in"] = f(inputs["ab_w_in"])
    shared["ab_sinkB"] = np.ascontiguousarray(np.broadcast_to(f(inputs["ab_sink"])[:, None, :], (2, 128, 16)))
    shared["ab_ln_gT"] = _fm(f(inputs["ab_ln_g"]), 16)
    shared["ab_ln_bT"] = _fm(f(inputs["ab_ln_b"]), 16)
    ws = f(inputs["ab_ws"])
    shared["ab_wsT"] = np.ascontiguousarray(ws.transpose(0, 3, 1, 2))
    wsb = f(inputs["ab_ws_b"]).reshape(2, 1, 2048)
    shared["ab_wsbB"] = np.ascontiguousarray(np.broadcast_to(wsb, (2, 128, 2048)))
    shared["ab_w_out"] = f(inputs["ab_w_out"])
    shared["cv_w_in"] = f(inputs["cv_w_in"])
    dw = f(inputs["cv_dw"])
    shared["cv_dwT"] = np.ascontiguousarray(dw.reshape(2, 31, KC, 128).transpose(0, 3, 2, 1))
    shared["cv_dw_bT"] = _fm(f(inputs["cv_dw_b"]))
    shared["cv_ln_gT"] = _fm(f(inputs["cv_ln_g"]))
    shared["cv_ln_bT"] = _fm(f(inputs["cv_ln_b"]))
    shared["cv_w_out"] = f(inputs["cv_w_out"])
    bf = ml_dtypes.bfloat16
    shared["c_ones"] = np.ones((128, 128), bf)
    shared["c_ident"] = np.eye(128, dtype=np.float32).astype(bf)
    shared["c_identf"] = np.eye(128, dtype=np.float32)
    pm = np.zeros((128, 128), np.float32)
    for d in range(128):
        pm[d + 32 if (d % 64) < 32 else d - 32, d] = 1.0
    shared["c_perm"] = pm.astype(bf)
    kj = np.arange(128)[:, None]
    qi = np.arange(128)[None, :]
    tri = np.stack([np.tile((kj >= qi).astype(np.float32), (1, 4)), np.tile((kj <= qi).astype(np.float32), (1, 4))], axis=1)
    shared["c_tri"] = tri.astype(bf)
    in_maps = []
    for core in range(NCORES):
        m = dict(shared)
        xw = np.zeros((TW, D), np.float32)
        p0 = core * OWN - HALO
        lo = max(p0, 0)
        hi = min(p0 + TW, SEQ)
        xw[lo - p0:hi - p0] = x[lo:hi]
        m["xT"] = np.ascontiguousarray(xw.T)
        cos, sin = _rope_tables(core)
        m["c_cos"] = cos
        m["c_sin"] = sin
        pos = p0 + np.arange(TW)
        valid = ((pos >= 0) & (pos < SEQ))
        kb = np.zeros((128, 18), np.float32)
        kb[:, :16] = np.where(valid.reshape(16, 128).T, 0.0, NEG)
        m["c_kbias"] = kb
        tm = np.ones((128, TALL), np.float32)
        tm[:, :TW] = valid[None, :].astype(np.float32)
        m["c_tmask"] = tm
        in_maps.append(m)
    return in_maps


_NC_CACHE = {}


def kernel(**inputs):
    in_maps = _prep_inputs(inputs)
    if "nc" not in _NC_CACHE:
        _NC_CACHE["nc"] = build_program()
    nc = _NC_CACHE["nc"]
    res = run_bass_kernel_spmd(nc, in_maps, core_ids=list(range(NCORES)))
    outs = [np.asarray(r["out"]) for r in res.results]
    full = np.concatenate([o.T for o in outs], axis=0)
    return np.ascontiguousarray(full[None].astype(np.float32))
```

```python
import numpy as np
import ml_dtypes
from contextlib import ExitStack
import concourse.bass as bass
import concourse.mybir as mybir
from concourse.bass_utils import run_bass_kernel_spmd

F32 = mybir.dt.float32
BF16 = mybir.dt.bfloat16
ALU = mybir.AluOpType
AF = mybir.ActivationFunctionType

NCORES = 8
D = 4096
KC = 32
SEQ = 8192
OWN = 1024
HALO = 512
TW = 2048
CTX = 256
TALL = TW + CTX
DEPTH = 4
RMS_EPS = 1e-6
LN_EPS = 1e-5
NEG = -30000.0
ATT_SCALE = 128 ** -0.5

IN_R = [(0, 16), (1, 15), (2, 14), (3, 13)]
OUT_R = [(1, 15), (2, 14), (3, 13), (4, 12)]
CTX_IN = [True, True, True, False]
CTX_OUT = [True, True, False, False]

ENGS = ["pe", "act", "dve", "pool", "sp"]
NDSEM = 64


class Buf:
    __slots__ = ("name", "w", "r", "ds")

    def __init__(self, name=""):
        self.name = name
        self.w = None
        self.r = []
        self.ds = None


class Sched:
    def __init__(self, nc, es):
        self.nc = nc
        self.q = {e: [] for e in ENGS}
        self.epoch_sems = []
        self.es = es
        self.cnt = {e: 0 for e in ENGS}
        self.esem = {}
        self.seen = {e: {} for e in ENGS}
        self.dsems = [es.enter_context(nc.semaphore(f"dq{i}")) for i in range(NDSEM)]
        self.dcnt = [0] * NDSEM
        self.dnext = 0
        self.dstage = 0
        self.dissued = {e: {} for e in ENGS}
        self.bar = es.enter_context(nc.semaphore("bar"))
        self.nbar = 0
        self.nep = 0
        self.bufs = []
        self.new_epoch()

    def new_epoch(self):
        for e in ENGS:
            self.esem[e] = (f"e{self.nep}_{e}", self.es.enter_context(self.nc.semaphore(f"s{self.nep}_{e}")))
            self.cnt[e] = 0
        self.nep += 1

    def buf(self, name=""):
        b = Buf(name)
        self.bufs.append(b)
        return b

    def _waits(self, eng, reads, writes):
        evs = []
        for b in reads:
            if b.w is not None:
                evs.append(b.w)
        for b in writes:
            if b.w is not None:
                evs.append(b.w)
            evs.extend(b.r)
        seen = self.seen[eng]
        for (key, sem, val) in evs:
            if eng == "pe" and key.endswith("_pe"):
                continue
            if seen.get(key, 0) < val:
                self.q[eng].append(("wait", sem, val))
                seen[key] = val

    def op(self, eng, fn, reads=(), writes=()):
        self._waits(eng, reads, writes)
        key, sem = self.esem[eng]
        self.cnt[eng] += 1
        ev = (key, sem, self.cnt[eng])
        self.q[eng].append(("op", fn, sem))
        for b in writes:
            b.w = ev
            b.r = []
        for b in reads:
            b.r.append(ev)

    def dma(self, eng, fn, sb, reads=(), writes=()):
        self._waits(eng, reads, writes)
        if sb.ds is None:
            sb.ds = self.dnext % NDSEM
            self.dnext += 1
            self.dstage += 1
            assert self.dstage <= NDSEM, "out of dma semaphores"
        i = sb.ds
        self.dcnt[i] += 16
        ev = (f"d{i}", self.dsems[i], self.dcnt[i])
        self.q[eng].append(("dma", fn, self.dsems[i]))
        self.dissued[eng][i] = self.dcnt[i]
        for b in writes:
            b.w = ev
            b.r = []
        for b in reads:
            b.r.append(ev)

    def barrier(self, new_epoch=False):
        self.nbar += 1
        for e in ENGS:
            key, sem = self.esem[e]
            if self.cnt[e] > 0 and self.seen[e].get(key, 0) < self.cnt[e]:
                self.q[e].append(("wait", sem, self.cnt[e]))
                self.seen[e][key] = self.cnt[e]
            for i, val in self.dissued[e].items():
                if self.seen[e].get(f"d{i}", 0) < val:
                    self.q[e].append(("wait", self.dsems[i], val))
                    self.seen[e][f"d{i}"] = val
            self.dissued[e] = {}
        for e in ENGS:
            self.q[e].append(("inc", self.bar))
        for e in ENGS:
            self.q[e].append(("wait", self.bar, 5 * self.nbar))
        for b in self.bufs:
            b.w = None
            b.r = []
            b.ds = None
        self.dstage = 0
        if new_epoch:
            self.new_epoch()

    def replay(self, eng_name, eng):
        for item in self.q[eng_name]:
            if item[0] == "wait":
                eng.wait_ge(item[1], item[2])
            elif item[0] == "op":
                ins = item[1](eng)
                ins.then_inc(item[2], 1)
            elif item[0] == "dma":
                ins = item[1](eng)
                ins.then_inc(item[2], 16)
            elif item[0] == "inc":
                eng.sem_inc(item[1], 1)


class Arena:
    def __init__(self, t, nbytes):
        self.t = t
        self.nbytes = nbytes
        self.off = 0

    def reset(self):
        self.off = 0

    def alloc(self, shape, dtype):
        esz = 4 if dtype == F32 else 2
        n = 1
        for s in shape:
            n *= s
        nb = n * esz
        nb = (nb + 63) // 64 * 64
        assert self.off + nb <= self.nbytes, f"arena overflow {self.off + nb} > {self.nbytes}"
        v = self.t[:, self.off // 2:(self.off + n * esz) // 2]
        self.off += nb
        if dtype == F32:
            v = v.bitcast(F32)
        if len(shape) == 2:
            v = v.rearrange("p (a b) -> p a b", a=shape[0])
        elif len(shape) == 3:
            v = v.rearrange("p (a b c) -> p a b c", a=shape[0], b=shape[1])
        return v


def tok_blocks(lo, hi, maxn=512):
    out = []
    c = lo
    while c < hi:
        n = min(maxn, hi - c)
        out.append((c, n))
        c += n
    return out


def build_program(depth=DEPTH, debug=False):
    nc = bass.Bass("TRN2", target_bir_lowering=False)

    def din(name, shape, dt=F32):
        return nc.dram_tensor(name, list(shape), dt, kind="ExternalInput").ap()

    def dscr(name, shape, dt=F32):
        kind = "ExternalOutput" if (debug and name in ("XA", "XB", "MT", "MODD", "CA")) else "Internal"
        return nc.dram_tensor(name, list(shape), dt, kind=kind).ap()

    xT = din("xT", [D, TW])
    ctxT = din("ctxT", [D, CTX])
    cT = din("cT", [128, KC, 2])
    ada_w = din("ada_w", [DEPTH, D, 3 * D])
    ada_bT = din("ada_bT", [DEPTH, 128, 96])
    pre_gT = din("pre_gT", [DEPTH, 128, KC])
    post_gT = din("post_gT", [DEPTH, 128, KC])
    ab_w_in = din("ab_w_in", [2, D, 11264])
    ab_sinkB = din("ab_sinkB", [2, 128, 16])
    ab_ln_gT = din("ab_ln_gT", [2, 128, 16])
    ab_ln_bT = din("ab_ln_bT", [2, 128, 16])
    ab_wsT = din("ab_wsT", [2, 128, 16, 128])
    ab_wsbB = din("ab_wsbB", [2, 128, 2048])
    ab_w_out = din("ab_w_out", [2, D, D])
    cv_w_in = din("cv_w_in", [2, D, 3 * D])
    cv_dwT = din("cv_dwT", [2, 128, KC, 31])
    cv_dw_bT = din("cv_dw_bT", [2, 128, KC])
    cv_ln_gT = din("cv_ln_gT", [2, 128, KC])
    cv_ln_bT = din("cv_ln_bT", [2, 128, KC])
    cv_w_out = din("cv_w_out", [2, D, D])
    c_ones = din("c_ones", [128, 128], BF16)
    c_ident = din("c_ident", [128, 128], BF16)
    c_identf = din("c_identf", [128, 128])
    c_perm = din("c_perm", [128, 128], BF16)
    c_tri = din("c_tri", [128, 2, 512], BF16)
    c_cos = din("c_cos", [128, TALL])
    c_sin = din("c_sin", [128, TALL])
    c_kbias = din("c_kbias", [128, 18])
    c_tmask = din("c_tmask", [128, TALL])
    out = nc.dram_tensor("out", [D, OWN], F32, kind="ExternalOutput").ap()

    XA = dscr("XA", [D, TW])
    XB = dscr("XB", [D, TW])
    CA = dscr("CA", [D, CTX])
    CB = dscr("CB", [D, CTX])
    QT = dscr("QT", [2048, TALL], BF16)
    KT = dscr("KT", [512, TALL], BF16)
    VT = dscr("VT", [512, TALL], BF16)
    GA = dscr("GA", [2048, TALL], BF16)
    UU = dscr("UU", [2048, TALL], BF16)
    VG = dscr("VG", [2048, TALL], BF16)
    GB = dscr("GB", [2048, TALL], BF16)
    MT = dscr("MT", [D, TALL], BF16)
    GLU = dscr("GLU", [D, TALL], BF16)
    SG = dscr("SG", [D, TALL], BF16)
    YY = dscr("YY", [D, TALL])
    MODD = dscr("MODD", [DEPTH, 2, 3 * D])

    with ExitStack() as es:
        ARENA_BYTES = 170 * 1024
        arena_t = es.enter_context(nc.sbuf_tensor("arena", [128, ARENA_BYTES // 2], BF16))
        AR = Arena(arena_t, ARENA_BYTES)
        NWB = 3
        wbt = [es.enter_context(nc.sbuf_tensor(f"wb{i}", [128, KC, 128], BF16)) for i in range(NWB)]
        ones = es.enter_context(nc.sbuf_tensor("ones", [128, 128], BF16))
        ident = es.enter_context(nc.sbuf_tensor("ident", [128, 128], BF16))
        identf = es.enter_context(nc.sbuf_tensor("identf", [128, 128], F32))
        perm = es.enter_context(nc.sbuf_tensor("perm", [128, 128], BF16))
        tri = es.enter_context(nc.sbuf_tensor("tri", [128, 2, 512], BF16))
        kbias = es.enter_context(nc.sbuf_tensor("kbias", [128, 18], F32))
        scT = es.enter_context(nc.sbuf_tensor("scT", [128, KC, 2], BF16))
        vecs = es.enter_context(nc.sbuf_tensor("vecs", [128, 6, KC], F32))
        rpost = es.enter_context(nc.sbuf_tensor("rpost", [128, TW], F32))
        ps = [es.enter_context(nc.psum_tensor(f"ps{i}", [128, 512], F32)) for i in range(8)]
        S = Sched(nc, es)
        import os as _os
        _lim = int(_os.environ.get("KSTOP", "100000"))
        _cnt = [0]

        class _Stop(Exception):
            pass

        def chk(name):
            _cnt[0] += 1
            if _cnt[0] > _lim:
                raise _Stop()
            if _os.environ.get("KVERB"):
                print("stage", _cnt[0], name, flush=True)

        def load(eng, dst_ap, src_ap, b):
            S.dma(eng, lambda e: e.dma_start(out=dst_ap, in_=src_ap), b, writes=[b])

        def store(eng, dst_ap, src_ap, b):
            S.dma(eng, lambda e: e.dma_start(out=dst_ap, in_=src_ap), b, reads=[b])

        def stage_setup():
            chk("stage_setup")
            AR.reset()
            cfb = AR.alloc([KC, 2], F32)
            bl = [S.buf() for _ in range(8)]
            load("sp", ones[:], c_ones[:, :], bl[0])
            load("sp", ident[:], c_ident[:, :], bl[1])
            load("sp", identf[:], c_identf[:, :], bl[2])
            load("sp", perm[:], c_perm[:, :], bl[3])
            load("sp", tri[:], c_tri[:, :, :], bl[4])
            load("sp", kbias[:], c_kbias[:, :], bl[5])
            load("sp", cfb, cT[:, :, :], bl[6])
            S.op("act", lambda e: e.activation(out=scT[:], in_=cfb, func=AF.Silu), reads=[bl[6]], writes=[bl[7]])
            S.barrier()

        def proj_stage(jobs, actT, tblocks, act_bufs_ready=None):
            wbufs = [S.buf(f"w{i}") for i in range(NWB)]
            wslot = [0]
            psb = [S.buf(f"psb{i}") for i in range(8)]
            ctx_state = {"psrot": 0}
            return wbufs, psb

        def load_w(wbuf_b, wtile, wsrc):
            src = wsrc.rearrange("(kc p) c -> p kc c", p=128)
            S.dma("pool", lambda e: e.dma_start(out=wtile[:], in_=src), wbuf_b, writes=[wbuf_b])

        def ada_jobs(l):
            return [{"kind": "ada", "w": [ada_w[l, :, j * 128:(j + 1) * 128]], "j": j, "l": l} for j in range(96)]

        def stage_modprep(l):
            chk("stage_modprep")
            AR.reset()
            m96 = AR.alloc([2, 128], F32)
            modT = AR.alloc([2, 96], F32)
            abT = AR.alloc([96], F32)
            pg = AR.alloc([KC], F32)
            qg = AR.alloc([KC], F32)
            tmp = AR.alloc([KC], F32)
            b_m96, b_ab, b_pg, b_qg, b_mod, b_tmp, b_vecs = [S.buf() for _ in range(7)]
            bps = S.buf()
            load("sp", m96[0:96], MODD[l].rearrange("r (j c) -> j r c", c=128), b_m96)
            load("sp", abT, ada_bT[l], b_ab)
            load("sp", pg, pre_gT[l], b_pg)
            load("sp", qg, post_gT[l], b_qg)
            for r in range(2):
                S.op("pe", lambda e, r=r: e.transpose(ps[0][:, r * 128:r * 128 + 96], m96[0:96, r, :], identf[0:96, 0:96]),
                     reads=[b_m96], writes=[bps])
            for r in range(2):
                S.op("dve", lambda e, r=r: e.tensor_tensor(out=modT[:, r, :], in0=ps[0][:, r * 128:r * 128 + 96], in1=abT, op=ALU.add),
                     reads=[bps, b_ab], writes=[b_mod])
            for r in range(2):
                S.op("dve", lambda e, r=r: e.tensor_scalar(out=tmp, in0=modT[:, r, 32:64], scalar1=1.0, scalar2=None, op0=ALU.add),
                     reads=[b_mod], writes=[b_tmp])
                S.op("dve", lambda e, r=r: e.tensor_tensor(out=vecs[:, 3 * r + 0, :], in0=tmp, in1=pg, op=ALU.mult),
                     reads=[b_tmp, b_pg], writes=[b_vecs])
                S.op("dve", lambda e, r=r: e.tensor_copy(out=vecs[:, 3 * r + 1, :], in_=modT[:, r, 0:32]),
                     reads=[b_mod], writes=[b_vecs])
                S.op("dve", lambda e, r=r: e.tensor_tensor(out=vecs[:, 3 * r + 2, :], in0=modT[:, r, 64:96], in1=qg, op=ALU.mult),
                     reads=[b_mod, b_qg], writes=[b_vecs])
            S.barrier()

        def stage_prenorm(l, x_in, c_in, actT, segs):
            chk("stage_prenorm")
            NX = 4
            xt = [AR.alloc([512], F32) for _ in range(NX)]
            xb = [S.buf() for _ in range(NX)]
            sq = [AR.alloc([512], BF16) for _ in range(2)]
            sqb = [S.buf() for _ in range(2)]
            tm = [AR.alloc([512], F32) for _ in range(2)]
            tmb = [S.buf() for _ in range(2)]
            rst = [AR.alloc([512], F32) for _ in range(2)]
            rstb = [S.buf() for _ in range(2)]
            accb = [S.buf() for _ in range(2)]
            ab = S.buf()
            xi = 0
            blocks_ = []
            for (is_ctx, sc0, ac0, sn_) in segs:
                for (b0, n) in tok_blocks(0, sn_):
                    blocks_.append((is_ctx, sc0 + b0, ac0 + b0, n))
            for bi, (is_ctx, sc0, ac0, n) in enumerate(blocks_):
                src = c_in if is_ctx else x_in
                vo = 3 if is_ctx else 0
                acc = ps[bi % 2]
                for kc in range(KC):
                    s = xi % NX
                    xi += 1
                    load("sp", xt[s][:, 0:n], src[kc * 128:(kc + 1) * 128, sc0:sc0 + n], xb[s])
                    q = kc % 2
                    S.op("act", lambda e, s=s, q=q, n=n: e.activation(out=sq[q][:, 0:n], in_=xt[s][:, 0:n], func=AF.Square),
                         reads=[xb[s]], writes=[sqb[q]])
                    S.op("pe", lambda e, q=q, n=n, kc=kc, acc=acc: e.matmul(acc[:, 0:n], ones[:], sq[q][:, 0:n], start=(kc == 0), stop=(kc == KC - 1)),
                         reads=[sqb[q]], writes=[accb[bi % 2]])
                r = rst[bi % 2]
                rb = rstb[bi % 2]
                S.op("dve", lambda e, r=r, n=n, acc=acc: e.tensor_scalar(out=r[:, 0:n], in0=acc[:, 0:n], scalar1=1.0 / D, scalar2=RMS_EPS, op0=ALU.mult, op1=ALU.add),
                     reads=[accb[bi % 2]], writes=[rb])
                S.op("act", lambda e, r=r, n=n: e.activation(out=r[:, 0:n], in_=r[:, 0:n], func=AF.Sqrt), reads=[rb], writes=[rb])
                S.op("dve", lambda e, r=r, n=n: e.reciprocal(out=r[:, 0:n], in_=r[:, 0:n]), reads=[rb], writes=[rb])
                for kc in range(KC):
                    s = xi % NX
                    xi += 1
                    load("sp", xt[s][:, 0:n], src[kc * 128:(kc + 1) * 128, sc0:sc0 + n], xb[s])
                    q = kc % 2
                    S.op("dve", lambda e, s=s, q=q, n=n, kc=kc, r=r, vo=vo: e.scalar_tensor_tensor(
                        out=tm[q][:, 0:n], in0=xt[s][:, 0:n], scalar=vecs[:, vo, kc:kc + 1], in1=r[:, 0:n], op0=ALU.mult, op1=ALU.mult),
                        reads=[xb[s], rb], writes=[tmb[q]])
                    S.op("act", lambda e, q=q, n=n, kc=kc, ac0=ac0, vo=vo: e.activation(
                        out=actT[:, kc, ac0:ac0 + n], in_=tm[q][:, 0:n], func=AF.Identity, bias=vecs[:, vo + 1, kc:kc + 1], scale=1.0),
                        reads=[tmb[q]], writes=[])

        class Proj:
            def __init__(self, actT, tblocks, n_of=3):
                self.actT = actT
                self.tb = tblocks
                self.wb = [S.buf() for _ in range(NWB)]
                self.wi = 0
                self.psb = [S.buf() for _ in range(8)]
                self.pr = 0
                self.NOB = 4
                self.ob = [AR.alloc([512], BF16) for _ in range(self.NOB)]
                self.obb = [S.buf() for _ in range(self.NOB)]
                self.oi = 0
                self.n_of = n_of
                self.of = [AR.alloc([512], F32) for _ in range(n_of)]
                self.ofb = [S.buf() for _ in range(n_of)]
                self.ofi = 0
                self.ad = [AR.alloc([128], F32) for _ in range(2)]
                self.adb = [S.buf() for _ in range(2)]
                self.adi = 0
                self.pending = []

            def next_w(self, wsrc):
                i = self.wi % NWB
                self.wi += 1
                load_w(self.wb[i], wbt[i], wsrc)
                return i

            def mm_group(self, wslot, c0, n, pbank):
                actT = self.actT

                def fn(e, wslot=wslot, c0=c0, n=n, pbank=pbank):
                    ins = None
                    for kc in range(KC):
                        ins = e.matmul(ps[pbank][:, 0:n], wbt[wslot][:, kc, :], actT[:, kc, c0:c0 + n],
                                       start=(kc == 0), stop=(kc == KC - 1))
                    return ins
                S.op("pe", fn, reads=[self.wb[wslot]], writes=[self.psb[pbank]])

            def out_bf(self):
                i = self.oi % self.NOB
                self.oi += 1
                return self.ob[i], self.obb[i]

            def out_f32(self):
                i = self.ofi % self.n_of
                self.ofi += 1
                return self.of[i], self.ofb[i]

            def out_ada(self):
                i = self.adi % 2
                self.adi += 1
                return self.ad[i], self.adb[i]

        def ada_job(P, job):
            l, j = job["l"], job["j"]
            wslot = P.next_w(job["w"][0])

            def fn(e):
                ins = None
                for kc in range(KC):
                    ins = e.matmul(ps[7][0:2, 0:128], scT[:, kc, :], wbt[wslot][:, kc, :], start=(kc == 0), stop=(kc == KC - 1))
                return ins
            S.op("pe", fn, reads=[P.wb[wslot]], writes=[P.psb[7]])
            o, ob = P.out_ada()
            S.op("act", lambda e: e.activation(out=o[0:2, 0:128], in_=ps[7][0:2, 0:128], func=AF.Copy), reads=[P.psb[7]], writes=[ob])
            store("sp", MODD[l, :, j * 128:(j + 1) * 128], o[0:2, 0:128], ob)

        def stage_ada_only(l):
            chk("stage_ada_only")
            AR.reset()
            P = Proj(None, [], n_of=0)
            for job in ada_jobs(l):
                ada_job(P, job)
            S.barrier()

        def act_to_global(segs, c0, n):
            res = []
            for (is_ctx, sc0, ac0, sn) in segs:
                lo = max(c0, ac0)
                hi = min(c0 + n, ac0 + sn)
                if lo < hi:
                    g = (TW if is_ctx else 0) + sc0 + (lo - ac0)
                    res.append((lo, g, hi - lo))
            return res

        def stage_inproj_even(l, actT, segs, ncols, next_ada):
            chk("stage_inproj_even")
            i = l // 2
            P = Proj(actT, tok_blocks(0, ncols), n_of=0)
            cs = [AR.alloc([512], F32) for _ in range(2)]
            sn = [AR.alloc([512], F32) for _ in range(2)]
            csb = [S.buf() for _ in range(2)]
            snb = [S.buf() for _ in range(2)]
            qb = [AR.alloc([512], BF16) for _ in range(2)]
            qbb = [S.buf() for _ in range(2)]
            t1 = [AR.alloc([512], F32) for _ in range(2)]
            t1b = [S.buf() for _ in range(2)]
            t2 = [AR.alloc([512], F32) for _ in range(2)]
            t2b = [S.buf() for _ in range(2)]
            ri = [0]
            kinds = [("q", 16, QT), ("k", 4, KT), ("v", 4, VT), ("ga", 16, GA), ("u", 16, UU), ("vg", 16, VG), ("gb", 16, GB)]
            jcol = 0
            adaj = list(next_ada)
            nada_per = (len(adaj) + 87) // 88 if adaj else 0
            _kj = int(_os.environ.get("KJOBS", "1000"))
            _ks = int(_os.environ.get("KSKIP", "0"))
            if _os.environ.get("KNOADA"):
                adaj = []
            for (kind, nblk, dst) in kinds:
                for jb in range(nblk):
                    if jcol >= _kj or jcol < _ks:
                        jcol += 1
                        continue
                    wslot = P.next_w(ab_w_in[i, :, jcol * 128:(jcol + 1) * 128])
                    jcol += 1
                    for (c0, n) in P.tb:
                        pbank = P.pr % 4
                        P.pr += 1
                        P.mm_group(wslot, c0, n, pbank)
                        o, ob = P.out_bf()
                        if kind in ("q", "k"):
                            r = ri[0] % 2
                            ri[0] += 1
                            for (ac, g, nn) in ([] if _os.environ.get("KROPE") in ("1", "2") else act_to_global(segs, c0, n)):
                                load("sp", cs[r][:, ac - c0:ac - c0 + nn], c_cos[:, g:g + nn], csb[r])
                                load("sp", sn[r][:, ac - c0:ac - c0 + nn], c_sin[:, g:g + nn], snb[r])
                            S.op("act", lambda e, r=r, n=n, pbank=pbank: e.activation(out=qb[r][:, 0:n], in_=ps[pbank][:, 0:n], func=AF.Copy),
                                 reads=[P.psb[pbank]], writes=[qbb[r]])
                            p2 = 4 + r
                            S.op("pe", lambda e, r=r, n=n, p2=p2: e.matmul(ps[p2][:, 0:n], perm[:], qb[r][:, 0:n], start=True, stop=True),
                                 reads=[qbb[r]], writes=[P.psb[p2]])
                            if _os.environ.get("KROPE") == "2":
                                S.op("act", lambda e, n=n, p2=p2, o=o: e.activation(out=o[:, 0:n], in_=ps[p2][:, 0:n], func=AF.Copy),
                                     reads=[P.psb[p2]], writes=[ob])
                                for (ac, g, nn) in act_to_global(segs, c0, n):
                                    store("sp", dst[jb * 128:(jb + 1) * 128, g:g + nn], o[:, ac - c0:ac - c0 + nn], ob)
                                continue
                            S.op("act", lambda e, r=r, n=n, pbank=pbank: e.activation(out=t1[r][:, 0:n], in_=ps[pbank][:, 0:n], func=AF.Copy),
                                 reads=[P.psb[pbank]], writes=[t1b[r]])
                            S.op("act", lambda e, r=r, n=n, p2=p2: e.activation(out=t2[r][:, 0:n], in_=ps[p2][:, 0:n], func=AF.Copy),
                                 reads=[P.psb[p2]], writes=[t2b[r]])
                            S.op("dve", lambda e, r=r, n=n: e.tensor_tensor(out=t1[r][:, 0:n], in0=t1[r][:, 0:n], in1=cs[r][:, 0:n], op=ALU.mult),
                                 reads=[t1b[r], csb[r]], writes=[t1b[r]])
                            S.op("dve", lambda e, r=r, n=n: e.tensor_tensor(out=t2[r][:, 0:n], in0=t2[r][:, 0:n], in1=sn[r][:, 0:n], op=ALU.mult),
                                 reads=[t2b[r], snb[r]], writes=[t2b[r]])
                            S.op("dve", lambda e, r=r, n=n, o=o: e.tensor_tensor(out=o[:, 0:n], in0=t1[r][:, 0:n], in1=t2[r][:, 0:n], op=ALU.add),
                                 reads=[t1b[r], t2b[r]], writes=[ob])
                        else:
                            func = {"v": AF.Copy, "ga": AF.Silu, "gb": AF.Silu, "u": AF.Gelu, "vg": AF.Gelu}[kind]
                            S.op("act", lambda e, n=n, pbank=pbank, o=o, func=func: e.activation(out=o[:, 0:n], in_=ps[pbank][:, 0:n], func=func),
                                 reads=[P.psb[pbank]], writes=[ob])
                        for (ac, g, nn) in act_to_global(segs, c0, n):
                            store("sp", dst[jb * 128:(jb + 1) * 128, g:g + nn], o[:, ac - c0:ac - c0 + nn], ob)
                    for _ in range(nada_per):
                        if adaj:
                            ada_job(P, adaj.pop(0))
            while adaj:
                ada_job(P, adaj.pop(0))

        def stage_inproj_odd(l, actT, segs, ncols, next_ada):
            chk("stage_inproj_odd")
            i = l // 2
            P = Proj(actT, tok_blocks(0, ncols), n_of=0)
            tmk = AR.alloc([ncols], F32)
            tmkb = S.buf()
            for (is_ctx, sc0, ac0, sn_) in segs:
                g = (TW if is_ctx else 0) + sc0
                load("sp", tmk[:, ac0:ac0 + sn_], c_tmask[:, g:g + sn_], tmkb)
            sg_ = [AR.alloc([512], F32) for _ in range(2)]
            sgb = [S.buf() for _ in range(2)]
            tt = [AR.alloc([512], F32) for _ in range(2)]
            ttb = [S.buf() for _ in range(2)]
            ri = 0
            adaj = list(next_ada)
            nada_per = (len(adaj) + 31) // 32 if adaj else 0
            for jb in range(KC):
                wa = P.next_w(cv_w_in[i, :, jb * 128:(jb + 1) * 128])
                wb_ = P.next_w(cv_w_in[i, :, D + jb * 128:D + (jb + 1) * 128])
                for (c0, n) in P.tb:
                    pa = (P.pr % 2) * 2
                    pb = pa + 1
                    P.pr += 1
                    P.mm_group(wa, c0, n, pa)
                    P.mm_group(wb_, c0, n, pb)
                    r = ri % 2
                    ri += 1
                    o, ob = P.out_bf()
                    S.op("act", lambda e, r=r, n=n, pb=pb: e.activation(out=sg_[r][:, 0:n], in_=ps[pb][:, 0:n], func=AF.Sigmoid),
                         reads=[P.psb[pb]], writes=[sgb[r]])
                    S.op("act", lambda e, r=r, n=n, pa=pa: e.activation(out=tt[r][:, 0:n], in_=ps[pa][:, 0:n], func=AF.Copy),
                         reads=[P.psb[pa]], writes=[ttb[r]])
                    S.op("dve", lambda e, r=r, n=n: e.tensor_tensor(out=tt[r][:, 0:n], in0=tt[r][:, 0:n], in1=sg_[r][:, 0:n], op=ALU.mult),
                         reads=[ttb[r], sgb[r]], writes=[ttb[r]])
                    S.op("dve", lambda e, r=r, n=n, c0=c0, o=o: e.tensor_tensor(out=o[:, 0:n], in0=tt[r][:, 0:n], in1=tmk[:, c0:c0 + n], op=ALU.mult),
                         reads=[ttb[r], tmkb], writes=[ob])
                    for (ac, g, nn) in act_to_global(segs, c0, n):
                        store("sp", GLU[jb * 128:(jb + 1) * 128, g:g + nn], o[:, ac - c0:ac - c0 + nn], ob)
                wg = P.next_w(cv_w_in[i, :, 2 * D + jb * 128:2 * D + (jb + 1) * 128])
                for (c0, n) in P.tb:
                    pg_ = 4 + (P.pr % 2)
                    P.pr += 1
                    P.mm_group(wg, c0, n, pg_)
                    o, ob = P.out_bf()
                    S.op("act", lambda e, n=n, pg_=pg_, o=o: e.activation(out=o[:, 0:n], in_=ps[pg_][:, 0:n], func=AF.Silu),
                         reads=[P.psb[pg_]], writes=[ob])
                    for (ac, g, nn) in act_to_global(segs, c0, n):
                        store("sp", SG[jb * 128:(jb + 1) * 128, g:g + nn], o[:, ac - c0:ac - c0 + nn], ob)
                for _ in range(nada_per):
                    if adaj:
                        ada_job(P, adaj.pop(0))
            while adaj:
                ada_job(P, adaj.pop(0))

        def stage_outproj(l, w_out, segs, ncols):
            chk("stage_outproj")
            AR.reset()
            actT = AR.alloc([KC, ncols], BF16)
            ldb = [S.buf() for _ in range(8)]
            for qd in range(8):
                for (is_ctx, sc0, ac0, sn_) in segs:
                    g = (TW if is_ctx else 0) + sc0
                    load("sp", actT[:, qd * 4:(qd + 1) * 4, ac0:ac0 + sn_],
                         MT[qd * 512:(qd + 1) * 512, g:g + sn_].rearrange("(c p) t -> p c t", p=128), ldb[qd])
            P = Proj(actT, tok_blocks(0, ncols))
            assert len(P.tb) <= 4
            sq = [AR.alloc([512], BF16) for _ in range(2)]
            sqb = [S.buf() for _ in range(2)]
            accb = [S.buf() for _ in range(4)]
            ri = 0
            first = True
            for jb in range(KC):
                wslot = P.next_w(w_out[:, jb * 128:(jb + 1) * 128])
                for ti, (c0, n) in enumerate(P.tb):
                    pbank = 4 + (P.pr % 3)
                    P.pr += 1
                    actT_ = actT

                    def fn(e, wslot=wslot, c0=c0, n=n, pbank=pbank):
                        ins = None
                        for kc in range(KC):
                            ins = e.matmul(ps[pbank][:, 0:n], wbt[wslot][:, kc, :], actT_[:, kc, c0:c0 + n],
                                           start=(kc == 0), stop=(kc == KC - 1))
                        return ins
                    S.op("pe", fn, reads=[P.wb[wslot]] + (ldb if first else []), writes=[P.psb[pbank]])
                    first = False
                    o, ob = P.out_f32()
                    r = ri % 2
                    ri += 1
                    S.op("act", lambda e, n=n, pbank=pbank, o=o: e.activation(out=o[:, 0:n], in_=ps[pbank][:, 0:n], func=AF.Copy),
                         reads=[P.psb[pbank]], writes=[ob])
                    S.op("act", lambda e, n=n, pbank=pbank, r=r: e.activation(out=sq[r][:, 0:n], in_=ps[pbank][:, 0:n], func=AF.Square),
                         reads=[P.psb[pbank]], writes=[sqb[r]])
                    S.op("pe", lambda e, n=n, r=r, ti=ti, jb=jb: e.matmul(ps[ti][:, 0:n], ones[:], sq[r][:, 0:n], start=(jb == 0), stop=(jb == KC - 1)),
                         reads=[sqb[r]], writes=[accb[ti]])
                    store("sp", YY[jb * 128:(jb + 1) * 128, c0:c0 + n], o[:, 0:n], ob)
            rpb = S.buf()
            for ti, (c0, n) in enumerate(P.tb):
                S.op("dve", lambda e, ti=ti, c0=c0, n=n: e.tensor_scalar(out=rpost[:, c0:c0 + n], in0=ps[ti][:, 0:n], scalar1=1.0 / D, scalar2=RMS_EPS, op0=ALU.mult, op1=ALU.add),
                     reads=[accb[ti]], writes=[rpb])
            S.op("act", lambda e: e.activation(out=rpost[:, 0:ncols], in_=rpost[:, 0:ncols], func=AF.Sqrt), reads=[rpb], writes=[rpb])
            S.op("dve", lambda e: e.reciprocal(out=rpost[:, 0:ncols], in_=rpost[:, 0:ncols]), reads=[rpb], writes=[rpb])
            S.barrier()

        def stage_postnorm(l, segs, x_in, c_in, x_out, c_out, final):
            chk("stage_postnorm")
            AR.reset()
            NB_ = 3
            yt = [AR.alloc([512], F32) for _ in range(NB_)]
            ytb = [S.buf() for _ in range(NB_)]
            xt = [AR.alloc([512], F32) for _ in range(NB_)]
            xtb = [S.buf() for _ in range(NB_)]
            ot = [AR.alloc([512], F32) for _ in range(NB_)]
            otb = [S.buf() for _ in range(NB_)]
            it = 0
            for (is_ctx, sc0, ac0, sn_) in segs:
                for (b0, n) in tok_blocks(0, sn_):
                    for kc in range(KC):
                        s = it % NB_
                        it += 1
                        load("sp", yt[s][:, 0:n], YY[kc * 128:(kc + 1) * 128, ac0 + b0:ac0 + b0 + n], ytb[s])
                        xsrc = c_in if is_ctx else x_in
                        load("sp", xt[s][:, 0:n], xsrc[kc * 128:(kc + 1) * 128, sc0 + b0:sc0 + b0 + n], xtb[s])
                        vo = 5 if is_ctx else 2
                        S.op("dve", lambda e, s=s, n=n, kc=kc, vo=vo, a0=ac0 + b0: e.scalar_tensor_tensor(
                            out=yt[s][:, 0:n], in0=yt[s][:, 0:n], scalar=vecs[:, vo, kc:kc + 1], in1=rpost[:, a0:a0 + n], op0=ALU.mult, op1=ALU.mult),
                            reads=[ytb[s]], writes=[ytb[s]])
                        S.op("pool", lambda e, s=s, n=n: e.tensor_tensor(out=ot[s][:, 0:n], in0=yt[s][:, 0:n], in1=xt[s][:, 0:n], op=ALU.add),
                             reads=[ytb[s], xtb[s]], writes=[otb[s]])
                        if is_ctx:
                            dst = c_out[kc * 128:(kc + 1) * 128, sc0 + b0:sc0 + b0 + n]
                        elif final:
                            dst = x_out[kc * 128:(kc + 1) * 128, sc0 + b0 - HALO:sc0 + b0 - HALO + n]
                        else:
                            dst = x_out[kc * 128:(kc + 1) * 128, sc0 + b0:sc0 + b0 + n]
                        store("sp", dst, ot[s][:, 0:n], otb[s])
            S.barrier()

        def stage_attn(l):
            chk("stage_attn")
            i = l // 2
            AR.reset()
            ilo, ihi = IN_R[l]
            olo, ohi = OUT_R[l]
            do_ctxq = CTX_OUT[l]
            nk_lat = (ihi - ilo) * 128
            NK = nk_lat + CTX
            nq_lat = (ohi - olo) * 128
            NQ = nq_lat + (CTX if do_ctxq else 0)
            QTh = [AR.alloc([4, NQ], BF16) for _ in range(1)]
            GAh = [AR.alloc([4, NQ], BF16) for _ in range(1)]
            KTh = AR.alloc([NK], BF16)
            VTh = AR.alloc([NK], BF16)
            nkt = NK // 128
            Vtok = AR.alloc([nkt, 128], BF16)
            sinkb = AR.alloc([16], F32)
            se = AR.alloc([16], F32)
            Eb = [AR.alloc([512], BF16) for _ in range(5)]
            dn = AR.alloc([512], F32)
            tO = AR.alloc([512], F32)
            NMO = 2
            mo = [AR.alloc([512], BF16) for _ in range(NMO)]
            bq, bg, bk, bv, bvt, bsk, bse, bdn, btO = [S.buf() for _ in range(9)]
            bE = [S.buf() for _ in range(5)]
            bmo = [S.buf() for _ in range(NMO)]
            pb = [S.buf() for _ in range(8)]
            load("sp", sinkb, ab_sinkB[i], bsk)
            S.op("act", lambda e: e.activation(out=se, in_=sinkb, func=AF.Exp), reads=[bsk], writes=[bse])
            moi = 0
            for hk in range(4):
                load("sp", QTh[0][:, :, 0:nq_lat], QT[hk * 512:(hk + 1) * 512, olo * 128:ohi * 128].rearrange("(h d) t -> d h t", d=128), bq)
                load("sp", GAh[0][:, :, 0:nq_lat], GA[hk * 512:(hk + 1) * 512, olo * 128:ohi * 128].rearrange("(h d) t -> d h t", d=128), bg)
                load("sp", KTh[:, 0:nk_lat], KT[hk * 128:(hk + 1) * 128, ilo * 128:ihi * 128], bk)
                load("sp", VTh[:, 0:nk_lat], VT[hk * 128:(hk + 1) * 128, ilo * 128:ihi * 128], bv)
                if do_ctxq:
                    load("sp", QTh[0][:, :, nq_lat:NQ], QT[hk * 512:(hk + 1) * 512, TW:TALL].rearrange("(h d) t -> d h t", d=128), bq)
                    load("sp", GAh[0][:, :, nq_lat:NQ], GA[hk * 512:(hk + 1) * 512, TW:TALL].rearrange("(h d) t -> d h t", d=128), bg)
                load("sp", KTh[:, nk_lat:NK], KT[hk * 128:(hk + 1) * 128, TW:TALL], bk)
                load("sp", VTh[:, nk_lat:NK], VT[hk * 128:(hk + 1) * 128, TW:TALL], bv)
                for kt in range(nkt):
                    pbank = 6 + (kt % 2)
                    pv = ps[pbank][:].bitcast(BF16)
                    S.op("pe", lambda e, kt=kt, pv=pv: e.transpose(pv[:, 0:128], VTh[:, kt * 128:(kt + 1) * 128], ident[:]),
                         reads=[bv], writes=[pb[pbank]])
                    S.op("dve", lambda e, kt=kt, pv=pv: e.tensor_copy(out=Vtok[:, kt, :], in_=pv[:, 0:128]),
                         reads=[pb[pbank]], writes=[bvt])
                qtiles = [("lat", t) for t in range(olo, ohi)] + ([("ctx", 0), ("ctx", 1)] if do_ctxq else [])
                for (qk, t) in qtiles:
                    if qk == "lat":
                        qc = (t - olo) * 128
                        klist = [(t - 1 - ilo, t - 1, 0), (t - ilo, t, None), (t + 1 - ilo, t + 1, 1)]
                        klist += [(nk_lat // 128, 16, None), (nk_lat // 128 + 1, 17, None)]
                    else:
                        qc = nq_lat + t * 128
                        klist = [(nk_lat // 128, 16, None), (nk_lat // 128 + 1, 17, None)]
                    nkl = len(klist)
                    for ki, (kti, kbi, mk) in enumerate(klist):
                        S.op("pe", lambda e, ki=ki, kti=kti, qc=qc: e.matmul(ps[ki][:, :].rearrange("p (h q) -> p h q", h=4),
                                                                           KTh[:, kti * 128:(kti + 1) * 128], QTh[0][:, :, qc:qc + 128], start=True, stop=True),
                             reads=[bk, bq], writes=[pb[ki]])
                        S.op("act", lambda e, ki=ki, kbi=kbi: e.activation(out=Eb[ki][:], in_=ps[ki][:], func=AF.Exp, bias=kbias[:, kbi:kbi + 1], scale=ATT_SCALE),
                             reads=[pb[ki]], writes=[bE[ki]])
                        if mk is not None:
                            S.op("dve", lambda e, ki=ki, mk=mk: e.tensor_tensor(out=Eb[ki][:], in0=Eb[ki][:], in1=tri[:, mk, :], op=ALU.mult),
                                 reads=[bE[ki]], writes=[bE[ki]])
                    for ki, (kti, kbi, mk) in enumerate(klist):
                        S.op("pe", lambda e, ki=ki, nkl=nkl: e.matmul(ps[5][:], ones[:], Eb[ki][:], start=(ki == 0), stop=(ki == nkl - 1)),
                             reads=[bE[ki]], writes=[pb[5]])
                    for ki, (kti, kbi, mk) in enumerate(klist):
                        S.op("pe", lambda e, ki=ki, kti=kti, nkl=nkl: e.matmul(ps[6][:], Vtok[:, kti, :], Eb[ki][:], start=(ki == 0), stop=(ki == nkl - 1)),
                             reads=[bE[ki], bvt], writes=[pb[6]])
                    for h in range(4):
                        S.op("dve", lambda e, h=h, hk=hk: e.tensor_scalar(out=dn[:, h * 128:(h + 1) * 128], in0=ps[5][:, h * 128:(h + 1) * 128],
                                                                         scalar1=se[:, hk * 4 + h:hk * 4 + h + 1], scalar2=None, op0=ALU.add),
                             reads=[pb[5], bse], writes=[bdn])
                    S.op("dve", lambda e: e.reciprocal(out=dn[:], in_=dn[:]), reads=[bdn], writes=[bdn])
                    S.op("dve", lambda e: e.tensor_tensor(out=tO[:], in0=ps[6][:], in1=dn[:], op=ALU.mult), reads=[pb[6], bdn], writes=[btO])
                    m = moi % NMO
                    moi += 1
                    S.op("dve", lambda e, m=m, qc=qc: e.tensor_tensor(out=mo[m][:].rearrange("p (h q) -> p h q", h=4), in0=tO[:].rearrange("p (h q) -> p h q", h=4),
                                                                     in1=GAh[0][:, :, qc:qc + 128], op=ALU.mult),
                         reads=[btO, bg], writes=[bmo[m]])
                    gcol = (t * 128) if qk == "lat" else (TW + t * 128)
                    store("sp", MT[hk * 512:(hk + 1) * 512, gcol:gcol + 128].rearrange("(h d) t -> d h t", d=128),
                          mo[m][:].rearrange("p (h q) -> p h q", h=4), bmo[m])
                S.barrier()

        def stage_gmlp(l):
            chk("stage_gmlp")
            i = l // 2
            AR.reset()
            olo, ohi = OUT_R[l]
            tiles = [t * 128 for t in range(olo, ohi)] + ([TW, TW + 128] if CTX_OUT[l] else [])
            wsT = AR.alloc([16, 128], BF16)
            wsb = AR.alloc([2048], F32)
            lg = AR.alloc([16], F32)
            lb = AR.alloc([16], F32)
            bws, bwsb, blg, blb = [S.buf() for _ in range(4)]
            wsf = AR.alloc([16, 128], F32)
            bwsf = S.buf()
            load("sp", wsf, ab_wsT[i], bwsf)
            S.op("act", lambda e: e.activation(out=wsT[:], in_=wsf[:], func=AF.Copy), reads=[bwsf], writes=[bws])
            load("sp", wsb, ab_wsbB[i], bwsb)
            load("sp", lg, ab_ln_gT[i], blg)
            load("sp", lb, ab_ln_bT[i], blb)
            NR = 2
            vg = [AR.alloc([16, 128], BF16) for _ in range(NR)]
            uu = [AR.alloc([16, 128], BF16) for _ in range(NR)]
            gb = [AR.alloc([16, 128], BF16) for _ in range(NR)]
            bvg = [S.buf() for _ in range(NR)]
            buu = [S.buf() for _ in range(NR)]
            bgb = [S.buf() for _ in range(NR)]
            sq = AR.alloc([16, 128], BF16)
            bsq = S.buf()
            mean = AR.alloc([128], F32)
            msq = AR.alloc([128], F32)
            rstd = AR.alloc([128], F32)
            bmean, bmsq, brstd = S.buf(), S.buf(), S.buf()
            tn = AR.alloc([16, 128], F32)
            btn = S.buf()
            vn = AR.alloc([16, 128], BF16)
            bvn = S.buf()
            vlnT = AR.alloc([2048], BF16)
            bvl = S.buf()
            t1 = [AR.alloc([512], F32) for _ in range(2)]
            bt1 = [S.buf() for _ in range(2)]
            mo = [AR.alloc([512], BF16) for _ in range(4)]
            bmo = [S.buf() for _ in range(4)]
            pb = [S.buf() for _ in range(8)]
            for ti, gc in enumerate(tiles):
                r = ti % NR
                load("sp", vg[r], VG[:, gc:gc + 128].rearrange("(g c) t -> c g t", c=128), bvg[r])
                load("sp", uu[r], UU[:, gc:gc + 128].rearrange("(g c) t -> c g t", c=128), buu[r])
                load("sp", gb[r], GB[:, gc:gc + 128].rearrange("(g c) t -> c g t", c=128), bgb[r])
                S.op("act", lambda e, r=r: e.activation(out=sq[:], in_=vg[r][:], func=AF.Square), reads=[bvg[r]], writes=[bsq])

                def fsum(e, r=r):
                    ins = None
                    for g in range(16):
                        ins = e.matmul(ps[0][:, 0:128], ones[:], vg[r][:, g, :], start=(g == 0), stop=(g == 15))
                    return ins
                S.op("pe", fsum, reads=[bvg[r]], writes=[pb[0]])

                def fsq(e):
                    ins = None
                    for g in range(16):
                        ins = e.matmul(ps[1][:, 0:128], ones[:], sq[:, g, :], start=(g == 0), stop=(g == 15))
                    return ins
                S.op("pe", fsq, reads=[bsq], writes=[pb[1]])
                S.op("dve", lambda e: e.tensor_scalar(out=mean[:], in0=ps[0][:, 0:128], scalar1=1.0 / 2048, scalar2=None, op0=ALU.mult),
                     reads=[pb[0]], writes=[bmean])
                S.op("dve", lambda e: e.tensor_tensor(out=msq[:], in0=mean[:], in1=mean[:], op=ALU.mult), reads=[bmean], writes=[bmsq])
                S.op("dve", lambda e: e.scalar_tensor_tensor(out=rstd[:], in0=ps[1][:, 0:128], scalar=1.0 / 2048, in1=msq[:], op0=ALU.mult, op1=ALU.subtract),
                     reads=[pb[1], bmsq], writes=[brstd])
                S.op("dve", lambda e: e.tensor_scalar(out=rstd[:], in0=rstd[:], scalar1=LN_EPS, scalar2=None, op0=ALU.add), reads=[brstd], writes=[brstd])
                S.op("act", lambda e: e.activation(out=rstd[:], in_=rstd[:], func=AF.Sqrt), reads=[brstd], writes=[brstd])
                S.op("dve", lambda e: e.reciprocal(out=rstd[:], in_=rstd[:]), reads=[brstd], writes=[brstd])
                for g in range(16):
                    S.op("dve", lambda e, r=r, g=g: e.tensor_tensor(out=tn[:, g, :], in0=vg[r][:, g, :], in1=mean[:], op=ALU.subtract),
                         reads=[bvg[r], bmean], writes=[btn])
                    S.op("dve", lambda e, g=g: e.tensor_tensor(out=tn[:, g, :], in0=tn[:, g, :], in1=rstd[:], op=ALU.mult),
                         reads=[btn, brstd], writes=[btn])
                    S.op("act", lambda e, g=g: e.activation(out=vn[:, g, :], in_=tn[:, g, :], func=AF.Identity, bias=lb[:, g:g + 1], scale=lg[:, g:g + 1]),
                         reads=[btn, blg, blb], writes=[bvn])
                for half in range(2):
                    pbank = 2 + half
                    pv = ps[pbank][:].bitcast(BF16)

                    def ftr(e, half=half, pv=pv):
                        ins = None
                        for g8 in range(8):
                            g = half * 8 + g8
                            ins = e.transpose(pv[:, g8 * 128:(g8 + 1) * 128], vn[:, g, :], ident[:])
                        return ins
                    S.op("pe", ftr, reads=[bvn], writes=[pb[pbank]])
                    S.op("act", lambda e, half=half, pv=pv: e.activation(out=vlnT[:, half * 1024:(half + 1) * 1024], in_=pv[:, 0:1024], func=AF.Copy),
                         reads=[pb[pbank]], writes=[bvl])
                for g4 in range(4):
                    pbank = 4 + g4

                    def fsp(e, g4=g4, pbank=pbank):
                        ins = None
                        for gg in range(4):
                            g = g4 * 4 + gg
                            ins = e.matmul(ps[pbank][:, gg * 128:(gg + 1) * 128], vlnT[:, g * 128:(g + 1) * 128], wsT[:, g, :], start=True, stop=True)
                        return ins
                    S.op("pe", fsp, reads=[bvl, bws], writes=[pb[pbank]])
                    q = g4 % 2
                    S.op("dve", lambda e, g4=g4, pbank=pbank, q=q: e.tensor_tensor(out=t1[q][:], in0=ps[pbank][:], in1=wsb[:, g4 * 512:(g4 + 1) * 512], op=ALU.add),
                         reads=[pb[pbank], bwsb], writes=[bt1[q]])
                    S.op("dve", lambda e, g4=g4, q=q, r=r: e.tensor_tensor(out=t1[q][:].rearrange("p (g t) -> p g t", g=4), in0=t1[q][:].rearrange("p (g t) -> p g t", g=4),
                                                                         in1=uu[r][:, g4 * 4:(g4 + 1) * 4, :], op=ALU.mult),
                         reads=[bt1[q], buu[r]], writes=[bt1[q]])
                    S.op("dve", lambda e, g4=g4, q=q, r=r: e.tensor_tensor(out=mo[g4][:].rearrange("p (g t) -> p g t", g=4), in0=t1[q][:].rearrange("p (g t) -> p g t", g=4),
                                                                         in1=gb[r][:, g4 * 4:(g4 + 1) * 4, :], op=ALU.mult),
                         reads=[bt1[q], bgb[r]], writes=[bmo[g4]])
                    store("sp", MT[2048 + g4 * 512:2048 + (g4 + 1) * 512, gc:gc + 128].rearrange("(g c) t -> c g t", c=128),
                          mo[g4][:].rearrange("p (g t) -> p g t", g=4), bmo[g4])
            S.barrier()

        def stage_conv(l):
            chk("stage_conv")
            i = l // 2
            AR.reset()
            olo, ohi = OUT_R[l]
            dw = AR.alloc([KC, 31], F32)
            dwb = AR.alloc([KC], F32)
            lg = AR.alloc([KC], F32)
            lb = AR.alloc([KC], F32)
            bdw, bdwb, blg, blb = [S.buf() for _ in range(4)]
            load("sp", dw, cv_dwT[i], bdw)
            load("sp", dwb, cv_dw_bT[i], bdwb)
            load("sp", lg, cv_ln_gT[i], blg)
            load("sp", lb, cv_ln_bT[i], blb)
            NB_ = 256
            blocks = [(False, c0, n) for (c0, n) in tok_blocks(olo * 128, ohi * 128, NB_)]
            if CTX_OUT[l]:
                blocks.append((True, 0, CTX))
            ybuf = [AR.alloc([KC, NB_], F32) for _ in range(2)]
            by = [[S.buf() for _ in range(KC)] for _ in range(2)]
            NG = 4
            gin = [AR.alloc([NB_ + 32], BF16) for _ in range(NG)]
            dg = [AR.alloc([31, 128], BF16) for _ in range(2)]
            bdg = [S.buf() for _ in range(2)]
            bgin = [S.buf() for _ in range(NG)]
            pacc = [AR.alloc([NB_], F32) for _ in range(3)]
            bpacc = [S.buf() for _ in range(3)]
            yb = [AR.alloc([NB_], BF16) for _ in range(2)]
            byb = [S.buf() for _ in range(2)]
            sq = [AR.alloc([NB_], BF16) for _ in range(2)]
            bsq = [S.buf() for _ in range(2)]
            mean = [AR.alloc([NB_], F32) for _ in range(2)]
            msq = AR.alloc([NB_], F32)
            rstd = [AR.alloc([NB_], F32) for _ in range(2)]
            bmean = [S.buf() for _ in range(2)]
            bmsq = S.buf()
            brstd = [S.buf() for _ in range(2)]
            sgt = [AR.alloc([NB_], BF16) for _ in range(3)]
            bsg = [S.buf() for _ in range(3)]
            tz = [AR.alloc([NB_], F32) for _ in range(2)]
            btz = [S.buf() for _ in range(2)]
            zz = [AR.alloc([NB_], F32) for _ in range(2)]
            bzz = [S.buf() for _ in range(2)]
            mo = [AR.alloc([NB_], BF16) for _ in range(3)]
            bmo = [S.buf() for _ in range(3)]
            pb = [S.buf() for _ in range(8)]
            gi = 0
            for bi, (is_ctx, c0, n) in enumerate(blocks):
                yy = ybuf[bi % 2]
                byy = by[bi % 2]
                psum_s = ps[(bi % 2) * 2]
                psum_q = ps[(bi % 2) * 2 + 1]
                pbs = pb[(bi % 2) * 2]
                pbq = pb[(bi % 2) * 2 + 1]
                gbase = TW if is_ctx else 0
                for kc in range(KC):
                    s = gi % NG
                    gi += 1
                    if is_ctx:
                        S.op("pool", lambda e, s=s: e.memset(gin[s][:, :], 0.0), writes=[bgin[s]])
                        load("sp", gin[s][:, 15:15 + n], GLU[kc * 128:(kc + 1) * 128, TW:TW + n], bgin[s])
                    else:
                        load("sp", gin[s][:, 0:n + 30], GLU[kc * 128:(kc + 1) * 128, c0 - 15:c0 + n + 15], bgin[s])
                    d2 = kc % 2
                    def fdg(e, d2=d2, kc=kc):
                        ins = None
                        for j in range(31):
                            ins = e.tensor_scalar(out=dg[d2][:, j, :], in0=ident[:], scalar1=dw[:, kc, j:j + 1], scalar2=None, op0=ALU.mult)
                        return ins
                    S.op("dve", fdg, reads=[bdw], writes=[bdg[d2]])
                    pc = 4 + (kc % 4)

                    def fconv(e, s=s, d2=d2, n=n, pc=pc):
                        ins = None
                        for j in range(31):
                            ins = e.matmul(ps[pc][:, 0:n], dg[d2][:, j, :], gin[s][:, j:j + n], start=(j == 0), stop=(j == 30))
                        return ins
                    S.op("pe", fconv, reads=[bdg[d2], bgin[s]], writes=[pb[pc]])
                    S.op("act", lambda e, kc=kc, n=n, pc=pc, yy=yy: e.activation(out=yy[:, kc, 0:n], in_=ps[pc][:, 0:n], func=AF.Identity, bias=dwb[:, kc:kc + 1], scale=1.0),
                         reads=[pb[pc], bdwb], writes=[byy[kc]])
                    q = kc % 2
                    S.op("act", lambda e, kc=kc, n=n, q=q, pc=pc: e.activation(out=yb[q][:, 0:n], in_=ps[pc][:, 0:n], func=AF.Identity, bias=dwb[:, kc:kc + 1], scale=1.0), reads=[pb[pc], bdwb], writes=[byb[q]])
                    S.op("act", lambda e, kc=kc, n=n, q=q, pc=pc: e.activation(out=sq[q][:, 0:n], in_=ps[pc][:, 0:n], func=AF.Square, bias=dwb[:, kc:kc + 1], scale=1.0), reads=[pb[pc], bdwb], writes=[bsq[q]])
                    S.op("pe", lambda e, kc=kc, n=n, q=q, psum_s=psum_s: e.matmul(psum_s[:, 0:n], ones[:], yb[q][:, 0:n], start=(kc == 0), stop=(kc == KC - 1)),
                         reads=[byb[q]], writes=[pbs])
                    S.op("pe", lambda e, kc=kc, n=n, q=q, psum_q=psum_q: e.matmul(psum_q[:, 0:n], ones[:], sq[q][:, 0:n], start=(kc == 0), stop=(kc == KC - 1)),
                         reads=[bsq[q]], writes=[pbq])
                mm = mean[bi % 2]
                rr = rstd[bi % 2]
                bm = bmean[bi % 2]
                br = brstd[bi % 2]
                S.op("dve", lambda e, n=n, mm=mm, psum_s=psum_s: e.tensor_scalar(out=mm[:, 0:n], in0=psum_s[:, 0:n], scalar1=1.0 / D, scalar2=None, op0=ALU.mult), reads=[pbs], writes=[bm])
                S.op("dve", lambda e, n=n, mm=mm: e.tensor_tensor(out=msq[:, 0:n], in0=mm[:, 0:n], in1=mm[:, 0:n], op=ALU.mult), reads=[bm], writes=[bmsq])
                S.op("dve", lambda e, n=n, rr=rr, psum_q=psum_q: e.scalar_tensor_tensor(out=rr[:, 0:n], in0=psum_q[:, 0:n], scalar=1.0 / D, in1=msq[:, 0:n], op0=ALU.mult, op1=ALU.subtract),
                     reads=[pbq, bmsq], writes=[br])
                S.op("dve", lambda e, n=n, rr=rr: e.tensor_scalar(out=rr[:, 0:n], in0=rr[:, 0:n], scalar1=LN_EPS, scalar2=None, op0=ALU.add), reads=[br], writes=[br])
                S.op("act", lambda e, n=n, rr=rr: e.activation(out=rr[:, 0:n], in_=rr[:, 0:n], func=AF.Sqrt), reads=[br], writes=[br])
                S.op("dve", lambda e, n=n, rr=rr: e.reciprocal(out=rr[:, 0:n], in_=rr[:, 0:n]), reads=[br], writes=[br])
                for kc in range(KC):
                    q = kc % 2
                    s3 = kc % 3
                    load("sp", sgt[s3][:, 0:n], SG[kc * 128:(kc + 1) * 128, gbase + c0:gbase + c0 + n], bsg[s3])
                    S.op("pool", lambda e, kc=kc, n=n, q=q, yy=yy, mm=mm: e.tensor_tensor(out=tz[q][:, 0:n], in0=yy[:, kc, 0:n], in1=mm[:, 0:n], op=ALU.subtract),
                         reads=[byy[kc], bm], writes=[btz[q]])
                    S.op("pool", lambda e, n=n, q=q, rr=rr: e.tensor_tensor(out=tz[q][:, 0:n], in0=tz[q][:, 0:n], in1=rr[:, 0:n], op=ALU.mult),
                         reads=[btz[q], br], writes=[btz[q]])
                    S.op("act", lambda e, kc=kc, n=n, q=q: e.activation(out=zz[q][:, 0:n], in_=tz[q][:, 0:n], func=AF.Silu, bias=lb[:, kc:kc + 1], scale=lg[:, kc:kc + 1]),
                         reads=[btz[q], blg, blb], writes=[bzz[q]])
                    S.op("pool", lambda e, n=n, q=q, s3=s3: e.tensor_tensor(out=mo[s3][:, 0:n], in0=zz[q][:, 0:n], in1=sgt[s3][:, 0:n], op=ALU.mult),
                         reads=[bzz[q], bsg[s3]], writes=[bmo[s3]])
                    store("sp", MT[kc * 128:(kc + 1) * 128, gbase + c0:gbase + c0 + n], mo[s3][:, 0:n], bmo[s3])
            S.barrier()

        def segs_for(lo, hi, with_ctx):
            segs = [(False, lo * 128, 0, (hi - lo) * 128)]
            ncols = (hi - lo) * 128
            if with_ctx:
                segs.append((True, 0, ncols, CTX))
                ncols += CTX
            return segs, ncols

        def _drive():
          stage_setup()
          stage_ada_only(0)
          x_bufs = [xT, XA, XB, XA, None]
          c_bufs = [ctxT, CA, CB, None, None]
          for l in range(depth):
            even = (l % 2 == 0)
            final = (l == DEPTH - 1)
            stage_modprep(l)
            x_in = x_bufs[l]
            x_out = out if final else x_bufs[l + 1]
            c_in = c_bufs[l]
            c_out = c_bufs[l + 1]
            ilo, ihi = IN_R[l]
            olo, ohi = OUT_R[l]
            segs_in, ncols_in = segs_for(ilo, ihi, CTX_IN[l])
            AR.reset()
            actT = AR.alloc([KC, ncols_in], BF16)
            mark = AR.off
            stage_prenorm(l, x_in, c_in, actT, segs_in)
            S.barrier()
            AR.off = mark
            nada = ada_jobs(l + 1) if l + 1 < depth else []
            if even:
                stage_inproj_even(l, actT, segs_in, ncols_in, nada)
                S.barrier()
                stage_attn(l)
                stage_gmlp(l)
                w_out = ab_w_out[l // 2]
            else:
                stage_inproj_odd(l, actT, segs_in, ncols_in, nada)
                S.barrier()
                stage_conv(l)
                w_out = cv_w_out[l // 2]
            segs_out, ncols_out = segs_for(olo, ohi, CTX_OUT[l])
            stage_outproj(l, w_out, segs_out, ncols_out)
            stage_postnorm(l, segs_out, x_in, c_in, x_out, c_out, final)
            S.barrier(new_epoch=True)

        try:
            _drive()
        except _Stop:
            S.barrier()
        block = es.enter_context(nc.Block())

        @block.tensor
        def _(e):
            S.replay("pe", e)

        @block.scalar
        def _(e):
            S.replay("act", e)

        @block.vector
        def _(e):
            S.replay("dve", e)

        @block.gpsimd
        def _(e):
            S.replay("pool", e)

        @block.sync
        def _(e):
            S.replay("sp", e)

    return nc


def _fm(v, k=KC):
    sh = v.shape[:-1]
    return np.ascontiguousarray(np.swapaxes(v.reshape(sh + (k, 128)), -1, -2))


def _rope_tables(core):
    pos = core * OWN - HALO + np.arange(TW)
    row = (pos // 64).astype(np.float32)
    col = (pos % 64).astype(np.float32)
    inv = (1.0 / (10000.0 ** (np.arange(32, dtype=np.float32) / 32))).astype(np.float32)
    cos = np.ones((128, TALL), np.float32)
    sin = np.zeros((128, TALL), np.float32)
    for d in range(128):
        p = row if d < 64 else col
        ang = (p * inv[d % 32]).astype(np.float32)
        cos[d, :TW] = np.cos(ang)
        s = np.sin(ang)
        sin[d, :TW] = -s if (d % 64) < 32 else s
    return cos, sin


def _prep_inputs(inputs):
    f = lambda a: np.ascontiguousarray(np.asarray(a, dtype=np.float32))
    x = f(inputs["x"])[0]
    ctx = f(inputs["ctx"])[0]
    c = f(inputs["c"])[0]
    c_ctx = f(inputs["c_ctx"])
    shared = {}
    shared["ctxT"] = np.ascontiguousarray(ctx.T)
    shared["cT"] = np.ascontiguousarray(np.stack([_fm(c), _fm(c_ctx)], axis=-1))
    shared["ada_w"] = f(inputs["ada_w"])
    shared["ada_bT"] = _fm(f(inputs["ada_b"]), 96)
    shared["pre_gT"] = _fm(f(inputs["pre_g"]))
    shared["post_gT"] = _fm(f(inputs["post_g"]))
    shared["ab_w_in"] = f(inputs["ab_w_in"])
    shared["ab_sinkB"] = np.ascontiguousarray(np.broadcast_to(f(inputs["ab_sink"])[:, None, :], (2, 128, 16)))
    shared["ab_ln_gT"] = _fm(f(inputs["ab_ln_g"]), 16)
    shared["ab_ln_bT"] = _fm(f(inputs["ab_ln_b"]), 16)
    ws = f(inputs["ab_ws"])
    shared["ab_wsT"] = np.ascontiguousarray(ws.transpose(0, 3, 1, 2))
    wsb = f(inputs["ab_ws_b"]).reshape(2, 1, 2048)
    shared["ab_wsbB"] = np.ascontiguousarray(np.broadcast_to(wsb, (2, 128, 2048)))
    shared["ab_w_out"] = f(inputs["ab_w_out"])
    shared["cv_w_in"] = f(inputs["cv_w_in"])
    dw = f(inputs["cv_dw"])
    shared["cv_dwT"] = np.ascontiguousarray(dw.reshape(2, 31, KC, 128).transpose(0, 3, 2, 1))
    shared["cv_dw_bT"] = _fm(f(inputs["cv_dw_b"]))
    shared["cv_ln_gT"] = _fm(f(inputs["cv_ln_g"]))
    shared["cv_ln_bT"] = _fm(f(inputs["cv_ln_b"]))
    shared["cv_w_out"] = f(inputs["cv_w_out"])
    bf = ml_dtypes.bfloat16
    shared["c_ones"] = np.ones((128, 128), bf)
    shared["c_ident"] = np.eye(128, dtype=np.float32).astype(bf)
    shared["c_identf"] = np.eye(128, dtype=np.float32)
    pm = np.zeros((128, 128), np.float32)
    for d in range(128):
        pm[d + 32 if (d % 64) < 32 else d - 32, d] = 1.0
    shared["c_perm"] = pm.astype(bf)
    kj = np.arange(128)[:, None]
    qi = np.arange(128)[None, :]
    tri = np.stack([np.tile((kj >= qi).astype(np.float32), (1, 4)), np.tile((kj <= qi).astype(np.float32), (1, 4))], axis=1)
    shared["c_tri"] = tri.astype(bf)
    in_maps = []
    for core in range(NCORES):
        m = dict(shared)
        xw = np.zeros((TW, D), np.float32)
        p0 = core * OWN - HALO
        lo = max(p0, 0)
        hi = min(p0 + TW, SEQ)
        xw[lo - p0:hi - p0] = x[lo:hi]
        m["xT"] = np.ascontiguousarray(xw.T)
        cos, sin = _rope_tables(core)
        m["c_cos"] = cos
        m["c_sin"] = sin
        pos = p0 + np.arange(TW)
        valid = ((pos >= 0) & (pos < SEQ))
        kb = np.zeros((128, 18), np.float32)
        kb[:, :16] = np.where(valid.reshape(16, 128).T, 0.0, NEG)
        m["c_kbias"] = kb
        tm = np.ones((128, TALL), np.float32)
        tm[:, :TW] = valid[None, :].astype(np.float32)
        m["c_tmask"] = tm
        in_maps.append(m)
    return in_maps


_NC_CACHE = {}


def kernel(**inputs):
    in_maps = _prep_inputs(inputs)
    if "nc" not in _NC_CACHE:
        _NC_CACHE["nc"] = build_program()
    nc = _NC_CACHE["nc"]
    res = run_bass_kernel_spmd(nc, in_maps, core_ids=list(range(NCORES)))
    outs = [np.asarray(r["out"]) for r in res.results]
    full = np.concatenate([o.T for o in outs], axis=0)
    return np.ascontiguousarray(full[None].astype(np.float32))
```

```python
import numpy as np
import ml_dtypes
from contextlib import ExitStack
import concourse.bass as bass
import concourse.mybir as mybir
from concourse.bass_utils import run_bass_kernel_spmd

F32 = mybir.dt.float32
BF16 = mybir.dt.bfloat16
ALU = mybir.AluOpType
AF = mybir.ActivationFunctionType

NCORES = 8
D = 4096
KC = 32
SEQ = 8192
OWN = 1024
HALO = 512
TW = 2048
CTX = 256
TALL = TW + CTX
DEPTH = 4
RMS_EPS = 1e-6
LN_EPS = 1e-5
NEG = -30000.0
ATT_SCALE = 128 ** -0.5

IN_R = [(0, 16), (1, 15), (2, 14), (3, 13)]
OUT_R = [(1, 15), (2, 14), (3, 13), (4, 12)]
CTX_IN = [True, True, True, False]
CTX_OUT = [True, True, False, False]

ENGS = ["pe", "act", "dve", "pool", "sp"]
NDSEM = 64


class Buf:
    __slots__ = ("name", "w", "r", "ds")

    def __init__(self, name=""):
        self.name = name
        self.w = None
        self.r = []
        self.ds = None


class Sched:
    def __init__(self, nc, es):
        self.nc = nc
        self.q = {e: [] for e in ENGS}
        self.epoch_sems = []
        self.es = es
        self.cnt = {e: 0 for e in ENGS}
        self.esem = {}
        self.seen = {e: {} for e in ENGS}
        self.dsems = [es.enter_context(nc.semaphore(f"dq{i}")) for i in range(NDSEM)]
        self.dcnt = [0] * NDSEM
        self.dnext = 0
        self.dstage = 0
        self.dissued = {e: {} for e in ENGS}
        self.bar = es.enter_context(nc.semaphore("bar"))
        self.nbar = 0
        self.nep = 0
        self.bufs = []
        self.new_epoch()

    def new_epoch(self):
        for e in ENGS:
            self.esem[e] = (f"e{self.nep}_{e}", self.es.enter_context(self.nc.semaphore(f"s{self.nep}_{e}")))
            self.cnt[e] = 0
        self.nep += 1

    def buf(self, name=""):
        b = Buf(name)
        self.bufs.append(b)
        return b

    def _waits(self, eng, reads, writes):
        evs = []
        for b in reads:
            if b.w is not None:
                evs.append(b.w)
        for b in writes:
            if b.w is not None:
                evs.append(b.w)
            evs.extend(b.r)
        seen = self.seen[eng]
        for (key, sem, val) in evs:
            if eng == "pe" and key.endswith("_pe"):
                continue
            if seen.get(key, 0) < val:
                self.q[eng].append(("wait", sem, val))
                seen[key] = val

    def op(self, eng, fn, reads=(), writes=()):
        self._waits(eng, reads, writes)
        key, sem = self.esem[eng]
        self.cnt[eng] += 1
        ev = (key, sem, self.cnt[eng])
        self.q[eng].append(("op", fn, sem))
        for b in writes:
            b.w = ev
            b.r = []
        for b in reads:
            b.r.append(ev)

    def dma(self, eng, fn, sb, reads=(), writes=()):
        self._waits(eng, reads, writes)
        if sb.ds is None:
            sb.ds = self.dnext % NDSEM
            self.dnext += 1
            self.dstage += 1
            assert self.dstage <= NDSEM, "out of dma semaphores"
        i = sb.ds
        self.dcnt[i] += 16
        ev = (f"d{i}", self.dsems[i], self.dcnt[i])
        self.q[eng].append(("dma", fn, self.dsems[i]))
        self.dissued[eng][i] = self.dcnt[i]
        for b in writes:
            b.w = ev
            b.r = []
        for b in reads:
            b.r.append(ev)

    def barrier(self, new_epoch=False):
        self.nbar += 1
        for e in ENGS:
            key, sem = self.esem[e]
            if self.cnt[e] > 0 and self.seen[e].get(key, 0) < self.cnt[e]:
                self.q[e].append(("wait", sem, self.cnt[e]))
                self.seen[e][key] = self.cnt[e]
            for i, val in self.dissued[e].items():
                if self.seen[e].get(f"d{i}", 0) < val:
                    self.q[e].append(("wait", self.dsems[i], val))
                    self.seen[e][f"d{i}"] = val
            self.dissued[e] = {}
        for e in ENGS:
            self.q[e].append(("inc", self.bar))
        for e in ENGS:
            self.q[e].append(("wait", self.bar, 5 * self.nbar))
        for b in self.bufs:
            b.w = None
            b.r = []
            b.ds = None
        self.dstage = 0
        if new_epoch:
            self.new_epoch()

    def replay(self, eng_name, eng):
        for item in self.q[eng_name]:
            if item[0] == "wait":
                eng.wait_ge(item[1], item[2])
            elif item[0] == "op":
                ins = item[1](eng)
                ins.then_inc(item[2], 1)
            elif item[0] == "dma":
                ins = item[1](eng)
                ins.then_inc(item[2], 16)
            elif item[0] == "inc":
                eng.sem_inc(item[1], 1)


class Arena:
    def __init__(self, t, nbytes):
        self.t = t
        self.nbytes = nbytes
        self.off = 0

    def reset(self):
        self.off = 0

    def alloc(self, shape, dtype):
        esz = 4 if dtype == F32 else 2
        n = 1
        for s in shape:
            n *= s
        nb = n * esz
        nb = (nb + 63) // 64 * 64
        assert self.off + nb <= self.nbytes, f"arena overflow {self.off + nb} > {self.nbytes}"
        v = self.t[:, self.off // 2:(self.off + n * esz) // 2]
        self.off += nb
        if dtype == F32:
            v = v.bitcast(F32)
        if len(shape) == 2:
            v = v.rearrange("p (a b) -> p a b", a=shape[0])
        elif len(shape) == 3:
            v = v.rearrange("p (a b c) -> p a b c", a=shape[0], b=shape[1])
        return v


def tok_blocks(lo, hi, maxn=512):
    out = []
    c = lo
    while c < hi:
        n = min(maxn, hi - c)
        out.append((c, n))
        c += n
    return out


def build_program(depth=DEPTH, debug=False):
    nc = bass.Bass("TRN2", target_bir_lowering=False)

    def din(name, shape, dt=F32):
        return nc.dram_tensor(name, list(shape), dt, kind="ExternalInput").ap()

    def dscr(name, shape, dt=F32):
        kind = "ExternalOutput" if (debug and name in ("XA", "XB", "MT", "MODD", "CA")) else "Internal"
        return nc.dram_tensor(name, list(shape), dt, kind=kind).ap()

    xT = din("xT", [D, TW])
    ctxT = din("ctxT", [D, CTX])
    cT = din("cT", [128, KC, 2])
    ada_w = din("ada_w", [DEPTH, D, 3 * D])
    ada_bT = din("ada_bT", [DEPTH, 128, 96])
    pre_gT = din("pre_gT", [DEPTH, 128, KC])
    post_gT = din("post_gT", [DEPTH, 128, KC])
    ab_w_in = din("ab_w_in", [2, D, 11264])
    ab_sinkB = din("ab_sinkB", [2, 128, 16])
    ab_ln_gT = din("ab_ln_gT", [2, 128, 16])
    ab_ln_bT = din("ab_ln_bT", [2, 128, 16])
    ab_wsT = din("ab_wsT", [2, 128, 16, 128])
    ab_wsbB = din("ab_wsbB", [2, 128, 2048])
    ab_w_out = din("ab_w_out", [2, D, D])
    cv_w_in = din("cv_w_in", [2, D, 3 * D])
    cv_dwT = din("cv_dwT", [2, 128, KC, 31])
    cv_dw_bT = din("cv_dw_bT", [2, 128, KC])
    cv_ln_gT = din("cv_ln_gT", [2, 128, KC])
    cv_ln_bT = din("cv_ln_bT", [2, 128, KC])
    cv_w_out = din("cv_w_out", [2, D, D])
    c_ones = din("c_ones", [128, 128], BF16)
    c_ident = din("c_ident", [128, 128], BF16)
    c_identf = din("c_identf", [128, 128])
    c_perm = din("c_perm", [128, 128], BF16)
    c_tri = din("c_tri", [128, 2, 512], BF16)
    c_cos = din("c_cos", [128, TALL])
    c_sin = din("c_sin", [128, TALL])
    c_kbias = din("c_kbias", [128, 18])
    c_tmask = din("c_tmask", [128, TALL])
    out = nc.dram_tensor("out", [D, OWN], F32, kind="ExternalOutput").ap()

    XA = dscr("XA", [D, TW])
    XB = dscr("XB", [D, TW])
    CA = dscr("CA", [D, CTX])
    CB = dscr("CB", [D, CTX])
    QT = dscr("QT", [2048, TALL], BF16)
    KT = dscr("KT", [512, TALL], BF16)
    VT = dscr("VT", [512, TALL], BF16)
    GA = dscr("GA", [2048, TALL], BF16)
    UU = dscr("UU", [2048, TALL], BF16)
    VG = dscr("VG", [2048, TALL], BF16)
    GB = dscr("GB", [2048, TALL], BF16)
    MT = dscr("MT", [D, TALL], BF16)
    GLU = dscr("GLU", [D, TALL], BF16)
    SG = dscr("SG", [D, TALL], BF16)
    YY = dscr("YY", [D, TALL])
    MODD = dscr("MODD", [DEPTH, 2, 3 * D])

    with ExitStack() as es:
        ARENA_BYTES = 170 * 1024
        arena_t = es.enter_context(nc.sbuf_tensor("arena", [128, ARENA_BYTES // 2], BF16))
        AR = Arena(arena_t, ARENA_BYTES)
        NWB = 3
        wbt = [es.enter_context(nc.sbuf_tensor(f"wb{i}", [128, KC, 128], BF16)) for i in range(NWB)]
        ones = es.enter_context(nc.sbuf_tensor("ones", [128, 128], BF16))
        ident = es.enter_context(nc.sbuf_tensor("ident", [128, 128], BF16))
        identf = es.enter_context(nc.sbuf_tensor("identf", [128, 128], F32))
        perm = es.enter_context(nc.sbuf_tensor("perm", [128, 128], BF16))
        tri = es.enter_context(nc.sbuf_tensor("tri", [128, 2, 512], BF16))
        kbias = es.enter_context(nc.sbuf_tensor("kbias", [128, 18], F32))
        scT = es.enter_context(nc.sbuf_tensor("scT", [128, KC, 2], BF16))
        vecs = es.enter_context(nc.sbuf_tensor("vecs", [128, 6, KC], F32))
        rpost = es.enter_context(nc.sbuf_tensor("rpost", [128, TW], F32))
        ps = [es.enter_context(nc.psum_tensor(f"ps{i}", [128, 512], F32)) for i in range(8)]
        S = Sched(nc, es)
        import os as _os
        _lim = int(_os.environ.get("KSTOP", "100000"))
        _cnt = [0]

        class _Stop(Exception):
            pass

        def chk(name):
            _cnt[0] += 1
            if _cnt[0] > _lim:
                raise _Stop()
            if _os.environ.get("KVERB"):
                print("stage", _cnt[0], name, flush=True)

        def load(eng, dst_ap, src_ap, b):
            S.dma(eng, lambda e: e.dma_start(out=dst_ap, in_=src_ap), b, writes=[b])

        def store(eng, dst_ap, src_ap, b):
            S.dma(eng, lambda e: e.dma_start(out=dst_ap, in_=src_ap), b, reads=[b])

        def stage_setup():
            chk("stage_setup")
            AR.reset()
            cfb = AR.alloc([KC, 2], F32)
            bl = [S.buf() for _ in range(8)]
            load("sp", ones[:], c_ones[:, :], bl[0])
            load("sp", ident[:], c_ident[:, :], bl[1])
            load("sp", identf[:], c_identf[:, :], bl[2])
            load("sp", perm[:], c_perm[:, :], bl[3])
            load("sp", tri[:], c_tri[:, :, :], bl[4])
            load("sp", kbias[:], c_kbias[:, :], bl[5])
            load("sp", cfb, cT[:, :, :], bl[6])
            S.op("act", lambda e: e.activation(out=scT[:], in_=cfb, func=AF.Silu), reads=[bl[6]], writes=[bl[7]])
            S.barrier()

        def proj_stage(jobs, actT, tblocks, act_bufs_ready=None):
            wbufs = [S.buf(f"w{i}") for i in range(NWB)]
            wslot = [0]
            psb = [S.buf(f"psb{i}") for i in range(8)]
            ctx_state = {"psrot": 0}
            return wbufs, psb

        def load_w(wbuf_b, wtile, wsrc):
            src = wsrc.rearrange("(kc p) c -> p kc c", p=128)
            S.dma("pool", lambda e: e.dma_start(out=wtile[:], in_=src), wbuf_b, writes=[wbuf_b])

        def ada_jobs(l):
            return [{"kind": "ada", "w": [ada_w[l, :, j * 128:(j + 1) * 128]], "j": j, "l": l} for j in range(96)]

        def stage_modprep(l):
            chk("stage_modprep")
            AR.reset()
            m96 = AR.alloc([2, 128], F32)
            modT = AR.alloc([2, 96], F32)
            abT = AR.alloc([96], F32)
            pg = AR.alloc([KC], F32)
            qg = AR.alloc([KC], F32)
            tmp = AR.alloc([KC], F32)
            b_m96, b_ab, b_pg, b_qg, b_mod, b_tmp, b_vecs = [S.buf() for _ in range(7)]
            bps = S.buf()
            load("sp", m96[0:96], MODD[l].rearrange("r (j c) -> j r c", c=128), b_m96)
            load("sp", abT, ada_bT[l], b_ab)
            load("sp", pg, pre_gT[l], b_pg)
            load("sp", qg, post_gT[l], b_qg)
            for r in range(2):
                S.op("pe", lambda e, r=r: e.transpose(ps[0][:, r * 128:r * 128 + 96], m96[0:96, r, :], identf[0:96, 0:96]),
                     reads=[b_m96], writes=[bps])
            for r in range(2):
                S.op("dve", lambda e, r=r: e.tensor_tensor(out=modT[:, r, :], in0=ps[0][:, r * 128:r * 128 + 96], in1=abT, op=ALU.add),
                     reads=[bps, b_ab], writes=[b_mod])
            for r in range(2):
                S.op("dve", lambda e, r=r: e.tensor_scalar(out=tmp, in0=modT[:, r, 32:64], scalar1=1.0, scalar2=None, op0=ALU.add),
                     reads=[b_mod], writes=[b_tmp])
                S.op("dve", lambda e, r=r: e.tensor_tensor(out=vecs[:, 3 * r + 0, :], in0=tmp, in1=pg, op=ALU.mult),
                     reads=[b_tmp, b_pg], writes=[b_vecs])
                S.op("dve", lambda e, r=r: e.tensor_copy(out=vecs[:, 3 * r + 1, :], in_=modT[:, r, 0:32]),
                     reads=[b_mod], writes=[b_vecs])
                S.op("dve", lambda e, r=r: e.tensor_tensor(out=vecs[:, 3 * r + 2, :], in0=modT[:, r, 64:96], in1=qg, op=ALU.mult),
                     reads=[b_mod, b_qg], writes=[b_vecs])
            S.barrier()

        def stage_prenorm(l, x_in, c_in, actT, segs):
            chk("stage_prenorm")
            NX = 4
            xt = [AR.alloc([512], F32) for _ in range(NX)]
            xb = [S.buf() for _ in range(NX)]
            sq = [AR.alloc([512], BF16) for _ in range(2)]
            sqb = [S.buf() for _ in range(2)]
            tm = [AR.alloc([512], F32) for _ in range(2)]
            tmb = [S.buf() for _ in range(2)]
            rst = [AR.alloc([512], F32) for _ in range(2)]
            rstb = [S.buf() for _ in range(2)]
            accb = [S.buf() for _ in range(2)]
            ab = S.buf()
            xi = 0
            blocks_ = []
            for (is_ctx, sc0, ac0, sn_) in segs:
                for (b0, n) in tok_blocks(0, sn_):
                    blocks_.append((is_ctx, sc0 + b0, ac0 + b0, n))
            for bi, (is_ctx, sc0, ac0, n) in enumerate(blocks_):
                src = c_in if is_ctx else x_in
                vo = 3 if is_ctx else 0
                acc = ps[bi % 2]
                for kc in range(KC):
                    s = xi % NX
                    xi += 1
                    load("sp", xt[s][:, 0:n], src[kc * 128:(kc + 1) * 128, sc0:sc0 + n], xb[s])
                    q = kc % 2
                    S.op("act", lambda e, s=s, q=q, n=n: e.activation(out=sq[q][:, 0:n], in_=xt[s][:, 0:n], func=AF.Square),
                         reads=[xb[s]], writes=[sqb[q]])
                    S.op("pe", lambda e, q=q, n=n, kc=kc, acc=acc: e.matmul(acc[:, 0:n], ones[:], sq[q][:, 0:n], start=(kc == 0), stop=(kc == KC - 1)),
                         reads=[sqb[q]], writes=[accb[bi % 2]])
                r = rst[bi % 2]
                rb = rstb[bi % 2]
                S.op("dve", lambda e, r=r, n=n, acc=acc: e.tensor_scalar(out=r[:, 0:n], in0=acc[:, 0:n], scalar1=1.0 / D, scalar2=RMS_EPS, op0=ALU.mult, op1=ALU.add),
                     reads=[accb[bi % 2]], writes=[rb])
                S.op("act", lambda e, r=r, n=n: e.activation(out=r[:, 0:n], in_=r[:, 0:n], func=AF.Sqrt), reads=[rb], writes=[rb])
                S.op("dve", lambda e, r=r, n=n: e.reciprocal(out=r[:, 0:n], in_=r[:, 0:n]), reads=[rb], writes=[rb])
                for kc in range(KC):
                    s = xi % NX
                    xi += 1
                    load("sp", xt[s][:, 0:n], src[kc * 128:(kc + 1) * 128, sc0:sc0 + n], xb[s])
                    q = kc % 2
                    S.op("dve", lambda e, s=s, q=q, n=n, kc=kc, r=r, vo=vo: e.scalar_tensor_tensor(
                        out=tm[q][:, 0:n], in0=xt[s][:, 0:n], scalar=vecs[:, vo, kc:kc + 1], in1=r[:, 0:n], op0=ALU.mult, op1=ALU.mult),
                        reads=[xb[s], rb], writes=[tmb[q]])
                    S.op("act", lambda e, q=q, n=n, kc=kc, ac0=ac0, vo=vo: e.activation(
                        out=actT[:, kc, ac0:ac0 + n], in_=tm[q][:, 0:n], func=AF.Identity, bias=vecs[:, vo + 1, kc:kc + 1], scale=1.0),
                        reads=[tmb[q]], writes=[])

        class Proj:
            def __init__(self, actT, tblocks, n_of=3):
                self.actT = actT
                self.tb = tblocks
                self.wb = [S.buf() for _ in range(NWB)]
                self.wi = 0
                self.psb = [S.buf() for _ in range(8)]
                self.pr = 0
                self.NOB = 4
                self.ob = [AR.alloc([512], BF16) for _ in range(self.NOB)]
                self.obb = [S.buf() for _ in range(self.NOB)]
                self.oi = 0
                self.n_of = n_of
                self.of = [AR.alloc([512], F32) for _ in range(n_of)]
                self.ofb = [S.buf() for _ in range(n_of)]
                self.ofi = 0
                self.ad = [AR.alloc([128], F32) for _ in range(2)]
                self.adb = [S.buf() for _ in range(2)]
                self.adi = 0
                self.pending = []

            def next_w(self, wsrc):
                i = self.wi % NWB
                self.wi += 1
                load_w(self.wb[i], wbt[i], wsrc)
                return i

            def mm_group(self, wslot, c0, n, pbank):
                actT = self.actT

                def fn(e, wslot=wslot, c0=c0, n=n, pbank=pbank):
                    ins = None
                    for kc in range(KC):
                        ins = e.matmul(ps[pbank][:, 0:n], wbt[wslot][:, kc, :], actT[:, kc, c0:c0 + n],
                                       start=(kc == 0), stop=(kc == KC - 1))
                    return ins
                S.op("pe", fn, reads=[self.wb[wslot]], writes=[self.psb[pbank]])

            def out_bf(self):
                i = self.oi % self.NOB
                self.oi += 1
                return self.ob[i], self.obb[i]

            def out_f32(self):
                i = self.ofi % self.n_of
                self.ofi += 1
                return self.of[i], self.ofb[i]

            def out_ada(self):
                i = self.adi % 2
                self.adi += 1
                return self.ad[i], self.adb[i]

        def ada_job(P, job):
            l, j = job["l"], job["j"]
            wslot = P.next_w(job["w"][0])

            def fn(e):
                ins = None
                for kc in range(KC):
                    ins = e.matmul(ps[7][0:2, 0:128], scT[:, kc, :], wbt[wslot][:, kc, :], start=(kc == 0), stop=(kc == KC - 1))
                return ins
            S.op("pe", fn, reads=[P.wb[wslot]], writes=[P.psb[7]])
            o, ob = P.out_ada()
            S.op("act", lambda e: e.activation(out=o[0:2, 0:128], in_=ps[7][0:2, 0:128], func=AF.Copy), reads=[P.psb[7]], writes=[ob])
            store("sp", MODD[l, :, j * 128:(j + 1) * 128], o[0:2, 0:128], ob)

        def stage_ada_only(l):
            chk("stage_ada_only")
            AR.reset()
            P = Proj(None, [], n_of=0)
            for job in ada_jobs(l):
                ada_job(P, job)
            S.barrier()

        def act_to_global(segs, c0, n):
            res = []
            for (is_ctx, sc0, ac0, sn) in segs:
                lo = max(c0, ac0)
                hi = min(c0 + n, ac0 + sn)
                if lo < hi:
                    g = (TW if is_ctx else 0) + sc0 + (lo - ac0)
                    res.append((lo, g, hi - lo))
            return res

        def stage_inproj_even(l, actT, segs, ncols, next_ada):
            chk("stage_inproj_even")
            i = l // 2
            P = Proj(actT, tok_blocks(0, ncols), n_of=0)
            cs = [AR.alloc([512], F32) for _ in range(2)]
            sn = [AR.alloc([512], F32) for _ in range(2)]
            csb = [S.buf() for _ in range(2)]
            snb = [S.buf() for _ in range(2)]
            qb = [AR.alloc([512], BF16) for _ in range(2)]
            qbb = [S.buf() for _ in range(2)]
            t1 = [AR.alloc([512], F32) for _ in range(2)]
            t1b = [S.buf() for _ in range(2)]
            t2 = [AR.alloc([512], F32) for _ in range(2)]
            t2b = [S.buf() for _ in range(2)]
            ri = [0]
            kinds = [("q", 16, QT), ("k", 4, KT), ("v", 4, VT), ("ga", 16, GA), ("u", 16, UU), ("vg", 16, VG), ("gb", 16, GB)]
            jcol = 0
            adaj = list(next_ada)
            n_ada_tot = len(adaj)
            nlat_ = sum(sn_ for (is_ctx, sc0, ac0, sn_) in segs if not is_ctx)
            tb_out = tok_blocks(128, nlat_ - 128) + (tok_blocks(nlat_, ncols) if CTX_OUT[l] else [])
            _kj = int(_os.environ.get("KJOBS", "1000"))
            _ks = int(_os.environ.get("KSKIP", "0"))
            if _os.environ.get("KNOADA"):
                adaj = []
            for (kind, nblk, dst) in kinds:
                for jb in range(nblk):
                    if jcol >= _kj or jcol < _ks:
                        jcol += 1
                        continue
                    wslot = P.next_w(ab_w_in[i, :, jcol * 128:(jcol + 1) * 128])
                    jcol += 1
                    for (c0, n) in (P.tb if kind in ("k", "v") else tb_out):
                        pbank = P.pr % 4
                        P.pr += 1
                        P.mm_group(wslot, c0, n, pbank)
                        o, ob = P.out_bf()
                        if kind in ("q", "k"):
                            r = ri[0] % 2
                            ri[0] += 1
                            for (ac, g, nn) in ([] if _os.environ.get("KROPE") in ("1", "2") else act_to_global(segs, c0, n)):
                                load("sp", cs[r][:, ac - c0:ac - c0 + nn], c_cos[:, g:g + nn], csb[r])
                                load("sp", sn[r][:, ac - c0:ac - c0 + nn], c_sin[:, g:g + nn], snb[r])
                            S.op("act", lambda e, r=r, n=n, pbank=pbank: e.activation(out=qb[r][:, 0:n], in_=ps[pbank][:, 0:n], func=AF.Copy),
                                 reads=[P.psb[pbank]], writes=[qbb[r]])
                            p2 = 4 + r
                            S.op("pe", lambda e, r=r, n=n, p2=p2: e.matmul(ps[p2][:, 0:n], perm[:], qb[r][:, 0:n], start=True, stop=True),
                                 reads=[qbb[r]], writes=[P.psb[p2]])
                            if _os.environ.get("KROPE") == "2":
                                S.op("act", lambda e, n=n, p2=p2, o=o: e.activation(out=o[:, 0:n], in_=ps[p2][:, 0:n], func=AF.Copy),
                                     reads=[P.psb[p2]], writes=[ob])
                                for (ac, g, nn) in act_to_global(segs, c0, n):
                                    store("sp", dst[jb * 128:(jb + 1) * 128, g:g + nn], o[:, ac - c0:ac - c0 + nn], ob)
                                continue
                            S.op("act", lambda e, r=r, n=n, pbank=pbank: e.activation(out=t1[r][:, 0:n], in_=ps[pbank][:, 0:n], func=AF.Copy),
                                 reads=[P.psb[pbank]], writes=[t1b[r]])
                            S.op("act", lambda e, r=r, n=n, p2=p2: e.activation(out=t2[r][:, 0:n], in_=ps[p2][:, 0:n], func=AF.Copy),
                                 reads=[P.psb[p2]], writes=[t2b[r]])
                            S.op("dve", lambda e, r=r, n=n: e.tensor_tensor(out=t1[r][:, 0:n], in0=t1[r][:, 0:n], in1=cs[r][:, 0:n], op=ALU.mult),
                                 reads=[t1b[r], csb[r]], writes=[t1b[r]])
                            S.op("dve", lambda e, r=r, n=n: e.tensor_tensor(out=t2[r][:, 0:n], in0=t2[r][:, 0:n], in1=sn[r][:, 0:n], op=ALU.mult),
                                 reads=[t2b[r], snb[r]], writes=[t2b[r]])
                            S.op("dve", lambda e, r=r, n=n, o=o: e.tensor_tensor(out=o[:, 0:n], in0=t1[r][:, 0:n], in1=t2[r][:, 0:n], op=ALU.add),
                                 reads=[t1b[r], t2b[r]], writes=[ob])
                        else:
                            func = {"v": AF.Copy, "ga": AF.Silu, "gb": AF.Silu, "u": AF.Gelu, "vg": AF.Gelu}[kind]
                            S.op("act", lambda e, n=n, pbank=pbank, o=o, func=func: e.activation(out=o[:, 0:n], in_=ps[pbank][:, 0:n], func=func),
                                 reads=[P.psb[pbank]], writes=[ob])
                        for (ac, g, nn) in act_to_global(segs, c0, n):
                            store("sp", dst[jb * 128:(jb + 1) * 128, g:g + nn], o[:, ac - c0:ac - c0 + nn], ob)
                    while adaj and (n_ada_tot - len(adaj)) * 88 < jcol * n_ada_tot:
                        ada_job(P, adaj.pop(0))
            while adaj:
                ada_job(P, adaj.pop(0))

        def stage_inproj_odd(l, actT, segs, ncols, next_ada):
            chk("stage_inproj_odd")
            i = l // 2
            P = Proj(actT, tok_blocks(0, ncols), n_of=0)
            tmk = AR.alloc([ncols], F32)
            tmkb = S.buf()
            for (is_ctx, sc0, ac0, sn_) in segs:
                g = (TW if is_ctx else 0) + sc0
                load("sp", tmk[:, ac0:ac0 + sn_], c_tmask[:, g:g + sn_], tmkb)
            sg_ = [AR.alloc([512], F32) for _ in range(2)]
            sgb = [S.buf() for _ in range(2)]
            tt = [AR.alloc([512], F32) for _ in range(2)]
            ttb = [S.buf() for _ in range(2)]
            ri = 0
            adaj = list(next_ada)
            nada_per = (len(adaj) + 31) // 32 if adaj else 0
            for jb in range(KC):
                wa = P.next_w(cv_w_in[i, :, jb * 128:(jb + 1) * 128])
                wb_ = P.next_w(cv_w_in[i, :, D + jb * 128:D + (jb + 1) * 128])
                for (c0, n) in P.tb:
                    pa = (P.pr % 2) * 2
                    pb = pa + 1
                    P.pr += 1
                    P.mm_group(wa, c0, n, pa)
                    P.mm_group(wb_, c0, n, pb)
                    r = ri % 2
                    ri += 1
                    o, ob = P.out_bf()
                    S.op("act", lambda e, r=r, n=n, pb=pb: e.activation(out=sg_[r][:, 0:n], in_=ps[pb][:, 0:n], func=AF.Sigmoid),
                         reads=[P.psb[pb]], writes=[sgb[r]])
                    S.op("act", lambda e, r=r, n=n, pa=pa: e.activation(out=tt[r][:, 0:n], in_=ps[pa][:, 0:n], func=AF.Copy),
                         reads=[P.psb[pa]], writes=[ttb[r]])
                    S.op("dve", lambda e, r=r, n=n: e.tensor_tensor(out=tt[r][:, 0:n], in0=tt[r][:, 0:n], in1=sg_[r][:, 0:n], op=ALU.mult),
                         reads=[ttb[r], sgb[r]], writes=[ttb[r]])
                    S.op("dve", lambda e, r=r, n=n, c0=c0, o=o: e.tensor_tensor(out=o[:, 0:n], in0=tt[r][:, 0:n], in1=tmk[:, c0:c0 + n], op=ALU.mult),
                         reads=[ttb[r], tmkb], writes=[ob])
                    for (ac, g, nn) in act_to_global(segs, c0, n):
                        store("sp", GLU[jb * 128:(jb + 1) * 128, g:g + nn], o[:, ac - c0:ac - c0 + nn], ob)
                wg = P.next_w(cv_w_in[i, :, 2 * D + jb * 128:2 * D + (jb + 1) * 128])
                for (c0, n) in P.tb:
                    pg_ = 4 + (P.pr % 2)
                    P.pr += 1
                    P.mm_group(wg, c0, n, pg_)
                    o, ob = P.out_bf()
                    S.op("act", lambda e, n=n, pg_=pg_, o=o: e.activation(out=o[:, 0:n], in_=ps[pg_][:, 0:n], func=AF.Silu),
                         reads=[P.psb[pg_]], writes=[ob])
                    for (ac, g, nn) in act_to_global(segs, c0, n):
                        store("sp", SG[jb * 128:(jb + 1) * 128, g:g + nn], o[:, ac - c0:ac - c0 + nn], ob)
                for _ in range(nada_per):
                    if adaj:
                        ada_job(P, adaj.pop(0))
            while adaj:
                ada_job(P, adaj.pop(0))

        def stage_outproj(l, w_out, segs, ncols):
            chk("stage_outproj")
            AR.reset()
            actT = AR.alloc([KC, ncols], BF16)
            ldb = [S.buf() for _ in range(8)]
            for qd in range(8):
                for (is_ctx, sc0, ac0, sn_) in segs:
                    g = (TW if is_ctx else 0) + sc0
                    load("sp", actT[:, qd * 4:(qd + 1) * 4, ac0:ac0 + sn_],
                         MT[qd * 512:(qd + 1) * 512, g:g + sn_].rearrange("(c p) t -> p c t", p=128), ldb[qd])
            P = Proj(actT, tok_blocks(0, ncols))
            assert len(P.tb) <= 4
            sq = [AR.alloc([512], BF16) for _ in range(2)]
            sqb = [S.buf() for _ in range(2)]
            accb = [S.buf() for _ in range(4)]
            ri = 0
            first = True
            for jb in range(KC):
                wslot = P.next_w(w_out[:, jb * 128:(jb + 1) * 128])
                for ti, (c0, n) in enumerate(P.tb):
                    pbank = 4 + (P.pr % 3)
                    P.pr += 1
                    actT_ = actT

                    def fn(e, wslot=wslot, c0=c0, n=n, pbank=pbank):
                        ins = None
                        for kc in range(KC):
                            ins = e.matmul(ps[pbank][:, 0:n], wbt[wslot][:, kc, :], actT_[:, kc, c0:c0 + n],
                                           start=(kc == 0), stop=(kc == KC - 1))
                        return ins
                    S.op("pe", fn, reads=[P.wb[wslot]] + (ldb if first else []), writes=[P.psb[pbank]])
                    first = False
                    o, ob = P.out_f32()
                    r = ri % 2
                    ri += 1
                    S.op("act", lambda e, n=n, pbank=pbank, o=o: e.activation(out=o[:, 0:n], in_=ps[pbank][:, 0:n], func=AF.Copy),
                         reads=[P.psb[pbank]], writes=[ob])
                    S.op("act", lambda e, n=n, pbank=pbank, r=r: e.activation(out=sq[r][:, 0:n], in_=ps[pbank][:, 0:n], func=AF.Square),
                         reads=[P.psb[pbank]], writes=[sqb[r]])
                    S.op("pe", lambda e, n=n, r=r, ti=ti, jb=jb: e.matmul(ps[ti][:, 0:n], ones[:], sq[r][:, 0:n], start=(jb == 0), stop=(jb == KC - 1)),
                         reads=[sqb[r]], writes=[accb[ti]])
                    store("sp", YY[jb * 128:(jb + 1) * 128, c0:c0 + n], o[:, 0:n], ob)
            rpb = S.buf()
            for ti, (c0, n) in enumerate(P.tb):
                S.op("dve", lambda e, ti=ti, c0=c0, n=n: e.tensor_scalar(out=rpost[:, c0:c0 + n], in0=ps[ti][:, 0:n], scalar1=1.0 / D, scalar2=RMS_EPS, op0=ALU.mult, op1=ALU.add),
                     reads=[accb[ti]], writes=[rpb])
            S.op("act", lambda e: e.activation(out=rpost[:, 0:ncols], in_=rpost[:, 0:ncols], func=AF.Sqrt), reads=[rpb], writes=[rpb])
            S.op("dve", lambda e: e.reciprocal(out=rpost[:, 0:ncols], in_=rpost[:, 0:ncols]), reads=[rpb], writes=[rpb])
            S.barrier()

        def stage_postnorm(l, segs, x_in, c_in, x_out, c_out, final):
            chk("stage_postnorm")
            AR.reset()
            NB_ = 3
            yt = [AR.alloc([512], F32) for _ in range(NB_)]
            ytb = [S.buf() for _ in range(NB_)]
            xt = [AR.alloc([512], F32) for _ in range(NB_)]
            xtb = [S.buf() for _ in range(NB_)]
            ot = [AR.alloc([512], F32) for _ in range(NB_)]
            otb = [S.buf() for _ in range(NB_)]
            it = 0
            for (is_ctx, sc0, ac0, sn_) in segs:
                for (b0, n) in tok_blocks(0, sn_):
                    for kc in range(KC):
                        s = it % NB_
                        it += 1
                        load("sp", yt[s][:, 0:n], YY[kc * 128:(kc + 1) * 128, ac0 + b0:ac0 + b0 + n], ytb[s])
                        xsrc = c_in if is_ctx else x_in
                        load("sp", xt[s][:, 0:n], xsrc[kc * 128:(kc + 1) * 128, sc0 + b0:sc0 + b0 + n], xtb[s])
                        vo = 5 if is_ctx else 2
                        S.op("dve", lambda e, s=s, n=n, kc=kc, vo=vo, a0=ac0 + b0: e.scalar_tensor_tensor(
                            out=yt[s][:, 0:n], in0=yt[s][:, 0:n], scalar=vecs[:, vo, kc:kc + 1], in1=rpost[:, a0:a0 + n], op0=ALU.mult, op1=ALU.mult),
                            reads=[ytb[s]], writes=[ytb[s]])
                        S.op("pool", lambda e, s=s, n=n: e.tensor_tensor(out=ot[s][:, 0:n], in0=yt[s][:, 0:n], in1=xt[s][:, 0:n], op=ALU.add),
                             reads=[ytb[s], xtb[s]], writes=[otb[s]])
                        if is_ctx:
                            dst = c_out[kc * 128:(kc + 1) * 128, sc0 + b0:sc0 + b0 + n]
                        elif final:
                            dst = x_out[kc * 128:(kc + 1) * 128, sc0 + b0 - HALO:sc0 + b0 - HALO + n]
                        else:
                            dst = x_out[kc * 128:(kc + 1) * 128, sc0 + b0:sc0 + b0 + n]
                        store("sp", dst, ot[s][:, 0:n], otb[s])
            S.barrier()

        def stage_attn(l):
            chk("stage_attn")
            i = l // 2
            AR.reset()
            ilo, ihi = IN_R[l]
            olo, ohi = OUT_R[l]
            do_ctxq = CTX_OUT[l]
            nk_lat = (ihi - ilo) * 128
            NK = nk_lat + CTX
            nq_lat = (ohi - olo) * 128
            NQ = nq_lat + (CTX if do_ctxq else 0)
            QTh = [AR.alloc([4, NQ], BF16) for _ in range(1)]
            GAh = [AR.alloc([4, NQ], BF16) for _ in range(1)]
            KTh = AR.alloc([NK], BF16)
            VTh = AR.alloc([NK], BF16)
            nkt = NK // 128
            Vtok = AR.alloc([nkt, 128], BF16)
            sinkb = AR.alloc([16], F32)
            se = AR.alloc([16], F32)
            Eb = [AR.alloc([512], BF16) for _ in range(5)]
            dn = AR.alloc([512], F32)
            tO = AR.alloc([512], F32)
            NMO = 2
            mo = [AR.alloc([512], BF16) for _ in range(NMO)]
            bq, bg, bk, bv, bvt, bsk, bse, bdn, btO = [S.buf() for _ in range(9)]
            bE = [S.buf() for _ in range(5)]
            bmo = [S.buf() for _ in range(NMO)]
            pb = [S.buf() for _ in range(8)]
            load("sp", sinkb, ab_sinkB[i], bsk)
            S.op("act", lambda e: e.activation(out=se, in_=sinkb, func=AF.Exp), reads=[bsk], writes=[bse])
            moi = 0
            for hk in range(4):
                load("sp", QTh[0][:, :, 0:nq_lat], QT[hk * 512:(hk + 1) * 512, olo * 128:ohi * 128].rearrange("(h d) t -> d h t", d=128), bq)
                load("sp", GAh[0][:, :, 0:nq_lat], GA[hk * 512:(hk + 1) * 512, olo * 128:ohi * 128].rearrange("(h d) t -> d h t", d=128), bg)
                load("sp", KTh[:, 0:nk_lat], KT[hk * 128:(hk + 1) * 128, ilo * 128:ihi * 128], bk)
                load("sp", VTh[:, 0:nk_lat], VT[hk * 128:(hk + 1) * 128, ilo * 128:ihi * 128], bv)
                if do_ctxq:
                    load("sp", QTh[0][:, :, nq_lat:NQ], QT[hk * 512:(hk + 1) * 512, TW:TALL].rearrange("(h d) t -> d h t", d=128), bq)
                    load("sp", GAh[0][:, :, nq_lat:NQ], GA[hk * 512:(hk + 1) * 512, TW:TALL].rearrange("(h d) t -> d h t", d=128), bg)
                load("sp", KTh[:, nk_lat:NK], KT[hk * 128:(hk + 1) * 128, TW:TALL], bk)
                load("sp", VTh[:, nk_lat:NK], VT[hk * 128:(hk + 1) * 128, TW:TALL], bv)
                for kt in range(nkt):
                    pbank = 6 + (kt % 2)
                    pv = ps[pbank][:].bitcast(BF16)
                    S.op("pe", lambda e, kt=kt, pv=pv: e.transpose(pv[:, 0:128], VTh[:, kt * 128:(kt + 1) * 128], ident[:]),
                         reads=[bv], writes=[pb[pbank]])
                    S.op("dve", lambda e, kt=kt, pv=pv: e.tensor_copy(out=Vtok[:, kt, :], in_=pv[:, 0:128]),
                         reads=[pb[pbank]], writes=[bvt])
                qtiles = [("lat", t) for t in range(olo, ohi)] + ([("ctx", 0), ("ctx", 1)] if do_ctxq else [])
                for (qk, t) in qtiles:
                    if qk == "lat":
                        qc = (t - olo) * 128
                        klist = [(t - 1 - ilo, t - 1, 0), (t - ilo, t, None), (t + 1 - ilo, t + 1, 1)]
                        klist += [(nk_lat // 128, 16, None), (nk_lat // 128 + 1, 17, None)]
                    else:
                        qc = nq_lat + t * 128
                        klist = [(nk_lat // 128, 16, None), (nk_lat // 128 + 1, 17, None)]
                    nkl = len(klist)
                    for ki, (kti, kbi, mk) in enumerate(klist):
                        S.op("pe", lambda e, ki=ki, kti=kti, qc=qc: e.matmul(ps[ki][:, :].rearrange("p (h q) -> p h q", h=4),
                                                                           KTh[:, kti * 128:(kti + 1) * 128], QTh[0][:, :, qc:qc + 128], start=True, stop=True),
                             reads=[bk, bq], writes=[pb[ki]])
                        S.op("act", lambda e, ki=ki, kbi=kbi: e.activation(out=Eb[ki][:], in_=ps[ki][:], func=AF.Exp, bias=kbias[:, kbi:kbi + 1], scale=ATT_SCALE),
                             reads=[pb[ki]], writes=[bE[ki]])
                        if mk is not None:
                            S.op("dve", lambda e, ki=ki, mk=mk: e.tensor_tensor(out=Eb[ki][:], in0=Eb[ki][:], in1=tri[:, mk, :], op=ALU.mult),
                                 reads=[bE[ki]], writes=[bE[ki]])
                    for ki, (kti, kbi, mk) in enumerate(klist):
                        S.op("pe", lambda e, ki=ki, nkl=nkl: e.matmul(ps[5][:], ones[:], Eb[ki][:], start=(ki == 0), stop=(ki == nkl - 1)),
                             reads=[bE[ki]], writes=[pb[5]])
                    for ki, (kti, kbi, mk) in enumerate(klist):
                        S.op("pe", lambda e, ki=ki, kti=kti, nkl=nkl: e.matmul(ps[6][:], Vtok[:, kti, :], Eb[ki][:], start=(ki == 0), stop=(ki == nkl - 1)),
                             reads=[bE[ki], bvt], writes=[pb[6]])
                    for h in range(4):
                        S.op("dve", lambda e, h=h, hk=hk: e.tensor_scalar(out=dn[:, h * 128:(h + 1) * 128], in0=ps[5][:, h * 128:(h + 1) * 128],
                                                                         scalar1=se[:, hk * 4 + h:hk * 4 + h + 1], scalar2=None, op0=ALU.add),
                             reads=[pb[5], bse], writes=[bdn])
                    S.op("dve", lambda e: e.reciprocal(out=dn[:], in_=dn[:]), reads=[bdn], writes=[bdn])
                    S.op("dve", lambda e: e.tensor_tensor(out=tO[:], in0=ps[6][:], in1=dn[:], op=ALU.mult), reads=[pb[6], bdn], writes=[btO])
                    m = moi % NMO
                    moi += 1
                    S.op("dve", lambda e, m=m, qc=qc: e.tensor_tensor(out=mo[m][:].rearrange("p (h q) -> p h q", h=4), in0=tO[:].rearrange("p (h q) -> p h q", h=4),
                                                                     in1=GAh[0][:, :, qc:qc + 128], op=ALU.mult),
                         reads=[btO, bg], writes=[bmo[m]])
                    gcol = (t * 128) if qk == "lat" else (TW + t * 128)
                    store("sp", MT[hk * 512:(hk + 1) * 512, gcol:gcol + 128].rearrange("(h d) t -> d h t", d=128),
                          mo[m][:].rearrange("p (h q) -> p h q", h=4), bmo[m])
                S.barrier()

        def stage_gmlp(l):
            chk("stage_gmlp")
            i = l // 2
            AR.reset()
            olo, ohi = OUT_R[l]
            tiles = [t * 128 for t in range(olo, ohi)] + ([TW, TW + 128] if CTX_OUT[l] else [])
            wsT = AR.alloc([16, 128], BF16)
            wsb = AR.alloc([2048], F32)
            lg = AR.alloc([16], F32)
            lb = AR.alloc([16], F32)
            bws, bwsb, blg, blb = [S.buf() for _ in range(4)]
            wsf = AR.alloc([16, 128], F32)
            bwsf = S.buf()
            load("sp", wsf, ab_wsT[i], bwsf)
            S.op("act", lambda e: e.activation(out=wsT[:], in_=wsf[:], func=AF.Copy), reads=[bwsf], writes=[bws])
            load("sp", wsb, ab_wsbB[i], bwsb)
            load("sp", lg, ab_ln_gT[i], blg)
            load("sp", lb, ab_ln_bT[i], blb)
            NR = 2
            vg = [AR.alloc([16, 128], BF16) for _ in range(NR)]
            uu = [AR.alloc([16, 128], BF16) for _ in range(NR)]
            gb = [AR.alloc([16, 128], BF16) for _ in range(NR)]
            bvg = [S.buf() for _ in range(NR)]
            buu = [S.buf() for _ in range(NR)]
            bgb = [S.buf() for _ in range(NR)]
            sq = AR.alloc([16, 128], BF16)
            bsq = S.buf()
            mean = AR.alloc([128], F32)
            msq = AR.alloc([128], F32)
            rstd = AR.alloc([128], F32)
            bmean, bmsq, brstd = S.buf(), S.buf(), S.buf()
            tn = AR.alloc([16, 128], F32)
            btn = S.buf()
            vn = AR.alloc([16, 128], BF16)
            bvn = S.buf()
            vlnT = AR.alloc([2048], BF16)
            bvl = S.buf()
            t1 = [AR.alloc([512], F32) for _ in range(2)]
            bt1 = [S.buf() for _ in range(2)]
            mo = [AR.alloc([512], BF16) for _ in range(4)]
            bmo = [S.buf() for _ in range(4)]
            pb = [S.buf() for _ in range(8)]
            for ti, gc in enumerate(tiles):
                r = ti % NR
                load("sp", vg[r], VG[:, gc:gc + 128].rearrange("(g c) t -> c g t", c=128), bvg[r])
                load("sp", uu[r], UU[:, gc:gc + 128].rearrange("(g c) t -> c g t", c=128), buu[r])
                load("sp", gb[r], GB[:, gc:gc + 128].rearrange("(g c) t -> c g t", c=128), bgb[r])
                S.op("act", lambda e, r=r: e.activation(out=sq[:], in_=vg[r][:], func=AF.Square), reads=[bvg[r]], writes=[bsq])

                def fsum(e, r=r):
                    ins = None
                    for g in range(16):
                        ins = e.matmul(ps[0][:, 0:128], ones[:], vg[r][:, g, :], start=(g == 0), stop=(g == 15))
                    return ins
                S.op("pe", fsum, reads=[bvg[r]], writes=[pb[0]])

                def fsq(e):
                    ins = None
                    for g in range(16):
                        ins = e.matmul(ps[1][:, 0:128], ones[:], sq[:, g, :], start=(g == 0), stop=(g == 15))
                    return ins
                S.op("pe", fsq, reads=[bsq], writes=[pb[1]])
                S.op("dve", lambda e: e.tensor_scalar(out=mean[:], in0=ps[0][:, 0:128], scalar1=1.0 / 2048, scalar2=None, op0=ALU.mult),
                     reads=[pb[0]], writes=[bmean])
                S.op("dve", lambda e: e.tensor_tensor(out=msq[:], in0=mean[:], in1=mean[:], op=ALU.mult), reads=[bmean], writes=[bmsq])
                S.op("dve", lambda e: e.scalar_tensor_tensor(out=rstd[:], in0=ps[1][:, 0:128], scalar=1.0 / 2048, in1=msq[:], op0=ALU.mult, op1=ALU.subtract),
                     reads=[pb[1], bmsq], writes=[brstd])
                S.op("dve", lambda e: e.tensor_scalar(out=rstd[:], in0=rstd[:], scalar1=LN_EPS, scalar2=None, op0=ALU.add), reads=[brstd], writes=[brstd])
                S.op("act", lambda e: e.activation(out=rstd[:], in_=rstd[:], func=AF.Sqrt), reads=[brstd], writes=[brstd])
                S.op("dve", lambda e: e.reciprocal(out=rstd[:], in_=rstd[:]), reads=[brstd], writes=[brstd])
                for g in range(16):
                    S.op("dve", lambda e, r=r, g=g: e.tensor_tensor(out=tn[:, g, :], in0=vg[r][:, g, :], in1=mean[:], op=ALU.subtract),
                         reads=[bvg[r], bmean], writes=[btn])
                    S.op("dve", lambda e, g=g: e.tensor_tensor(out=tn[:, g, :], in0=tn[:, g, :], in1=rstd[:], op=ALU.mult),
                         reads=[btn, brstd], writes=[btn])
                    S.op("act", lambda e, g=g: e.activation(out=vn[:, g, :], in_=tn[:, g, :], func=AF.Identity, bias=lb[:, g:g + 1], scale=lg[:, g:g + 1]),
                         reads=[btn, blg, blb], writes=[bvn])
                for half in range(2):
                    pbank = 2 + half
                    pv = ps[pbank][:].bitcast(BF16)

                    def ftr(e, half=half, pv=pv):
                        ins = None
                        for g8 in range(8):
                            g = half * 8 + g8
                            ins = e.transpose(pv[:, g8 * 128:(g8 + 1) * 128], vn[:, g, :], ident[:])
                        return ins
                    S.op("pe", ftr, reads=[bvn], writes=[pb[pbank]])
                    S.op("act", lambda e, half=half, pv=pv: e.activation(out=vlnT[:, half * 1024:(half + 1) * 1024], in_=pv[:, 0:1024], func=AF.Copy),
                         reads=[pb[pbank]], writes=[bvl])
                for g4 in range(4):
                    pbank = 4 + g4

                    def fsp(e, g4=g4, pbank=pbank):
                        ins = None
                        for gg in range(4):
                            g = g4 * 4 + gg
                            ins = e.matmul(ps[pbank][:, gg * 128:(gg + 1) * 128], vlnT[:, g * 128:(g + 1) * 128], wsT[:, g, :], start=True, stop=True)
                        return ins
                    S.op("pe", fsp, reads=[bvl, bws], writes=[pb[pbank]])
                    q = g4 % 2
                    S.op("dve", lambda e, g4=g4, pbank=pbank, q=q: e.tensor_tensor(out=t1[q][:], in0=ps[pbank][:], in1=wsb[:, g4 * 512:(g4 + 1) * 512], op=ALU.add),
                         reads=[pb[pbank], bwsb], writes=[bt1[q]])
                    S.op("dve", lambda e, g4=g4, q=q, r=r: e.tensor_tensor(out=t1[q][:].rearrange("p (g t) -> p g t", g=4), in0=t1[q][:].rearrange("p (g t) -> p g t", g=4),
                                                                         in1=uu[r][:, g4 * 4:(g4 + 1) * 4, :], op=ALU.mult),
                         reads=[bt1[q], buu[r]], writes=[bt1[q]])
                    S.op("dve", lambda e, g4=g4, q=q, r=r: e.tensor_tensor(out=mo[g4][:].rearrange("p (g t) -> p g t", g=4), in0=t1[q][:].rearrange("p (g t) -> p g t", g=4),
                                                                         in1=gb[r][:, g4 * 4:(g4 + 1) * 4, :], op=ALU.mult),
                         reads=[bt1[q], bgb[r]], writes=[bmo[g4]])
                    store("sp", MT[2048 + g4 * 512:2048 + (g4 + 1) * 512, gc:gc + 128].rearrange("(g c) t -> c g t", c=128),
                          mo[g4][:].rearrange("p (g t) -> p g t", g=4), bmo[g4])
            S.barrier()

        def stage_conv(l):
            chk("stage_conv")
            i = l // 2
            AR.reset()
            olo, ohi = OUT_R[l]
            dw = AR.alloc([KC, 31], F32)
            dwb = AR.alloc([KC], F32)
            lg = AR.alloc([KC], F32)
            lb = AR.alloc([KC], F32)
            bdw, bdwb, blg, blb = [S.buf() for _ in range(4)]
            load("sp", dw, cv_dwT[i], bdw)
            load("sp", dwb, cv_dw_bT[i], bdwb)
            load("sp", lg, cv_ln_gT[i], blg)
            load("sp", lb, cv_ln_bT[i], blb)
            NB_ = 256
            blocks = [(False, c0, n) for (c0, n) in tok_blocks(olo * 128, ohi * 128, NB_)]
            if CTX_OUT[l]:
                blocks.append((True, 0, CTX))
            ybuf = [AR.alloc([KC, NB_], F32) for _ in range(2)]
            by = [[S.buf() for _ in range(KC)] for _ in range(2)]
            NG = 4
            gin = [AR.alloc([NB_ + 32], BF16) for _ in range(NG)]
            NDG = 3
            dg = [AR.alloc([31, 128], BF16) for _ in range(NDG)]
            bdg = [S.buf() for _ in range(NDG)]
            bgin = [S.buf() for _ in range(NG)]
            pacc = [AR.alloc([NB_], F32) for _ in range(3)]
            bpacc = [S.buf() for _ in range(3)]
            yb = [AR.alloc([NB_], BF16) for _ in range(2)]
            byb = [S.buf() for _ in range(2)]
            sq = [AR.alloc([NB_], BF16) for _ in range(2)]
            bsq = [S.buf() for _ in range(2)]
            mean = [AR.alloc([NB_], F32) for _ in range(2)]
            msq = AR.alloc([NB_], F32)
            rstd = [AR.alloc([NB_], F32) for _ in range(2)]
            bmean = [S.buf() for _ in range(2)]
            bmsq = S.buf()
            brstd = [S.buf() for _ in range(2)]
            sgt = [AR.alloc([NB_], BF16) for _ in range(3)]
            bsg = [S.buf() for _ in range(3)]
            tz = [AR.alloc([NB_], F32) for _ in range(2)]
            btz = [S.buf() for _ in range(2)]
            zz = [AR.alloc([NB_], F32) for _ in range(2)]
            bzz = [S.buf() for _ in range(2)]
            mo = [AR.alloc([NB_], BF16) for _ in range(3)]
            bmo = [S.buf() for _ in range(3)]
            pb = [S.buf() for _ in range(8)]
            gi = 0
            for bi, (is_ctx, c0, n) in enumerate(blocks):
                yy = ybuf[bi % 2]
                byy = by[bi % 2]
                psum_s = ps[(bi % 2) * 2]
                psum_q = ps[(bi % 2) * 2 + 1]
                pbs = pb[(bi % 2) * 2]
                pbq = pb[(bi % 2) * 2 + 1]
                gbase = TW if is_ctx else 0
                pend_stat = None
                for kc in range(KC):
                    s = gi % NG
                    gi += 1
                    if is_ctx:
                        S.op("pool", lambda e, s=s: e.memset(gin[s][:, :], 0.0), writes=[bgin[s]])
                        load("sp", gin[s][:, 15:15 + n], GLU[kc * 128:(kc + 1) * 128, TW:TW + n], bgin[s])
                    else:
                        load("sp", gin[s][:, 0:n + 30], GLU[kc * 128:(kc + 1) * 128, c0 - 15:c0 + n + 15], bgin[s])
                    d2 = kc % NDG
                    def fdg(e, d2=d2, kc=kc):
                        ins = None
                        for j in range(31):
                            ins = e.tensor_scalar(out=dg[d2][:, j, :], in0=ident[:], scalar1=dw[:, kc, j:j + 1], scalar2=None, op0=ALU.mult)
                        return ins
                    S.op("dve", fdg, reads=[bdw], writes=[bdg[d2]])
                    pc = 4 + (kc % 4)

                    def fconv(e, s=s, d2=d2, n=n, pc=pc):
                        ins = None
                        for j in range(31):
                            ins = e.matmul(ps[pc][:, 0:n], dg[d2][:, j, :], gin[s][:, j:j + n], start=(j == 0), stop=(j == 30))
                        return ins
                    S.op("pe", fconv, reads=[bdg[d2], bgin[s]], writes=[pb[pc]])
                    S.op("act", lambda e, kc=kc, n=n, pc=pc, yy=yy: e.activation(out=yy[:, kc, 0:n], in_=ps[pc][:, 0:n], func=AF.Identity, bias=dwb[:, kc:kc + 1], scale=1.0),
                         reads=[pb[pc], bdwb], writes=[byy[kc]])
                    q = kc % 2
                    S.op("act", lambda e, kc=kc, n=n, q=q, pc=pc: e.activation(out=yb[q][:, 0:n], in_=ps[pc][:, 0:n], func=AF.Identity, bias=dwb[:, kc:kc + 1], scale=1.0), reads=[pb[pc], bdwb], writes=[byb[q]])
                    S.op("act", lambda e, kc=kc, n=n, q=q, pc=pc: e.activation(out=sq[q][:, 0:n], in_=ps[pc][:, 0:n], func=AF.Square, bias=dwb[:, kc:kc + 1], scale=1.0), reads=[pb[pc], bdwb], writes=[bsq[q]])
                    def fstat(kc=kc, n=n, q=q, psum_s=psum_s, psum_q=psum_q, pbs=pbs, pbq=pbq):
                        S.op("pe", lambda e: e.matmul(psum_s[:, 0:n], ones[:], yb[q][:, 0:n], start=(kc == 0), stop=(kc == KC - 1)),
                             reads=[byb[q]], writes=[pbs])
                        S.op("pe", lambda e: e.matmul(psum_q[:, 0:n], ones[:], sq[q][:, 0:n], start=(kc == 0), stop=(kc == KC - 1)),
                             reads=[bsq[q]], writes=[pbq])
                    if pend_stat is not None:
                        pend_stat()
                    pend_stat = fstat
                pend_stat()
                pend_stat = None
                mm = mean[bi % 2]
                rr = rstd[bi % 2]
                bm = bmean[bi % 2]
                br = brstd[bi % 2]
                S.op("dve", lambda e, n=n, mm=mm, psum_s=psum_s: e.tensor_scalar(out=mm[:, 0:n], in0=psum_s[:, 0:n], scalar1=1.0 / D, scalar2=None, op0=ALU.mult), reads=[pbs], writes=[bm])
                S.op("dve", lambda e, n=n, mm=mm: e.tensor_tensor(out=msq[:, 0:n], in0=mm[:, 0:n], in1=mm[:, 0:n], op=ALU.mult), reads=[bm], writes=[bmsq])
                S.op("dve", lambda e, n=n, rr=rr, psum_q=psum_q: e.scalar_tensor_tensor(out=rr[:, 0:n], in0=psum_q[:, 0:n], scalar=1.0 / D, in1=msq[:, 0:n], op0=ALU.mult, op1=ALU.subtract),
                     reads=[pbq, bmsq], writes=[br])
                S.op("dve", lambda e, n=n, rr=rr: e.tensor_scalar(out=rr[:, 0:n], in0=rr[:, 0:n], scalar1=LN_EPS, scalar2=None, op0=ALU.add), reads=[br], writes=[br])
                S.op("act", lambda e, n=n, rr=rr: e.activation(out=rr[:, 0:n], in_=rr[:, 0:n], func=AF.Sqrt), reads=[br], writes=[br])
                S.op("dve", lambda e, n=n, rr=rr: e.reciprocal(out=rr[:, 0:n], in_=rr[:, 0:n]), reads=[br], writes=[br])
                for kc in range(KC):
                    q = kc % 2
                    s3 = kc % 3
                    load("sp", sgt[s3][:, 0:n], SG[kc * 128:(kc + 1) * 128, gbase + c0:gbase + c0 + n], bsg[s3])
                    S.op("pool", lambda e, kc=kc, n=n, q=q, yy=yy, mm=mm: e.tensor_tensor(out=tz[q][:, 0:n], in0=yy[:, kc, 0:n], in1=mm[:, 0:n], op=ALU.subtract),
                         reads=[byy[kc], bm], writes=[btz[q]])
                    S.op("pool", lambda e, n=n, q=q, rr=rr: e.tensor_tensor(out=tz[q][:, 0:n], in0=tz[q][:, 0:n], in1=rr[:, 0:n], op=ALU.mult),
                         reads=[btz[q], br], writes=[btz[q]])
                    S.op("act", lambda e, kc=kc, n=n, q=q: e.activation(out=zz[q][:, 0:n], in_=tz[q][:, 0:n], func=AF.Silu, bias=lb[:, kc:kc + 1], scale=lg[:, kc:kc + 1]),
                         reads=[btz[q], blg, blb], writes=[bzz[q]])
                    S.op("pool", lambda e, n=n, q=q, s3=s3: e.tensor_tensor(out=mo[s3][:, 0:n], in0=zz[q][:, 0:n], in1=sgt[s3][:, 0:n], op=ALU.mult),
                         reads=[bzz[q], bsg[s3]], writes=[bmo[s3]])
                    store("sp", MT[kc * 128:(kc + 1) * 128, gbase + c0:gbase + c0 + n], mo[s3][:, 0:n], bmo[s3])
            S.barrier()

        def segs_for(lo, hi, with_ctx):
            segs = [(False, lo * 128, 0, (hi - lo) * 128)]
            ncols = (hi - lo) * 128
            if with_ctx:
                segs.append((True, 0, ncols, CTX))
                ncols += CTX
            return segs, ncols

        def _drive():
          stage_setup()
          stage_ada_only(0)
          x_bufs = [xT, XA, XB, XA, None]
          c_bufs = [ctxT, CA, CB, None, None]
          for l in range(depth):
            even = (l % 2 == 0)
            final = (l == DEPTH - 1)
            stage_modprep(l)
            x_in = x_bufs[l]
            x_out = out if final else x_bufs[l + 1]
            c_in = c_bufs[l]
            c_out = c_bufs[l + 1]
            ilo, ihi = IN_R[l]
            olo, ohi = OUT_R[l]
            segs_in, ncols_in = segs_for(ilo, ihi, CTX_IN[l])
            AR.reset()
            actT = AR.alloc([KC, ncols_in], BF16)
            mark = AR.off
            stage_prenorm(l, x_in, c_in, actT, segs_in)
            S.barrier()
            AR.off = mark
            nada = ada_jobs(l + 1) if l + 1 < depth else []
            if even:
                stage_inproj_even(l, actT, segs_in, ncols_in, nada)
                S.barrier()
                stage_attn(l)
                stage_gmlp(l)
                w_out = ab_w_out[l // 2]
            else:
                stage_inproj_odd(l, actT, segs_in, ncols_in, nada)
                S.barrier()
                stage_conv(l)
                w_out = cv_w_out[l // 2]
            segs_out, ncols_out = segs_for(olo, ohi, CTX_OUT[l])
            stage_outproj(l, w_out, segs_out, ncols_out)
            stage_postnorm(l, segs_out, x_in, c_in, x_out, c_out, final)
            S.barrier(new_epoch=True)

        try:
            _drive()
        except _Stop:
            S.barrier()
        block = es.enter_context(nc.Block())

        @block.tensor
        def _(e):
            S.replay("pe", e)

        @block.scalar
        def _(e):
            S.replay("act", e)

        @block.vector
        def _(e):
            S.replay("dve", e)

        @block.gpsimd
        def _(e):
            S.replay("pool", e)

        @block.sync
        def _(e):
            S.replay("sp", e)

    return nc


def _fm(v, k=KC):
    sh = v.shape[:-1]
    return np.ascontiguousarray(np.swapaxes(v.reshape(sh + (k, 128)), -1, -2))


def _rope_tables(core):
    pos = core * OWN - HALO + np.arange(TW)
    row = (pos // 64).astype(np.float32)
    col = (pos % 64).astype(np.float32)
    inv = (1.0 / (10000.0 ** (np.arange(32, dtype=np.float32) / 32))).astype(np.float32)
    cos = np.ones((128, TALL), np.float32)
    sin = np.zeros((128, TALL), np.float32)
    for d in range(128):
        p = row if d < 64 else col
        ang = (p * inv[d % 32]).astype(np.float32)
        cos[d, :TW] = np.cos(ang)
        s = np.sin(ang)
        sin[d, :TW] = -s if (d % 64) < 32 else s
    return cos, sin


def _prep_inputs(inputs):
    f = lambda a: np.ascontiguousarray(np.asarray(a, dtype=np.float32))
    x = f(inputs["x"])[0]
    ctx = f(inputs["ctx"])[0]
    c = f(inputs["c"])[0]
    c_ctx = f(inputs["c_ctx"])
    shared = {}
    shared["ctxT"] = np.ascontiguousarray(ctx.T)
    shared["cT"] = np.ascontiguousarray(np.stack([_fm(c), _fm(c_ctx)], axis=-1))
    shared["ada_w"] = f(inputs["ada_w"])
    shared["ada_bT"] = _fm(f(inputs["ada_b"]), 96)
    shared["pre_gT"] = _fm(f(inputs["pre_g"]))
    shared["post_gT"] = _fm(f(inputs["post_g"]))
    shared["ab_w_in"] = f(inputs["ab_w_in"])
    shared["ab_sinkB"] = np.ascontiguousarray(np.broadcast_to(f(inputs["ab_sink"])[:, None, :], (2, 128, 16)))
    shared["ab_ln_gT"] = _fm(f(inputs["ab_ln_g"]), 16)
    shared["ab_ln_bT"] = _fm(f(inputs["ab_ln_b"]), 16)
    ws = f(inputs["ab_ws"])
    shared["ab_wsT"] = np.ascontiguousarray(ws.transpose(0, 3, 1, 2))
    wsb = f(inputs["ab_ws_b"]).reshape(2, 1, 2048)
    shared["ab_wsbB"] = np.ascontiguousarray(np.broadcast_to(wsb, (2, 128, 2048)))
    shared["ab_w_out"] = f(inputs["ab_w_out"])
    shared["cv_w_in"] = f(inputs["cv_w_in"])
    dw = f(inputs["cv_dw"])
    shared["cv_dwT"] = np.ascontiguousarray(dw.reshape(2, 31, KC, 128).transpose(0, 3, 2, 1))
    shared["cv_dw_bT"] = _fm(f(inputs["cv_dw_b"]))
    shared["cv_ln_gT"] = _fm(f(inputs["cv_ln_g"]))
    shared["cv_ln_bT"] = _fm(f(inputs["cv_ln_b"]))
    shared["cv_w_out"] = f(inputs["cv_w_out"])
    bf = ml_dtypes.bfloat16
    shared["c_ones"] = np.ones((128, 128), bf)
    shared["c_ident"] = np.eye(128, dtype=np.float32).astype(bf)
    shared["c_identf"] = np.eye(128, dtype=np.float32)
    pm = np.zeros((128, 128), np.float32)
    for d in range(128):
        pm[d + 32 if (d % 64) < 32 else d - 32, d] = 1.0
    shared["c_perm"] = pm.astype(bf)
    kj = np.arange(128)[:, None]
    qi = np.arange(128)[None, :]
    tri = np.stack([np.tile((kj >= qi).astype(np.float32), (1, 4)), np.tile((kj <= qi).astype(np.float32), (1, 4))], axis=1)
    shared["c_tri"] = tri.astype(bf)
    in_maps = []
    for core in range(NCORES):
        m = dict(shared)
        xw = np.zeros((TW, D), np.float32)
        p0 = core * OWN - HALO
        lo = max(p0, 0)
        hi = min(p0 + TW, SEQ)
        xw[lo - p0:hi - p0] = x[lo:hi]
        m["xT"] = np.ascontiguousarray(xw.T)
        cos, sin = _rope_tables(core)
        m["c_cos"] = cos
        m["c_sin"] = sin
        pos = p0 + np.arange(TW)
        valid = ((pos >= 0) & (pos < SEQ))
        kb = np.zeros((128, 18), np.float32)
        kb[:, :16] = np.where(valid.reshape(16, 128).T, 0.0, NEG)
        m["c_kbias"] = kb
        tm = np.ones((128, TALL), np.float32)
        tm[:, :TW] = valid[None, :].astype(np.float32)
        m["c_tmask"] = tm
        in_maps.append(m)
    return in_maps


_NC_CACHE = {}


def kernel(**inputs):
    in_maps = _prep_inputs(inputs)
    if "nc" not in _NC_CACHE:
        _NC_CACHE["nc"] = build_program()
    nc = _NC_CACHE["nc"]
    res = run_bass_kernel_spmd(nc, in_maps, core_ids=list(range(NCORES)))
    outs = [np.asarray(r["out"]) for r in res.results]
    full = np.concatenate([o.T for o in outs], axis=0)
    return np.ascontiguousarray(full[None].astype(np.float32))
```

```python
import numpy as np
import ml_dtypes
from contextlib import ExitStack
import concourse.bass as bass
import concourse.mybir as mybir
from concourse.bass_utils import run_bass_kernel_spmd

F32 = mybir.dt.float32
BF16 = mybir.dt.bfloat16
ALU = mybir.AluOpType
AF = mybir.ActivationFunctionType

NCORES = 8
D = 4096
KC = 32
SEQ = 8192
OWN = 1024
HALO = 512
TW = 2048
CTX = 256
TALL = TW + CTX
DEPTH = 4
RMS_EPS = 1e-6
LN_EPS = 1e-5
NEG = -30000.0
ATT_SCALE = 128 ** -0.5

IN_R = [(0, 16), (1, 15), (2, 14), (3, 13)]
OUT_R = [(1, 15), (2, 14), (3, 13), (4, 12)]
CTX_IN = [True, True, True, False]
CTX_OUT = [True, True, False, False]

ENGS = ["pe", "act", "dve", "pool", "sp"]
NDSEM = 64


class Buf:
    __slots__ = ("name", "w", "r", "ds")

    def __init__(self, name=""):
        self.name = name
        self.w = None
        self.r = []
        self.ds = None


class Sched:
    def __init__(self, nc, es):
        self.nc = nc
        self.q = {e: [] for e in ENGS}
        self.epoch_sems = []
        self.es = es
        self.cnt = {e: 0 for e in ENGS}
        self.esem = {}
        self.seen = {e: {} for e in ENGS}
        self.dsems = [es.enter_context(nc.semaphore(f"dq{i}")) for i in range(NDSEM)]
        self.dcnt = [0] * NDSEM
        self.dnext = 0
        self.dstage = 0
        self.dissued = {e: {} for e in ENGS}
        self.bar = es.enter_context(nc.semaphore("bar"))
        self.nbar = 0
        self.nep = 0
        self.bufs = []
        self.new_epoch()

    def new_epoch(self):
        for e in ENGS:
            self.esem[e] = (f"e{self.nep}_{e}", self.es.enter_context(self.nc.semaphore(f"s{self.nep}_{e}")))
            self.cnt[e] = 0
        self.nep += 1

    def buf(self, name=""):
        b = Buf(name)
        self.bufs.append(b)
        return b

    def _waits(self, eng, reads, writes):
        evs = []
        for b in reads:
            if b.w is not None:
                evs.append(b.w)
        for b in writes:
            if b.w is not None:
                evs.append(b.w)
            evs.extend(b.r)
        seen = self.seen[eng]
        for (key, sem, val) in evs:
            if eng == "pe" and key.endswith("_pe"):
                continue
            if seen.get(key, 0) < val:
                self.q[eng].append(("wait", sem, val))
                seen[key] = val

    def op(self, eng, fn, reads=(), writes=()):
        self._waits(eng, reads, writes)
        key, sem = self.esem[eng]
        self.cnt[eng] += 1
        ev = (key, sem, self.cnt[eng])
        self.q[eng].append(("op", fn, sem))
        for b in writes:
            b.w = ev
            b.r = []
        for b in reads:
            b.r.append(ev)

    def dma(self, eng, fn, sb, reads=(), writes=()):
        self._waits(eng, reads, writes)
        if sb.ds is None:
            sb.ds = self.dnext % NDSEM
            self.dnext += 1
            self.dstage += 1
            assert self.dstage <= NDSEM, "out of dma semaphores"
        i = sb.ds
        self.dcnt[i] += 16
        ev = (f"d{i}", self.dsems[i], self.dcnt[i])
        self.q[eng].append(("dma", fn, self.dsems[i]))
        self.dissued[eng][i] = self.dcnt[i]
        for b in writes:
            b.w = ev
            b.r = []
        for b in reads:
            b.r.append(ev)

    def barrier(self, new_epoch=False):
        self.nbar += 1
        for e in ENGS:
            key, sem = self.esem[e]
            if self.cnt[e] > 0 and self.seen[e].get(key, 0) < self.cnt[e]:
                self.q[e].append(("wait", sem, self.cnt[e]))
                self.seen[e][key] = self.cnt[e]
            for i, val in self.dissued[e].items():
                if self.seen[e].get(f"d{i}", 0) < val:
                    self.q[e].append(("wait", self.dsems[i], val))
                    self.seen[e][f"d{i}"] = val
            self.dissued[e] = {}
        for e in ENGS:
            self.q[e].append(("inc", self.bar))
        for e in ENGS:
            self.q[e].append(("wait", self.bar, 5 * self.nbar))
        for b in self.bufs:
            b.w = None
            b.r = []
            b.ds = None
        self.dstage = 0
        if new_epoch:
            self.new_epoch()

    def replay(self, eng_name, eng):
        for item in self.q[eng_name]:
            if item[0] == "wait":
                eng.wait_ge(item[1], item[2])
            elif item[0] == "op":
                ins = item[1](eng)
                ins.then_inc(item[2], 1)
            elif item[0] == "dma":
                ins = item[1](eng)
                ins.then_inc(item[2], 16)
            elif item[0] == "inc":
                eng.sem_inc(item[1], 1)


class Arena:
    def __init__(self, t, nbytes):
        self.t = t
        self.nbytes = nbytes
        self.off = 0

    def reset(self):
        self.off = 0

    def alloc(self, shape, dtype):
        esz = 4 if dtype == F32 else 2
        n = 1
        for s in shape:
            n *= s
        nb = n * esz
        nb = (nb + 63) // 64 * 64
        assert self.off + nb <= self.nbytes, f"arena overflow {self.off + nb} > {self.nbytes}"
        v = self.t[:, self.off // 2:(self.off + n * esz) // 2]
        self.off += nb
        if dtype == F32:
            v = v.bitcast(F32)
        if len(shape) == 2:
            v = v.rearrange("p (a b) -> p a b", a=shape[0])
        elif len(shape) == 3:
            v = v.rearrange("p (a b c) -> p a b c", a=shape[0], b=shape[1])
        return v


def tok_blocks(lo, hi, maxn=512):
    out = []
    c = lo
    while c < hi:
        n = min(maxn, hi - c)
        out.append((c, n))
        c += n
    return out


def build_program(depth=DEPTH, debug=False):
    nc = bass.Bass("TRN2", target_bir_lowering=False)

    def din(name, shape, dt=F32):
        return nc.dram_tensor(name, list(shape), dt, kind="ExternalInput").ap()

    def dscr(name, shape, dt=F32):
        kind = "ExternalOutput" if (debug and name in ("XA", "XB", "MT", "MODD", "CA")) else "Internal"
        return nc.dram_tensor(name, list(shape), dt, kind=kind).ap()

    xT = din("xT", [D, TW])
    ctxT = din("ctxT", [D, CTX])
    cT = din("cT", [128, KC, 2])
    ada_w = din("ada_w", [DEPTH, D, 3 * D])
    ada_bT = din("ada_bT", [DEPTH, 128, 96])
    pre_gT = din("pre_gT", [DEPTH, 128, KC])
    post_gT = din("post_gT", [DEPTH, 128, KC])
    ab_w_in = din("ab_w_in", [2, D, 11264])
    ab_sinkB = din("ab_sinkB", [2, 128, 16])
    ab_ln_gT = din("ab_ln_gT", [2, 128, 16])
    ab_ln_bT = din("ab_ln_bT", [2, 128, 16])
    ab_wsT = din("ab_wsT", [2, 128, 16, 128])
    ab_wsbB = din("ab_wsbB", [2, 128, 2048])
    ab_w_out = din("ab_w_out", [2, D, D])
    cv_w_in = din("cv_w_in", [2, D, 3 * D])
    cv_dwT = din("cv_dwT", [2, 128, KC, 31])
    cv_dw_bT = din("cv_dw_bT", [2, 128, KC])
    cv_ln_gT = din("cv_ln_gT", [2, 128, KC])
    cv_ln_bT = din("cv_ln_bT", [2, 128, KC])
    cv_w_out = din("cv_w_out", [2, D, D])
    c_ones = din("c_ones", [128, 128], BF16)
    c_ident = din("c_ident", [128, 128], BF16)
    c_identf = din("c_identf", [128, 128])
    c_perm = din("c_perm", [128, 128], BF16)
    c_tri = din("c_tri", [128, 2, 512], BF16)
    c_cos = din("c_cos", [128, TALL])
    c_sin = din("c_sin", [128, TALL])
    c_kbias = din("c_kbias", [128, 18])
    c_tmask = din("c_tmask", [128, TALL])
    out = nc.dram_tensor("out", [D, OWN], F32, kind="ExternalOutput").ap()

    XA = dscr("XA", [D, TW])
    XB = dscr("XB", [D, TW])
    CA = dscr("CA", [D, CTX])
    CB = dscr("CB", [D, CTX])
    QT = dscr("QT", [2048, TALL], BF16)
    KT = dscr("KT", [512, TALL], BF16)
    VT = dscr("VT", [512, TALL], BF16)
    GA = dscr("GA", [2048, TALL], BF16)
    UU = dscr("UU", [2048, TALL], BF16)
    VG = dscr("VG", [2048, TALL], BF16)
    GB = dscr("GB", [2048, TALL], BF16)
    MT = dscr("MT", [D, TALL], BF16)
    GLU = dscr("GLU", [D, TALL], BF16)
    SG = dscr("SG", [D, TALL], BF16)
    YY = dscr("YY", [D, TALL])
    MODD = dscr("MODD", [DEPTH, 2, 3 * D])

    with ExitStack() as es:
        ARENA_BYTES = 170 * 1024
        arena_t = es.enter_context(nc.sbuf_tensor("arena", [128, ARENA_BYTES // 2], BF16))
        AR = Arena(arena_t, ARENA_BYTES)
        NWB = 3
        wbt = [es.enter_context(nc.sbuf_tensor(f"wb{i}", [128, KC, 128], BF16)) for i in range(NWB)]
        ones = es.enter_context(nc.sbuf_tensor("ones", [128, 128], BF16))
        ident = es.enter_context(nc.sbuf_tensor("ident", [128, 128], BF16))
        identf = es.enter_context(nc.sbuf_tensor("identf", [128, 128], F32))
        perm = es.enter_context(nc.sbuf_tensor("perm", [128, 128], BF16))
        tri = es.enter_context(nc.sbuf_tensor("tri", [128, 2, 512], BF16))
        kbias = es.enter_context(nc.sbuf_tensor("kbias", [128, 18], F32))
        scT = es.enter_context(nc.sbuf_tensor("scT", [128, KC, 2], BF16))
        vecs = es.enter_context(nc.sbuf_tensor("vecs", [128, 6, KC], F32))
        rpost = es.enter_context(nc.sbuf_tensor("rpost", [128, TW], F32))
        ps = [es.enter_context(nc.psum_tensor(f"ps{i}", [128, 512], F32)) for i in range(8)]
        S = Sched(nc, es)
        import os as _os
        _lim = int(_os.environ.get("KSTOP", "100000"))
        _cnt = [0]

        class _Stop(Exception):
            pass

        def chk(name):
            _cnt[0] += 1
            if _cnt[0] > _lim:
                raise _Stop()
            if _os.environ.get("KVERB"):
                print("stage", _cnt[0], name, flush=True)

        def load(eng, dst_ap, src_ap, b):
            S.dma(eng, lambda e: e.dma_start(out=dst_ap, in_=src_ap), b, writes=[b])

        def store(eng, dst_ap, src_ap, b):
            S.dma(eng, lambda e: e.dma_start(out=dst_ap, in_=src_ap), b, reads=[b])

        def stage_setup():
            chk("stage_setup")
            AR.reset()
            cfb = AR.alloc([KC, 2], F32)
            bl = [S.buf() for _ in range(8)]
            load("sp", ones[:], c_ones[:, :], bl[0])
            load("sp", ident[:], c_ident[:, :], bl[1])
            load("sp", identf[:], c_identf[:, :], bl[2])
            load("sp", perm[:], c_perm[:, :], bl[3])
            load("sp", tri[:], c_tri[:, :, :], bl[4])
            load("sp", kbias[:], c_kbias[:, :], bl[5])
            load("sp", cfb, cT[:, :, :], bl[6])
            S.op("act", lambda e: e.activation(out=scT[:], in_=cfb, func=AF.Silu), reads=[bl[6]], writes=[bl[7]])
            S.barrier()

        def proj_stage(jobs, actT, tblocks, act_bufs_ready=None):
            wbufs = [S.buf(f"w{i}") for i in range(NWB)]
            wslot = [0]
            psb = [S.buf(f"psb{i}") for i in range(8)]
            ctx_state = {"psrot": 0}
            return wbufs, psb

        def load_w(wbuf_b, wtile, wsrc):
            src = wsrc.rearrange("(kc p) c -> p kc c", p=128)
            S.dma("pool", lambda e: e.dma_start(out=wtile[:], in_=src), wbuf_b, writes=[wbuf_b])

        def ada_jobs(l):
            return [{"kind": "ada", "w": [ada_w[l, :, j * 128:(j + 1) * 128]], "j": j, "l": l} for j in range(96)]

        def stage_modprep(l):
            chk("stage_modprep")
            AR.reset()
            m96 = AR.alloc([2, 128], F32)
            modT = AR.alloc([2, 96], F32)
            abT = AR.alloc([96], F32)
            pg = AR.alloc([KC], F32)
            qg = AR.alloc([KC], F32)
            tmp = AR.alloc([KC], F32)
            b_m96, b_ab, b_pg, b_qg, b_mod, b_tmp, b_vecs = [S.buf() for _ in range(7)]
            bps = S.buf()
            load("sp", m96[0:96], MODD[l].rearrange("r (j c) -> j r c", c=128), b_m96)
            load("sp", abT, ada_bT[l], b_ab)
            load("sp", pg, pre_gT[l], b_pg)
            load("sp", qg, post_gT[l], b_qg)
            for r in range(2):
                S.op("pe", lambda e, r=r: e.transpose(ps[0][:, r * 128:r * 128 + 96], m96[0:96, r, :], identf[0:96, 0:96]),
                     reads=[b_m96], writes=[bps])
            for r in range(2):
                S.op("dve", lambda e, r=r: e.tensor_tensor(out=modT[:, r, :], in0=ps[0][:, r * 128:r * 128 + 96], in1=abT, op=ALU.add),
                     reads=[bps, b_ab], writes=[b_mod])
            for r in range(2):
                S.op("dve", lambda e, r=r: e.tensor_scalar(out=tmp, in0=modT[:, r, 32:64], scalar1=1.0, scalar2=None, op0=ALU.add),
                     reads=[b_mod], writes=[b_tmp])
                S.op("dve", lambda e, r=r: e.tensor_tensor(out=vecs[:, 3 * r + 0, :], in0=tmp, in1=pg, op=ALU.mult),
                     reads=[b_tmp, b_pg], writes=[b_vecs])
                S.op("dve", lambda e, r=r: e.tensor_copy(out=vecs[:, 3 * r + 1, :], in_=modT[:, r, 0:32]),
                     reads=[b_mod], writes=[b_vecs])
                S.op("dve", lambda e, r=r: e.tensor_tensor(out=vecs[:, 3 * r + 2, :], in0=modT[:, r, 64:96], in1=qg, op=ALU.mult),
                     reads=[b_mod, b_qg], writes=[b_vecs])
            S.barrier()

        def stage_prenorm(l, x_in, c_in, actT, segs):
            chk("stage_prenorm")
            NX = 4
            xt = [AR.alloc([512], F32) for _ in range(NX)]
            xb = [S.buf() for _ in range(NX)]
            sq = [AR.alloc([512], BF16) for _ in range(2)]
            sqb = [S.buf() for _ in range(2)]
            tm = [AR.alloc([512], F32) for _ in range(2)]
            tmb = [S.buf() for _ in range(2)]
            rst = [AR.alloc([512], F32) for _ in range(2)]
            rstb = [S.buf() for _ in range(2)]
            accb = [S.buf() for _ in range(2)]
            ab = S.buf()
            xi = 0
            blocks_ = []
            for (is_ctx, sc0, ac0, sn_) in segs:
                for (b0, n) in tok_blocks(0, sn_):
                    blocks_.append((is_ctx, sc0 + b0, ac0 + b0, n))
            for bi, (is_ctx, sc0, ac0, n) in enumerate(blocks_):
                src = c_in if is_ctx else x_in
                vo = 3 if is_ctx else 0
                acc = ps[bi % 2]
                for kc in range(KC):
                    s = xi % NX
                    xi += 1
                    load("sp", xt[s][:, 0:n], src[kc * 128:(kc + 1) * 128, sc0:sc0 + n], xb[s])
                    q = kc % 2
                    S.op("act", lambda e, s=s, q=q, n=n: e.activation(out=sq[q][:, 0:n], in_=xt[s][:, 0:n], func=AF.Square),
                         reads=[xb[s]], writes=[sqb[q]])
                    S.op("pe", lambda e, q=q, n=n, kc=kc, acc=acc: e.matmul(acc[:, 0:n], ones[:], sq[q][:, 0:n], start=(kc == 0), stop=(kc == KC - 1)),
                         reads=[sqb[q]], writes=[accb[bi % 2]])
                r = rst[bi % 2]
                rb = rstb[bi % 2]
                S.op("dve", lambda e, r=r, n=n, acc=acc: e.tensor_scalar(out=r[:, 0:n], in0=acc[:, 0:n], scalar1=1.0 / D, scalar2=RMS_EPS, op0=ALU.mult, op1=ALU.add),
                     reads=[accb[bi % 2]], writes=[rb])
                S.op("act", lambda e, r=r, n=n: e.activation(out=r[:, 0:n], in_=r[:, 0:n], func=AF.Sqrt), reads=[rb], writes=[rb])
                S.op("dve", lambda e, r=r, n=n: e.reciprocal(out=r[:, 0:n], in_=r[:, 0:n]), reads=[rb], writes=[rb])
                for kc in range(KC):
                    s = xi % NX
                    xi += 1
                    load("sp", xt[s][:, 0:n], src[kc * 128:(kc + 1) * 128, sc0:sc0 + n], xb[s])
                    q = kc % 2
                    S.op("dve", lambda e, s=s, q=q, n=n, kc=kc, r=r, vo=vo: e.scalar_tensor_tensor(
                        out=tm[q][:, 0:n], in0=xt[s][:, 0:n], scalar=vecs[:, vo, kc:kc + 1], in1=r[:, 0:n], op0=ALU.mult, op1=ALU.mult),
                        reads=[xb[s], rb], writes=[tmb[q]])
                    S.op("act", lambda e, q=q, n=n, kc=kc, ac0=ac0, vo=vo: e.activation(
                        out=actT[:, kc, ac0:ac0 + n], in_=tm[q][:, 0:n], func=AF.Identity, bias=vecs[:, vo + 1, kc:kc + 1], scale=1.0),
                        reads=[tmb[q]], writes=[])

        class Proj:
            def __init__(self, actT, tblocks, n_of=3):
                self.actT = actT
                self.tb = tblocks
                self.wb = [S.buf() for _ in range(NWB)]
                self.wi = 0
                self.psb = [S.buf() for _ in range(8)]
                self.pr = 0
                self.NOB = 4
                self.ob = [AR.alloc([512], BF16) for _ in range(self.NOB)]
                self.obb = [S.buf() for _ in range(self.NOB)]
                self.oi = 0
                self.n_of = n_of
                self.of = [AR.alloc([512], F32) for _ in range(n_of)]
                self.ofb = [S.buf() for _ in range(n_of)]
                self.ofi = 0
                self.ad = [AR.alloc([128], F32) for _ in range(2)]
                self.adb = [S.buf() for _ in range(2)]
                self.adi = 0
                self.pending = []

            def next_w(self, wsrc):
                i = self.wi % NWB
                self.wi += 1
                load_w(self.wb[i], wbt[i], wsrc)
                return i

            def mm_group(self, wslot, c0, n, pbank):
                actT = self.actT

                def fn(e, wslot=wslot, c0=c0, n=n, pbank=pbank):
                    ins = None
                    for kc in range(KC):
                        ins = e.matmul(ps[pbank][:, 0:n], wbt[wslot][:, kc, :], actT[:, kc, c0:c0 + n],
                                       start=(kc == 0), stop=(kc == KC - 1))
                    return ins
                S.op("pe", fn, reads=[self.wb[wslot]], writes=[self.psb[pbank]])

            def out_bf(self):
                i = self.oi % self.NOB
                self.oi += 1
                return self.ob[i], self.obb[i]

            def out_f32(self):
                i = self.ofi % self.n_of
                self.ofi += 1
                return self.of[i], self.ofb[i]

            def out_ada(self):
                i = self.adi % 2
                self.adi += 1
                return self.ad[i], self.adb[i]

        def ada_job(P, job):
            l, j = job["l"], job["j"]
            wslot = P.next_w(job["w"][0])

            def fn(e):
                ins = None
                for kc in range(KC):
                    ins = e.matmul(ps[7][0:2, 0:128], scT[:, kc, :], wbt[wslot][:, kc, :], start=(kc == 0), stop=(kc == KC - 1))
                return ins
            S.op("pe", fn, reads=[P.wb[wslot]], writes=[P.psb[7]])
            o, ob = P.out_ada()
            S.op("act", lambda e: e.activation(out=o[0:2, 0:128], in_=ps[7][0:2, 0:128], func=AF.Copy), reads=[P.psb[7]], writes=[ob])
            store("sp", MODD[l, :, j * 128:(j + 1) * 128], o[0:2, 0:128], ob)

        def stage_ada_only(l):
            chk("stage_ada_only")
            AR.reset()
            P = Proj(None, [], n_of=0)
            for job in ada_jobs(l):
                ada_job(P, job)
            S.barrier()

        def act_to_global(segs, c0, n):
            res = []
            for (is_ctx, sc0, ac0, sn) in segs:
                lo = max(c0, ac0)
                hi = min(c0 + n, ac0 + sn)
                if lo < hi:
                    g = (TW if is_ctx else 0) + sc0 + (lo - ac0)
                    res.append((lo, g, hi - lo))
            return res

        def stage_inproj_even(l, actT, segs, ncols, next_ada):
            chk("stage_inproj_even")
            i = l // 2
            P = Proj(actT, tok_blocks(0, ncols), n_of=0)
            cs = [AR.alloc([512], F32) for _ in range(2)]
            sn = [AR.alloc([512], F32) for _ in range(2)]
            csb = [S.buf() for _ in range(2)]
            snb = [S.buf() for _ in range(2)]
            qb = [AR.alloc([512], BF16) for _ in range(2)]
            qbb = [S.buf() for _ in range(2)]
            t1 = [AR.alloc([512], F32) for _ in range(2)]
            t1b = [S.buf() for _ in range(2)]
            t2 = [AR.alloc([512], F32) for _ in range(2)]
            t2b = [S.buf() for _ in range(2)]
            ri = [0]
            kinds = [("q", 16, QT), ("k", 4, KT), ("v", 4, VT), ("ga", 16, GA), ("u", 16, UU), ("vg", 16, VG), ("gb", 16, GB)]
            jcol = 0
            adaj = list(next_ada)
            n_ada_tot = len(adaj)
            nlat_ = sum(sn_ for (is_ctx, sc0, ac0, sn_) in segs if not is_ctx)
            tb_out = tok_blocks(128, nlat_ - 128) + (tok_blocks(nlat_, ncols) if CTX_OUT[l] else [])
            _kj = int(_os.environ.get("KJOBS", "1000"))
            _ks = int(_os.environ.get("KSKIP", "0"))
            if _os.environ.get("KNOADA"):
                adaj = []
            for (kind, nblk, dst) in kinds:
                for jb in range(nblk):
                    if jcol >= _kj or jcol < _ks:
                        jcol += 1
                        continue
                    wslot = P.next_w(ab_w_in[i, :, jcol * 128:(jcol + 1) * 128])
                    jcol += 1
                    for (c0, n) in (P.tb if kind in ("k", "v") else tb_out):
                        pbank = P.pr % 4
                        P.pr += 1
                        P.mm_group(wslot, c0, n, pbank)
                        o, ob = P.out_bf()
                        if kind in ("q", "k"):
                            r = ri[0] % 2
                            ri[0] += 1
                            for (ac, g, nn) in ([] if _os.environ.get("KROPE") in ("1", "2") else act_to_global(segs, c0, n)):
                                load("sp", cs[r][:, ac - c0:ac - c0 + nn], c_cos[:, g:g + nn], csb[r])
                                load("sp", sn[r][:, ac - c0:ac - c0 + nn], c_sin[:, g:g + nn], snb[r])
                            S.op("act", lambda e, r=r, n=n, pbank=pbank: e.activation(out=qb[r][:, 0:n], in_=ps[pbank][:, 0:n], func=AF.Copy),
                                 reads=[P.psb[pbank]], writes=[qbb[r]])
                            p2 = 4 + r
                            S.op("pe", lambda e, r=r, n=n, p2=p2: e.matmul(ps[p2][:, 0:n], perm[:], qb[r][:, 0:n], start=True, stop=True),
                                 reads=[qbb[r]], writes=[P.psb[p2]])
                            if _os.environ.get("KROPE") == "2":
                                S.op("act", lambda e, n=n, p2=p2, o=o: e.activation(out=o[:, 0:n], in_=ps[p2][:, 0:n], func=AF.Copy),
                                     reads=[P.psb[p2]], writes=[ob])
                                for (ac, g, nn) in act_to_global(segs, c0, n):
                                    store("sp", dst[jb * 128:(jb + 1) * 128, g:g + nn], o[:, ac - c0:ac - c0 + nn], ob)
                                continue
                            S.op("act", lambda e, r=r, n=n, pbank=pbank: e.activation(out=t1[r][:, 0:n], in_=ps[pbank][:, 0:n], func=AF.Copy),
                                 reads=[P.psb[pbank]], writes=[t1b[r]])
                            S.op("act", lambda e, r=r, n=n, p2=p2: e.activation(out=t2[r][:, 0:n], in_=ps[p2][:, 0:n], func=AF.Copy),
                                 reads=[P.psb[p2]], writes=[t2b[r]])
                            S.op("dve", lambda e, r=r, n=n: e.tensor_tensor(out=t1[r][:, 0:n], in0=t1[r][:, 0:n], in1=cs[r][:, 0:n], op=ALU.mult),
                                 reads=[t1b[r], csb[r]], writes=[t1b[r]])
                            S.op("dve", lambda e, r=r, n=n: e.tensor_tensor(out=t2[r][:, 0:n], in0=t2[r][:, 0:n], in1=sn[r][:, 0:n], op=ALU.mult),
                                 reads=[t2b[r], snb[r]], writes=[t2b[r]])
                            S.op("dve", lambda e, r=r, n=n, o=o: e.tensor_tensor(out=o[:, 0:n], in0=t1[r][:, 0:n], in1=t2[r][:, 0:n], op=ALU.add),
                                 reads=[t1b[r], t2b[r]], writes=[ob])
                        else:
                            func = {"v": AF.Copy, "ga": AF.Silu, "gb": AF.Silu, "u": AF.Gelu, "vg": AF.Gelu}[kind]
                            S.op("act", lambda e, n=n, pbank=pbank, o=o, func=func: e.activation(out=o[:, 0:n], in_=ps[pbank][:, 0:n], func=func),
                                 reads=[P.psb[pbank]], writes=[ob])
                        for (ac, g, nn) in act_to_global(segs, c0, n):
                            store("sp", dst[jb * 128:(jb + 1) * 128, g:g + nn], o[:, ac - c0:ac - c0 + nn], ob)
                    while adaj and (n_ada_tot - len(adaj)) * 88 < jcol * n_ada_tot:
                        ada_job(P, adaj.pop(0))
            while adaj:
                ada_job(P, adaj.pop(0))

        def stage_inproj_odd(l, actT, segs, ncols, next_ada):
            chk("stage_inproj_odd")
            i = l // 2
            P = Proj(actT, tok_blocks(0, ncols), n_of=0)
            tmk = AR.alloc([ncols], F32)
            tmkb = S.buf()
            for (is_ctx, sc0, ac0, sn_) in segs:
                g = (TW if is_ctx else 0) + sc0
                load("sp", tmk[:, ac0:ac0 + sn_], c_tmask[:, g:g + sn_], tmkb)
            sg_ = [AR.alloc([512], F32) for _ in range(2)]
            sgb = [S.buf() for _ in range(2)]
            tt = [AR.alloc([512], F32) for _ in range(2)]
            ttb = [S.buf() for _ in range(2)]
            ri = 0
            adaj = list(next_ada)
            nada_per = (len(adaj) + 31) // 32 if adaj else 0
            for jb in range(KC):
                wa = P.next_w(cv_w_in[i, :, jb * 128:(jb + 1) * 128])
                wb_ = P.next_w(cv_w_in[i, :, D + jb * 128:D + (jb + 1) * 128])
                for (c0, n) in P.tb:
                    pa = (P.pr % 2) * 2
                    pb = pa + 1
                    P.pr += 1
                    P.mm_group(wa, c0, n, pa)
                    P.mm_group(wb_, c0, n, pb)
                    r = ri % 2
                    ri += 1
                    o, ob = P.out_bf()
                    S.op("act", lambda e, r=r, n=n, pb=pb: e.activation(out=sg_[r][:, 0:n], in_=ps[pb][:, 0:n], func=AF.Sigmoid),
                         reads=[P.psb[pb]], writes=[sgb[r]])
                    S.op("act", lambda e, r=r, n=n, pa=pa: e.activation(out=tt[r][:, 0:n], in_=ps[pa][:, 0:n], func=AF.Copy),
                         reads=[P.psb[pa]], writes=[ttb[r]])
                    S.op("dve", lambda e, r=r, n=n: e.tensor_tensor(out=tt[r][:, 0:n], in0=tt[r][:, 0:n], in1=sg_[r][:, 0:n], op=ALU.mult),
                         reads=[ttb[r], sgb[r]], writes=[ttb[r]])
                    S.op("dve", lambda e, r=r, n=n, c0=c0, o=o: e.tensor_tensor(out=o[:, 0:n], in0=tt[r][:, 0:n], in1=tmk[:, c0:c0 + n], op=ALU.mult),
                         reads=[ttb[r], tmkb], writes=[ob])
                    for (ac, g, nn) in act_to_global(segs, c0, n):
                        store("sp", GLU[jb * 128:(jb + 1) * 128, g:g + nn], o[:, ac - c0:ac - c0 + nn], ob)
                wg = P.next_w(cv_w_in[i, :, 2 * D + jb * 128:2 * D + (jb + 1) * 128])
                for (c0, n) in P.tb:
                    pg_ = 4 + (P.pr % 2)
                    P.pr += 1
                    P.mm_group(wg, c0, n, pg_)
                    o, ob = P.out_bf()
                    S.op("act", lambda e, n=n, pg_=pg_, o=o: e.activation(out=o[:, 0:n], in_=ps[pg_][:, 0:n], func=AF.Silu),
                         reads=[P.psb[pg_]], writes=[ob])
                    for (ac, g, nn) in act_to_global(segs, c0, n):
                        store("sp", SG[jb * 128:(jb + 1) * 128, g:g + nn], o[:, ac - c0:ac - c0 + nn], ob)
                for _ in range(nada_per):
                    if adaj:
                        ada_job(P, adaj.pop(0))
            while adaj:
                ada_job(P, adaj.pop(0))

        def stage_outproj(l, w_out, segs, ncols):
            chk("stage_outproj")
            AR.reset()
            actT = AR.alloc([KC, ncols], BF16)
            ldb = [S.buf() for _ in range(8)]
            for qd in range(8):
                for (is_ctx, sc0, ac0, sn_) in segs:
                    g = (TW if is_ctx else 0) + sc0
                    load("sp", actT[:, qd * 4:(qd + 1) * 4, ac0:ac0 + sn_],
                         MT[qd * 512:(qd + 1) * 512, g:g + sn_].rearrange("(c p) t -> p c t", p=128), ldb[qd])
            P = Proj(actT, tok_blocks(0, ncols))
            assert len(P.tb) <= 4
            sq = [AR.alloc([512], BF16) for _ in range(2)]
            sqb = [S.buf() for _ in range(2)]
            accb = [S.buf() for _ in range(4)]
            ri = 0
            first = True
            for jb in range(KC):
                wslot = P.next_w(w_out[:, jb * 128:(jb + 1) * 128])
                for ti, (c0, n) in enumerate(P.tb):
                    pbank = 4 + (P.pr % 3)
                    P.pr += 1
                    actT_ = actT

                    def fn(e, wslot=wslot, c0=c0, n=n, pbank=pbank):
                        ins = None
                        for kc in range(KC):
                            ins = e.matmul(ps[pbank][:, 0:n], wbt[wslot][:, kc, :], actT_[:, kc, c0:c0 + n],
                                           start=(kc == 0), stop=(kc == KC - 1))
                        return ins
                    S.op("pe", fn, reads=[P.wb[wslot]] + (ldb if first else []), writes=[P.psb[pbank]])
                    first = False
                    o, ob = P.out_f32()
                    r = ri % 2
                    ri += 1
                    S.op("act", lambda e, n=n, pbank=pbank, o=o: e.activation(out=o[:, 0:n], in_=ps[pbank][:, 0:n], func=AF.Copy),
                         reads=[P.psb[pbank]], writes=[ob])
                    S.op("act", lambda e, n=n, pbank=pbank, r=r: e.activation(out=sq[r][:, 0:n], in_=ps[pbank][:, 0:n], func=AF.Square),
                         reads=[P.psb[pbank]], writes=[sqb[r]])
                    S.op("pe", lambda e, n=n, r=r, ti=ti, jb=jb: e.matmul(ps[ti][:, 0:n], ones[:], sq[r][:, 0:n], start=(jb == 0), stop=(jb == KC - 1)),
                         reads=[sqb[r]], writes=[accb[ti]])
                    store("sp", YY[jb * 128:(jb + 1) * 128, c0:c0 + n], o[:, 0:n], ob)
            rpb = S.buf()
            for ti, (c0, n) in enumerate(P.tb):
                S.op("dve", lambda e, ti=ti, c0=c0, n=n: e.tensor_scalar(out=rpost[:, c0:c0 + n], in0=ps[ti][:, 0:n], scalar1=1.0 / D, scalar2=RMS_EPS, op0=ALU.mult, op1=ALU.add),
                     reads=[accb[ti]], writes=[rpb])
            S.op("act", lambda e: e.activation(out=rpost[:, 0:ncols], in_=rpost[:, 0:ncols], func=AF.Sqrt), reads=[rpb], writes=[rpb])
            S.op("dve", lambda e: e.reciprocal(out=rpost[:, 0:ncols], in_=rpost[:, 0:ncols]), reads=[rpb], writes=[rpb])
            S.barrier()

        def stage_postnorm(l, segs, x_in, c_in, x_out, c_out, final):
            chk("stage_postnorm")
            AR.reset()
            NB_ = 3
            yt = [AR.alloc([512], F32) for _ in range(NB_)]
            ytb = [S.buf() for _ in range(NB_)]
            xt = [AR.alloc([512], F32) for _ in range(NB_)]
            xtb = [S.buf() for _ in range(NB_)]
            ot = [AR.alloc([512], F32) for _ in range(NB_)]
            otb = [S.buf() for _ in range(NB_)]
            it = 0
            for (is_ctx, sc0, ac0, sn_) in segs:
                for (b0, n) in tok_blocks(0, sn_):
                    for kc in range(KC):
                        s = it % NB_
                        it += 1
                        load("sp", yt[s][:, 0:n], YY[kc * 128:(kc + 1) * 128, ac0 + b0:ac0 + b0 + n], ytb[s])
                        xsrc = c_in if is_ctx else x_in
                        load("sp", xt[s][:, 0:n], xsrc[kc * 128:(kc + 1) * 128, sc0 + b0:sc0 + b0 + n], xtb[s])
                        vo = 5 if is_ctx else 2
                        S.op("dve", lambda e, s=s, n=n, kc=kc, vo=vo, a0=ac0 + b0: e.scalar_tensor_tensor(
                            out=yt[s][:, 0:n], in0=yt[s][:, 0:n], scalar=vecs[:, vo, kc:kc + 1], in1=rpost[:, a0:a0 + n], op0=ALU.mult, op1=ALU.mult),
                            reads=[ytb[s]], writes=[ytb[s]])
                        S.op("pool", lambda e, s=s, n=n: e.tensor_tensor(out=ot[s][:, 0:n], in0=yt[s][:, 0:n], in1=xt[s][:, 0:n], op=ALU.add),
                             reads=[ytb[s], xtb[s]], writes=[otb[s]])
                        if is_ctx:
                            dst = c_out[kc * 128:(kc + 1) * 128, sc0 + b0:sc0 + b0 + n]
                        elif final:
                            dst = x_out[kc * 128:(kc + 1) * 128, sc0 + b0 - HALO:sc0 + b0 - HALO + n]
                        else:
                            dst = x_out[kc * 128:(kc + 1) * 128, sc0 + b0:sc0 + b0 + n]
                        store("sp", dst, ot[s][:, 0:n], otb[s])
            S.barrier()

        def stage_attn(l):
            chk("stage_attn")
            i = l // 2
            AR.reset()
            ilo, ihi = IN_R[l]
            olo, ohi = OUT_R[l]
            do_ctxq = CTX_OUT[l]
            nk_lat = (ihi - ilo) * 128
            NK = nk_lat + CTX
            nq_lat = (ohi - olo) * 128
            NQ = nq_lat + (CTX if do_ctxq else 0)
            QTh = [AR.alloc([4, NQ], BF16) for _ in range(1)]
            GAh = [AR.alloc([4, NQ], BF16) for _ in range(1)]
            KTh = AR.alloc([NK], BF16)
            VTh = AR.alloc([NK], BF16)
            nkt = NK // 128
            Vtok = AR.alloc([nkt, 128], BF16)
            sinkb = AR.alloc([16], F32)
            se = AR.alloc([16], F32)
            Eb = [AR.alloc([512], BF16) for _ in range(5)]
            dn = AR.alloc([512], F32)
            tO = AR.alloc([512], F32)
            NMO = 2
            mo = [AR.alloc([512], BF16) for _ in range(NMO)]
            bq, bg, bk, bv, bvt, bsk, bse, bdn, btO = [S.buf() for _ in range(9)]
            bE = [S.buf() for _ in range(5)]
            bmo = [S.buf() for _ in range(NMO)]
            pb = [S.buf() for _ in range(8)]
            load("sp", sinkb, ab_sinkB[i], bsk)
            S.op("act", lambda e: e.activation(out=se, in_=sinkb, func=AF.Exp), reads=[bsk], writes=[bse])
            moi = 0
            for hk in range(4):
                load("sp", QTh[0][:, :, 0:nq_lat], QT[hk * 512:(hk + 1) * 512, olo * 128:ohi * 128].rearrange("(h d) t -> d h t", d=128), bq)
                load("sp", GAh[0][:, :, 0:nq_lat], GA[hk * 512:(hk + 1) * 512, olo * 128:ohi * 128].rearrange("(h d) t -> d h t", d=128), bg)
                load("sp", KTh[:, 0:nk_lat], KT[hk * 128:(hk + 1) * 128, ilo * 128:ihi * 128], bk)
                load("sp", VTh[:, 0:nk_lat], VT[hk * 128:(hk + 1) * 128, ilo * 128:ihi * 128], bv)
                if do_ctxq:
                    load("sp", QTh[0][:, :, nq_lat:NQ], QT[hk * 512:(hk + 1) * 512, TW:TALL].rearrange("(h d) t -> d h t", d=128), bq)
                    load("sp", GAh[0][:, :, nq_lat:NQ], GA[hk * 512:(hk + 1) * 512, TW:TALL].rearrange("(h d) t -> d h t", d=128), bg)
                load("sp", KTh[:, nk_lat:NK], KT[hk * 128:(hk + 1) * 128, TW:TALL], bk)
                load("sp", VTh[:, nk_lat:NK], VT[hk * 128:(hk + 1) * 128, TW:TALL], bv)
                for kt in range(nkt):
                    pbank = 6 + (kt % 2)
                    pv = ps[pbank][:].bitcast(BF16)
                    S.op("pe", lambda e, kt=kt, pv=pv: e.transpose(pv[:, 0:128], VTh[:, kt * 128:(kt + 1) * 128], ident[:]),
                         reads=[bv], writes=[pb[pbank]])
                    S.op("dve", lambda e, kt=kt, pv=pv: e.tensor_copy(out=Vtok[:, kt, :], in_=pv[:, 0:128]),
                         reads=[pb[pbank]], writes=[bvt])
                qtiles = [("lat", t) for t in range(olo, ohi)] + ([("ctx", 0), ("ctx", 1)] if do_ctxq else [])
                for (qk, t) in qtiles:
                    if qk == "lat":
                        qc = (t - olo) * 128
                        klist = [(t - 1 - ilo, t - 1, 0), (t - ilo, t, None), (t + 1 - ilo, t + 1, 1)]
                        klist += [(nk_lat // 128, 16, None), (nk_lat // 128 + 1, 17, None)]
                    else:
                        qc = nq_lat + t * 128
                        klist = [(nk_lat // 128, 16, None), (nk_lat // 128 + 1, 17, None)]
                    nkl = len(klist)
                    for ki, (kti, kbi, mk) in enumerate(klist):
                        S.op("pe", lambda e, ki=ki, kti=kti, qc=qc: e.matmul(ps[ki][:, :].rearrange("p (h q) -> p h q", h=4),
                                                                           KTh[:, kti * 128:(kti + 1) * 128], QTh[0][:, :, qc:qc + 128], start=True, stop=True),
                             reads=[bk, bq], writes=[pb[ki]])
                        S.op("act", lambda e, ki=ki, kbi=kbi: e.activation(out=Eb[ki][:], in_=ps[ki][:], func=AF.Exp, bias=kbias[:, kbi:kbi + 1], scale=ATT_SCALE),
                             reads=[pb[ki]], writes=[bE[ki]])
                        if mk is not None:
                            S.op("dve", lambda e, ki=ki, mk=mk: e.tensor_tensor(out=Eb[ki][:], in0=Eb[ki][:], in1=tri[:, mk, :], op=ALU.mult),
                                 reads=[bE[ki]], writes=[bE[ki]])
                    for ki, (kti, kbi, mk) in enumerate(klist):
                        S.op("pe", lambda e, ki=ki, nkl=nkl: e.matmul(ps[5][:], ones[:], Eb[ki][:], start=(ki == 0), stop=(ki == nkl - 1)),
                             reads=[bE[ki]], writes=[pb[5]])
                    for ki, (kti, kbi, mk) in enumerate(klist):
                        S.op("pe", lambda e, ki=ki, kti=kti, nkl=nkl: e.matmul(ps[6][:], Vtok[:, kti, :], Eb[ki][:], start=(ki == 0), stop=(ki == nkl - 1)),
                             reads=[bE[ki], bvt], writes=[pb[6]])
                    for h in range(4):
                        S.op("dve", lambda e, h=h, hk=hk: e.tensor_scalar(out=dn[:, h * 128:(h + 1) * 128], in0=ps[5][:, h * 128:(h + 1) * 128],
                                                                         scalar1=se[:, hk * 4 + h:hk * 4 + h + 1], scalar2=None, op0=ALU.add),
                             reads=[pb[5], bse], writes=[bdn])
                    S.op("dve", lambda e: e.reciprocal(out=dn[:], in_=dn[:]), reads=[bdn], writes=[bdn])
                    S.op("dve", lambda e: e.tensor_tensor(out=tO[:], in0=ps[6][:], in1=dn[:], op=ALU.mult), reads=[pb[6], bdn], writes=[btO])
                    m = moi % NMO
                    moi += 1
                    S.op("dve", lambda e, m=m, qc=qc: e.tensor_tensor(out=mo[m][:].rearrange("p (h q) -> p h q", h=4), in0=tO[:].rearrange("p (h q) -> p h q", h=4),
                                                                     in1=GAh[0][:, :, qc:qc + 128], op=ALU.mult),
                         reads=[btO, bg], writes=[bmo[m]])
                    gcol = (t * 128) if qk == "lat" else (TW + t * 128)
                    store("sp", MT[hk * 512:(hk + 1) * 512, gcol:gcol + 128].rearrange("(h d) t -> d h t", d=128),
                          mo[m][:].rearrange("p (h q) -> p h q", h=4), bmo[m])
                S.barrier()

        def stage_gmlp(l):
            chk("stage_gmlp")
            i = l // 2
            AR.reset()
            olo, ohi = OUT_R[l]
            tiles = [t * 128 for t in range(olo, ohi)] + ([TW, TW + 128] if CTX_OUT[l] else [])
            wsT = AR.alloc([16, 128], BF16)
            wsb = AR.alloc([2048], F32)
            lg = AR.alloc([16], F32)
            lb = AR.alloc([16], F32)
            bws, bwsb, blg, blb = [S.buf() for _ in range(4)]
            wsf = AR.alloc([16, 128], F32)
            bwsf = S.buf()
            load("sp", wsf, ab_wsT[i], bwsf)
            S.op("act", lambda e: e.activation(out=wsT[:], in_=wsf[:], func=AF.Copy), reads=[bwsf], writes=[bws])
            load("sp", wsb, ab_wsbB[i], bwsb)
            load("sp", lg, ab_ln_gT[i], blg)
            load("sp", lb, ab_ln_bT[i], blb)
            NR = 2
            vg = [AR.alloc([16, 128], BF16) for _ in range(NR)]
            uu = [AR.alloc([16, 128], BF16) for _ in range(NR)]
            gb = [AR.alloc([16, 128], BF16) for _ in range(NR)]
            bvg = [S.buf() for _ in range(NR)]
            buu = [S.buf() for _ in range(NR)]
            bgb = [S.buf() for _ in range(NR)]
            sq = AR.alloc([16, 128], BF16)
            bsq = S.buf()
            mean = AR.alloc([128], F32)
            msq = AR.alloc([128], F32)
            rstd = AR.alloc([128], F32)
            bmean, bmsq, brstd = S.buf(), S.buf(), S.buf()
            tn = AR.alloc([16, 128], F32)
            btn = S.buf()
            vn = AR.alloc([16, 128], BF16)
            bvn = S.buf()
            vlnT = AR.alloc([2048], BF16)
            bvl = S.buf()
            t1 = [AR.alloc([512], F32) for _ in range(2)]
            bt1 = [S.buf() for _ in range(2)]
            mo = [AR.alloc([512], BF16) for _ in range(4)]
            bmo = [S.buf() for _ in range(4)]
            pb = [S.buf() for _ in range(8)]
            for ti, gc in enumerate(tiles):
                r = ti % NR
                load("sp", vg[r], VG[:, gc:gc + 128].rearrange("(g c) t -> c g t", c=128), bvg[r])
                load("sp", uu[r], UU[:, gc:gc + 128].rearrange("(g c) t -> c g t", c=128), buu[r])
                load("sp", gb[r], GB[:, gc:gc + 128].rearrange("(g c) t -> c g t", c=128), bgb[r])
                S.op("act", lambda e, r=r: e.activation(out=sq[:], in_=vg[r][:], func=AF.Square), reads=[bvg[r]], writes=[bsq])

                def fsum(e, r=r):
                    ins = None
                    for g in range(16):
                        ins = e.matmul(ps[0][:, 0:128], ones[:], vg[r][:, g, :], start=(g == 0), stop=(g == 15))
                    return ins
                S.op("pe", fsum, reads=[bvg[r]], writes=[pb[0]])

                def fsq(e):
                    ins = None
                    for g in range(16):
                        ins = e.matmul(ps[1][:, 0:128], ones[:], sq[:, g, :], start=(g == 0), stop=(g == 15))
                    return ins
                S.op("pe", fsq, reads=[bsq], writes=[pb[1]])
                S.op("dve", lambda e: e.tensor_scalar(out=mean[:], in0=ps[0][:, 0:128], scalar1=1.0 / 2048, scalar2=None, op0=ALU.mult),
                     reads=[pb[0]], writes=[bmean])
                S.op("dve", lambda e: e.tensor_tensor(out=msq[:], in0=mean[:], in1=mean[:], op=ALU.mult), reads=[bmean], writes=[bmsq])
                S.op("dve", lambda e: e.scalar_tensor_tensor(out=rstd[:], in0=ps[1][:, 0:128], scalar=1.0 / 2048, in1=msq[:], op0=ALU.mult, op1=ALU.subtract),
                     reads=[pb[1], bmsq], writes=[brstd])
                S.op("dve", lambda e: e.tensor_scalar(out=rstd[:], in0=rstd[:], scalar1=LN_EPS, scalar2=None, op0=ALU.add), reads=[brstd], writes=[brstd])
                S.op("act", lambda e: e.activation(out=rstd[:], in_=rstd[:], func=AF.Sqrt), reads=[brstd], writes=[brstd])
                S.op("dve", lambda e: e.reciprocal(out=rstd[:], in_=rstd[:]), reads=[brstd], writes=[brstd])
                for g in range(16):
                    S.op("dve", lambda e, r=r, g=g: e.tensor_tensor(out=tn[:, g, :], in0=vg[r][:, g, :], in1=mean[:], op=ALU.subtract),
                         reads=[bvg[r], bmean], writes=[btn])
                    S.op("dve", lambda e, g=g: e.tensor_tensor(out=tn[:, g, :], in0=tn[:, g, :], in1=rstd[:], op=ALU.mult),
                         reads=[btn, brstd], writes=[btn])
                    S.op("act", lambda e, g=g: e.activation(out=vn[:, g, :], in_=tn[:, g, :], func=AF.Identity, bias=lb[:, g:g + 1], scale=lg[:, g:g + 1]),
                         reads=[btn, blg, blb], writes=[bvn])
                for half in range(2):
                    pbank = 2 + half
                    pv = ps[pbank][:].bitcast(BF16)

                    def ftr(e, half=half, pv=pv):
                        ins = None
                        for g8 in range(8):
                            g = half * 8 + g8
                            ins = e.transpose(pv[:, g8 * 128:(g8 + 1) * 128], vn[:, g, :], ident[:])
                        return ins
                    S.op("pe", ftr, reads=[bvn], writes=[pb[pbank]])
                    S.op("act", lambda e, half=half, pv=pv: e.activation(out=vlnT[:, half * 1024:(half + 1) * 1024], in_=pv[:, 0:1024], func=AF.Copy),
                         reads=[pb[pbank]], writes=[bvl])
                for g4 in range(4):
                    pbank = 4 + g4

                    def fsp(e, g4=g4, pbank=pbank):
                        ins = None
                        for gg in range(4):
                            g = g4 * 4 + gg
                            ins = e.matmul(ps[pbank][:, gg * 128:(gg + 1) * 128], vlnT[:, g * 128:(g + 1) * 128], wsT[:, g, :], start=True, stop=True)
                        return ins
                    S.op("pe", fsp, reads=[bvl, bws], writes=[pb[pbank]])
                    q = g4 % 2
                    S.op("dve", lambda e, g4=g4, pbank=pbank, q=q: e.tensor_tensor(out=t1[q][:], in0=ps[pbank][:], in1=wsb[:, g4 * 512:(g4 + 1) * 512], op=ALU.add),
                         reads=[pb[pbank], bwsb], writes=[bt1[q]])
                    S.op("dve", lambda e, g4=g4, q=q, r=r: e.tensor_tensor(out=t1[q][:].rearrange("p (g t) -> p g t", g=4), in0=t1[q][:].rearrange("p (g t) -> p g t", g=4),
                                                                         in1=uu[r][:, g4 * 4:(g4 + 1) * 4, :], op=ALU.mult),
                         reads=[bt1[q], buu[r]], writes=[bt1[q]])
                    S.op("dve", lambda e, g4=g4, q=q, r=r: e.tensor_tensor(out=mo[g4][:].rearrange("p (g t) -> p g t", g=4), in0=t1[q][:].rearrange("p (g t) -> p g t", g=4),
                                                                         in1=gb[r][:, g4 * 4:(g4 + 1) * 4, :], op=ALU.mult),
                         reads=[bt1[q], bgb[r]], writes=[bmo[g4]])
                    store("sp", MT[2048 + g4 * 512:2048 + (g4 + 1) * 512, gc:gc + 128].rearrange("(g c) t -> c g t", c=128),
                          mo[g4][:].rearrange("p (g t) -> p g t", g=4), bmo[g4])
            S.barrier()

        def stage_conv(l):
            chk("stage_conv")
            i = l // 2
            AR.reset()
            olo, ohi = OUT_R[l]
            dw = AR.alloc([KC, 31], F32)
            dwb = AR.alloc([KC], F32)
            lg = AR.alloc([KC], F32)
            lb = AR.alloc([KC], F32)
            bdw, bdwb, blg, blb = [S.buf() for _ in range(4)]
            load("sp", dw, cv_dwT[i], bdw)
            load("sp", dwb, cv_dw_bT[i], bdwb)
            load("sp", lg, cv_ln_gT[i], blg)
            load("sp", lb, cv_ln_bT[i], blb)
            NB_ = 256
            blocks = [(False, c0, n) for (c0, n) in tok_blocks(olo * 128, ohi * 128, NB_)]
            if CTX_OUT[l]:
                blocks.append((True, 0, CTX))
            ybuf = [AR.alloc([KC, NB_], F32) for _ in range(2)]
            by = [[S.buf() for _ in range(KC)] for _ in range(2)]
            NG = 4
            gin = [AR.alloc([NB_ + 32], BF16) for _ in range(NG)]
            NDG = 3
            dg = [AR.alloc([31, 128], BF16) for _ in range(NDG)]
            bdg = [S.buf() for _ in range(NDG)]
            bgin = [S.buf() for _ in range(NG)]
            bdgB = [S.buf() for _ in range(NDG)]
            zb = AR.alloc([1], F32)
            bzb = S.buf()
            S.op("pool", lambda e: e.memset(zb[:, :], 0.0), writes=[bzb])
            NDV = 21

            def emit_dg(kc):
                d2 = kc % NDG

                def fdg(e, d2=d2, kc=kc):
                    ins = None
                    for j in range(NDV):
                        ins = e.tensor_scalar(out=dg[d2][:, j, :], in0=ident[:], scalar1=dw[:, kc, j:j + 1], scalar2=None, op0=ALU.mult)
                    return ins
                S.op("dve", fdg, reads=[bdw], writes=[bdg[d2]])

                def fdga(e, d2=d2, kc=kc):
                    ins = None
                    for j in range(NDV, 31):
                        ins = e.activation(out=dg[d2][:, j, :], in_=ident[:], func=AF.Identity, bias=zb[:, 0:1], scale=dw[:, kc, j:j + 1])
                    return ins
                S.op("act", fdga, reads=[bdw, bzb], writes=[bdgB[d2]])
            pacc = [AR.alloc([NB_], F32) for _ in range(3)]
            bpacc = [S.buf() for _ in range(3)]
            yb = [AR.alloc([NB_], BF16) for _ in range(2)]
            byb = [S.buf() for _ in range(2)]
            sq = [AR.alloc([NB_], BF16) for _ in range(2)]
            bsq = [S.buf() for _ in range(2)]
            mean = [AR.alloc([NB_], F32) for _ in range(2)]
            msq = AR.alloc([NB_], F32)
            rstd = [AR.alloc([NB_], F32) for _ in range(2)]
            bmean = [S.buf() for _ in range(2)]
            bmsq = S.buf()
            brstd = [S.buf() for _ in range(2)]
            sgt = [AR.alloc([NB_], BF16) for _ in range(3)]
            bsg = [S.buf() for _ in range(3)]
            tz = [AR.alloc([NB_], F32) for _ in range(2)]
            btz = [S.buf() for _ in range(2)]
            zz = [AR.alloc([NB_], F32) for _ in range(2)]
            bzz = [S.buf() for _ in range(2)]
            mo = [AR.alloc([NB_], BF16) for _ in range(3)]
            bmo = [S.buf() for _ in range(3)]
            pb = [S.buf() for _ in range(8)]
            gi = 0
            for bi, (is_ctx, c0, n) in enumerate(blocks):
                yy = ybuf[bi % 2]
                byy = by[bi % 2]
                psum_s = ps[(bi % 2) * 2]
                psum_q = ps[(bi % 2) * 2 + 1]
                pbs = pb[(bi % 2) * 2]
                pbq = pb[(bi % 2) * 2 + 1]
                gbase = TW if is_ctx else 0
                pend_stat = None
                for kc in range(KC):
                    s = gi % NG
                    gi += 1
                    if is_ctx:
                        S.op("pool", lambda e, s=s: e.memset(gin[s][:, :], 0.0), writes=[bgin[s]])
                        load("sp", gin[s][:, 15:15 + n], GLU[kc * 128:(kc + 1) * 128, TW:TW + n], bgin[s])
                    else:
                        load("sp", gin[s][:, 0:n + 30], GLU[kc * 128:(kc + 1) * 128, c0 - 15:c0 + n + 15], bgin[s])
                    d2 = kc % NDG
                    if kc == 0:
                        emit_dg(0)
                    if kc + 1 < KC:
                        emit_dg(kc + 1)
                    pc = 4 + (kc % 4)

                    def fconv(e, s=s, d2=d2, n=n, pc=pc):
                        ins = None
                        for j in range(31):
                            ins = e.matmul(ps[pc][:, 0:n], dg[d2][:, j, :], gin[s][:, j:j + n], start=(j == 0), stop=(j == 30))
                        return ins
                    S.op("pe", fconv, reads=[bdg[d2], bdgB[d2], bgin[s]], writes=[pb[pc]])
                    S.op("act", lambda e, kc=kc, n=n, pc=pc, yy=yy: e.activation(out=yy[:, kc, 0:n], in_=ps[pc][:, 0:n], func=AF.Identity, bias=dwb[:, kc:kc + 1], scale=1.0),
                         reads=[pb[pc], bdwb], writes=[byy[kc]])
                    q = kc % 2
                    S.op("act", lambda e, kc=kc, n=n, q=q, pc=pc: e.activation(out=yb[q][:, 0:n], in_=ps[pc][:, 0:n], func=AF.Identity, bias=dwb[:, kc:kc + 1], scale=1.0), reads=[pb[pc], bdwb], writes=[byb[q]])
                    S.op("act", lambda e, kc=kc, n=n, q=q, pc=pc: e.activation(out=sq[q][:, 0:n], in_=ps[pc][:, 0:n], func=AF.Square, bias=dwb[:, kc:kc + 1], scale=1.0), reads=[pb[pc], bdwb], writes=[bsq[q]])
                    def fstat(kc=kc, n=n, q=q, psum_s=psum_s, psum_q=psum_q, pbs=pbs, pbq=pbq):
                        S.op("pe", lambda e: e.matmul(psum_s[:, 0:n], ones[:], yb[q][:, 0:n], start=(kc == 0), stop=(kc == KC - 1)),
                             reads=[byb[q]], writes=[pbs])
                        S.op("pe", lambda e: e.matmul(psum_q[:, 0:n], ones[:], sq[q][:, 0:n], start=(kc == 0), stop=(kc == KC - 1)),
                             reads=[bsq[q]], writes=[pbq])
                    if pend_stat is not None:
                        pend_stat()
                    pend_stat = fstat
                pend_stat()
                pend_stat = None
                mm = mean[bi % 2]
                rr = rstd[bi % 2]
                bm = bmean[bi % 2]
                br = brstd[bi % 2]
                S.op("dve", lambda e, n=n, mm=mm, psum_s=psum_s: e.tensor_scalar(out=mm[:, 0:n], in0=psum_s[:, 0:n], scalar1=1.0 / D, scalar2=None, op0=ALU.mult), reads=[pbs], writes=[bm])
                S.op("dve", lambda e, n=n, mm=mm: e.tensor_tensor(out=msq[:, 0:n], in0=mm[:, 0:n], in1=mm[:, 0:n], op=ALU.mult), reads=[bm], writes=[bmsq])
                S.op("dve", lambda e, n=n, rr=rr, psum_q=psum_q: e.scalar_tensor_tensor(out=rr[:, 0:n], in0=psum_q[:, 0:n], scalar=1.0 / D, in1=msq[:, 0:n], op0=ALU.mult, op1=ALU.subtract),
                     reads=[pbq, bmsq], writes=[br])
                S.op("dve", lambda e, n=n, rr=rr: e.tensor_scalar(out=rr[:, 0:n], in0=rr[:, 0:n], scalar1=LN_EPS, scalar2=None, op0=ALU.add), reads=[br], writes=[br])
                S.op("act", lambda e, n=n, rr=rr: e.activation(out=rr[:, 0:n], in_=rr[:, 0:n], func=AF.Sqrt), reads=[br], writes=[br])
                S.op("dve", lambda e, n=n, rr=rr: e.reciprocal(out=rr[:, 0:n], in_=rr[:, 0:n]), reads=[br], writes=[br])
                for kc in range(KC):
                    q = kc % 2
                    s3 = kc % 3
                    load("sp", sgt[s3][:, 0:n], SG[kc * 128:(kc + 1) * 128, gbase + c0:gbase + c0 + n], bsg[s3])
                    S.op("pool", lambda e, kc=kc, n=n, q=q, yy=yy, mm=mm: e.tensor_tensor(out=tz[q][:, 0:n], in0=yy[:, kc, 0:n], in1=mm[:, 0:n], op=ALU.subtract),
                         reads=[byy[kc], bm], writes=[btz[q]])
                    S.op("pool", lambda e, n=n, q=q, rr=rr: e.tensor_tensor(out=tz[q][:, 0:n], in0=tz[q][:, 0:n], in1=rr[:, 0:n], op=ALU.mult),
                         reads=[btz[q], br], writes=[btz[q]])
                    S.op("act", lambda e, kc=kc, n=n, q=q: e.activation(out=zz[q][:, 0:n], in_=tz[q][:, 0:n], func=AF.Silu, bias=lb[:, kc:kc + 1], scale=lg[:, kc:kc + 1]),
                         reads=[btz[q], blg, blb], writes=[bzz[q]])
                    S.op("pool", lambda e, n=n, q=q, s3=s3: e.tensor_tensor(out=mo[s3][:, 0:n], in0=zz[q][:, 0:n], in1=sgt[s3][:, 0:n], op=ALU.mult),
                         reads=[bzz[q], bsg[s3]], writes=[bmo[s3]])
                    store("sp", MT[kc * 128:(kc + 1) * 128, gbase + c0:gbase + c0 + n], mo[s3][:, 0:n], bmo[s3])
            S.barrier()

        def segs_for(lo, hi, with_ctx):
            segs = [(False, lo * 128, 0, (hi - lo) * 128)]
            ncols = (hi - lo) * 128
            if with_ctx:
                segs.append((True, 0, ncols, CTX))
                ncols += CTX
            return segs, ncols

        def _drive():
          stage_setup()
          stage_ada_only(0)
          x_bufs = [xT, XA, XB, XA, None]
          c_bufs = [ctxT, CA, CB, None, None]
          for l in range(depth):
            even = (l % 2 == 0)
            final = (l == DEPTH - 1)
            stage_modprep(l)
            x_in = x_bufs[l]
            x_out = out if final else x_bufs[l + 1]
            c_in = c_bufs[l]
            c_out = c_bufs[l + 1]
            ilo, ihi = IN_R[l]
            olo, ohi = OUT_R[l]
            segs_in, ncols_in = segs_for(ilo, ihi, CTX_IN[l])
            AR.reset()
            actT = AR.alloc([KC, ncols_in], BF16)
            mark = AR.off
            stage_prenorm(l, x_in, c_in, actT, segs_in)
            S.barrier()
            AR.off = mark
            nada = ada_jobs(l + 1) if l + 1 < depth else []
            if even:
                stage_inproj_even(l, actT, segs_in, ncols_in, nada)
                S.barrier()
                stage_attn(l)
                stage_gmlp(l)
                w_out = ab_w_out[l // 2]
            else:
                stage_inproj_odd(l, actT, segs_in, ncols_in, nada)
                S.barrier()
                stage_conv(l)
                w_out = cv_w_out[l // 2]
            segs_out, ncols_out = segs_for(olo, ohi, CTX_OUT[l])
            stage_outproj(l, w_out, segs_out, ncols_out)
            stage_postnorm(l, segs_out, x_in, c_in, x_out, c_out, final)
            S.barrier(new_epoch=True)

        try:
            _drive()
        except _Stop:
            S.barrier()
        block = es.enter_context(nc.Block())

        @block.tensor
        def _(e):
            S.replay("pe", e)

        @block.scalar
        def _(e):
            S.replay("act", e)

        @block.vector
        def _(e):
            S.replay("dve", e)

        @block.gpsimd
        def _(e):
            S.replay("pool", e)

        @block.sync
        def _(e):
            S.replay("sp", e)

    return nc


def _fm(v, k=KC):
    sh = v.shape[:-1]
    return np.ascontiguousarray(np.swapaxes(v.reshape(sh + (k, 128)), -1, -2))


def _rope_tables(core):
    pos = core * OWN - HALO + np.arange(TW)
    row = (pos // 64).astype(np.float32)
    col = (pos % 64).astype(np.float32)
    inv = (1.0 / (10000.0 ** (np.arange(32, dtype=np.float32) / 32))).astype(np.float32)
    cos = np.ones((128, TALL), np.float32)
    sin = np.zeros((128, TALL), np.float32)
    for d in range(128):
        p = row if d < 64 else col
        ang = (p * inv[d % 32]).astype(np.float32)
        cos[d, :TW] = np.cos(ang)
        s = np.sin(ang)
        sin[d, :TW] = -s if (d % 64) < 32 else s
    return cos, sin


def _prep_inputs(inputs):
    f = lambda a: np.ascontiguousarray(np.asarray(a, dtype=np.float32))
    x = f(inputs["x"])[0]
    ctx = f(inputs["ctx"])[0]
    c = f(inputs["c"])[0]
    c_ctx = f(inputs["c_ctx"])
    shared = {}
    shared["ctxT"] = np.ascontiguousarray(ctx.T)
    shared["cT"] = np.ascontiguousarray(np.stack([_fm(c), _fm(c_ctx)], axis=-1))
    shared["ada_w"] = f(inputs["ada_w"])
    shared["ada_bT"] = _fm(f(inputs["ada_b"]), 96)
    shared["pre_gT"] = _fm(f(inputs["pre_g"]))
    shared["post_gT"] = _fm(f(inputs["post_g"]))
    shared["ab_w_in"] = f(inputs["ab_w_in"])
    shared["ab_sinkB"] = np.ascontiguousarray(np.broadcast_to(f(inputs["ab_sink"])[:, None, :], (2, 128, 16)))
    shared["ab_ln_gT"] = _fm(f(inputs["ab_ln_g"]), 16)
    shared["ab_ln_bT"] = _fm(f(inputs["ab_ln_b"]), 16)
    ws = f(inputs["ab_ws"])
    shared["ab_wsT"] = np.ascontiguousarray(ws.transpose(0, 3, 1, 2))
    wsb = f(inputs["ab_ws_b"]).reshape(2, 1, 2048)
    shared["ab_wsbB"] = np.ascontiguousarray(np.broadcast_to(wsb, (2, 128, 2048)))
    shared["ab_w_out"] = f(inputs["ab_w_out"])
    shared["cv_w_in"] = f(inputs["cv_w_in"])
    dw = f(inputs["cv_dw"])
    shared["cv_dwT"] = np.ascontiguousarray(dw.reshape(2, 31, KC, 128).transpose(0, 3, 2, 1))
    shared["cv_dw_bT"] = _fm(f(inputs["cv_dw_b"]))
    shared["cv_ln_gT"] = _fm(f(inputs["cv_ln_g"]))
    shared["cv_ln_bT"] = _fm(f(inputs["cv_ln_b"]))
    shared["cv_w_out"] = f(inputs["cv_w_out"])
    bf = ml_dtypes.bfloat16
    shared["c_ones"] = np.ones((128, 128), bf)
    shared["c_ident"] = np.eye(128, dtype=np.float32).astype(bf)
    shared["c_identf"] = np.eye(128, dtype=np.float32)
    pm = np.zeros((128, 128), np.float32)
    for d in range(128):
        pm[d + 32 if (d % 64) < 32 else d - 32, d] = 1.0
    shared["c_perm"] = pm.astype(bf)
    kj = np.arange(128)[:, None]
    qi = np.arange(128)[None, :]
    tri = np.stack([np.tile((kj >= qi).astype(np.float32), (1, 4)), np.tile((kj <= qi).astype(np.float32), (1, 4))], axis=1)
    shared["c_tri"] = tri.astype(bf)
    in_maps = []
    for core in range(NCORES):
        m = dict(shared)
        xw = np.zeros((TW, D), np.float32)
        p0 = core * OWN - HALO
        lo = max(p0, 0)
        hi = min(p0 + TW, SEQ)
        xw[lo - p0:hi - p0] = x[lo:hi]
        m["xT"] = np.ascontiguousarray(xw.T)
        cos, sin = _rope_tables(core)
        m["c_cos"] = cos
        m["c_sin"] = sin
        pos = p0 + np.arange(TW)
        valid = ((pos >= 0) & (pos < SEQ))
        kb = np.zeros((128, 18), np.float32)
        kb[:, :16] = np.where(valid.reshape(16, 128).T, 0.0, NEG)
        m["c_kbias"] = kb
        tm = np.ones((128, TALL), np.float32)
        tm[:, :TW] = valid[None, :].astype(np.float32)
        m["c_tmask"] = tm
        in_maps.append(m)
    return in_maps


_NC_CACHE = {}


def kernel(**inputs):
    in_maps = _prep_inputs(inputs)
    if "nc" not in _NC_CACHE:
        _NC_CACHE["nc"] = build_program()
    nc = _NC_CACHE["nc"]
    res = run_bass_kernel_spmd(nc, in_maps, core_ids=list(range(NCORES)))
    outs = [np.asarray(r["out"]) for r in res.results]
    full = np.concatenate([o.T for o in outs], axis=0)
    return np.ascontiguousarray(full[None].astype(np.float32))
```
